# Optimizing a Trainium2 kernel written in Bass

```python
import math
import jax, jax.numpy as jnp
from jax import lax
import numpy as np

D_MODEL = 4096
BATCH = 4
SEQ = 2048
DEPTH = 4

CTX_LEN = 256
GRID_W = 64
Q_BLOCK = 128
ROPE_BASE = 10000.0
NORM_EPS = 1e-6

N_BRANCH = 4
BRANCH_W = D_MODEL // N_BRANCH

MLA_HEADS = BRANCH_W // 128
MLA_NOPE = 128
MLA_ROPE = 64
MLA_V = 128
MLA_Q_LORA = 3 * D_MODEL // 16
MLA_KV_LORA = D_MODEL // 16

RWKV_HEAD = 64
RWKV_HEADS = BRANCH_W // RWKV_HEAD
RWKV_DECAY_LORA = 64
RWKV_A_LORA = 64
RWKV_GATE_LORA = 160
RWKV_GN_EPS = 64e-5

CONV_W = 3

DIFF_HEADS = BRANCH_W // 128
DIFF_QK = 64
DIFF_V = 2 * DIFF_QK

GATE_RANK = 256
MOD_RANK = 256
N_MOD = 6
D_FF = 4 * D_MODEL

MLA_COLS = MLA_Q_LORA + MLA_KV_LORA + MLA_ROPE
RWKV_COLS = 3 * BRANCH_W + 2 * RWKV_DECAY_LORA + 2 * RWKV_A_LORA + RWKV_GATE_LORA
CONV_COLS = 3 * BRANCH_W
DIFF_COLS = 2 * DIFF_HEADS * 2 * DIFF_QK + DIFF_HEADS * DIFF_V
IN_COLS = MLA_COLS + RWKV_COLS + CONV_COLS + DIFF_COLS
IN_SPLIT = (MLA_COLS, MLA_COLS + RWKV_COLS, MLA_COLS + RWKV_COLS + CONV_COLS)
MLA_SPLIT = (MLA_Q_LORA, MLA_Q_LORA + MLA_KV_LORA)
RWKV_SPLIT = (BRANCH_W, 2 * BRANCH_W, 3 * BRANCH_W, 3 * BRANCH_W + 2 * RWKV_DECAY_LORA,
              3 * BRANCH_W + 2 * RWKV_DECAY_LORA + 2 * RWKV_A_LORA)
DIFF_SPLIT = (DIFF_HEADS * 2 * DIFF_QK, 2 * DIFF_HEADS * 2 * DIFF_QK)

kernel_name = "hybrid_mla_rwkv7_conv_diffattn_dit"


def _rmsnorm(x, g):
    xf = x.astype(jnp.float32)
    y = xf * lax.rsqrt(jnp.mean(xf * xf, axis=-1, keepdims=True) + NORM_EPS)
    return (y * g.astype(jnp.float32)).astype(x.dtype)


def _axial_rope_tables(n_tokens, rot_dim):
    rows = n_tokens // GRID_W
    row = jnp.repeat(jnp.arange(rows, dtype=jnp.float32), GRID_W)
    col = jnp.tile(jnp.arange(GRID_W, dtype=jnp.float32), rows)
    n_freq = rot_dim // 4
    inv = ROPE_BASE ** (-jnp.arange(n_freq, dtype=jnp.float32) / n_freq)
    ang = jnp.concatenate([row[:, None] * inv, col[:, None] * inv], axis=-1)
    return jnp.cos(ang), jnp.sin(ang)


def _apply_rope(x, cos, sin):
    half = x.shape[-1] // 2
    shape = (1, x.shape[1]) + (1,) * (x.ndim - 3) + (half,)
    cos = cos.reshape(shape).astype(x.dtype)
    sin = sin.reshape(shape).astype(x.dtype)
    x1, x2 = x[..., :half], x[..., half:]
    return jnp.concatenate([x1 * cos - x2 * sin, x1 * sin + x2 * cos], axis=-1)


def _modulation(cond, down, up, b):
    m = (jax.nn.silu(cond) @ down) @ up + b
    return m.reshape(cond.shape[:-1] + (N_MOD, D_MODEL))


def _sweep_query_blocks(fn, *qs):
    B, T = qs[0].shape[:2]
    nb = T // Q_BLOCK
    blocks = tuple(jnp.moveaxis(q.reshape((B, nb, Q_BLOCK) + q.shape[2:]), 1, 0) for q in qs)
    out = lax.map(lambda qb: fn(*qb), blocks)
    return jnp.moveaxis(out, 0, 1).reshape((B, T) + out.shape[3:])


def _mla_attend(q_nope, q_rope, k_nope, k_rope, v):
    scale = (MLA_NOPE + MLA_ROPE) ** -0.5

    def block(qn, qr):
        s = jnp.einsum('bqhd,bshd->bhqs', qn, k_nope) + jnp.einsum('bqhr,bsr->bhqs', qr, k_rope)
        prob = jax.nn.softmax(s.astype(jnp.float32) * scale, axis=-1).astype(v.dtype)
        return jnp.einsum('bhqs,bshd->bqhd', prob, v)

    return _sweep_query_blocks(block, q_nope, q_rope)


def _diff_attend(q, k, v, lam):
    scale = DIFF_QK ** -0.5

    def block(qb):
        s = jnp.einsum('bqhnd,bshnd->bnhqs', qb, k).astype(jnp.float32) * scale
        prob = jax.nn.softmax(s, axis=-1)
        w = prob[:, 0] - lam * prob[:, 1]
        return jnp.einsum('bhqs,bshd->bqhd', w.astype(v.dtype), v)

    return _sweep_query_blocks(block, q)


def _short_conv_mixer(p_conv, conv_w):
    b_gate, c_gate, u = jnp.split(p_conv, 3, axis=-1)
    z = c_gate * u
    T = z.shape[1]
    zp = jnp.pad(z, ((0, 0), (1, 1), (0, 0)))
    y = conv_w[0] * zp[:, :T] + conv_w[1] * zp[:, 1:T + 1] + conv_w[2] * zp[:, 2:]
    return b_gate * y


def _centred_token_shift(p, mu):
    T = p.shape[1]
    pp = jnp.pad(p, ((0, 0), (1, 1), (0, 0)))
    return p + mu[0] * (pp[:, :T] - p) + mu[1] * (pp[:, 2:] - p)


def _rwkv_prepare(p, lp, need_out):
    B, T, _ = p.shape
    H, N = RWKV_HEADS, RWKV_HEAD
    p = _centred_token_shift(p, lp["rwkv_mu"])
    r, k, v, wd, ad, gd = jnp.split(p, RWKV_SPLIT, axis=-1)
    wd = jnp.tanh(wd.reshape(B, T, 2, RWKV_DECAY_LORA))
    w_log = -jax.nn.softplus(-(lp["rwkv_w0"] + jnp.einsum('btdr,drc->btdc', wd, lp["rwkv_w2"]))) - 0.5
    decay = jnp.exp(-jnp.exp(w_log.astype(jnp.float32))).reshape(B, T, 2, H, N)
    a = jax.nn.sigmoid(lp["rwkv_a0"] + jnp.einsum('btdr,drc->btdc', ad.reshape(B, T, 2, RWKV_A_LORA),
                                                  lp["rwkv_a2"])).reshape(B, T, 2, H, N)
    k = k.reshape(B, T, H, N)
    kkf = (k * lp["rwkv_k_k"].reshape(H, N)).astype(jnp.float32)
    kk = (kkf * lax.rsqrt(jnp.sum(kkf * kkf, axis=-1, keepdims=True) + 1e-12)).astype(k.dtype)
    k_dir = k[:, :, None] * (1.0 + (a - 1.0) * lp["rwkv_k_a"].reshape(H, N))
    out = dict(r=r.reshape(B, T, H, N), k=k_dir, v=v.reshape(B, T, H, N), kk=kk, a=a, w=decay)
    if need_out:
        out["g"] = jax.nn.sigmoid(gd) @ lp["rwkv_g2"]
    return out


def _rwkv7_scan(state0, r, w, k, v, kk, a, reverse, with_outputs):
    xs = tuple(jnp.moveaxis(t.astype(jnp.float32), 1, 0) for t in (r, w, k, v, kk, a))

    def step(S, inp):
        r_t, w_t, k_t, v_t, kk_t, a_t = inp
        sa = jnp.einsum('bhvk,bhk->bhv', S, kk_t)
        S = (S * w_t[:, :, None, :] - sa[..., None] * (kk_t * a_t)[:, :, None, :]
             + v_t[..., None] * k_t[:, :, None, :])
        o = jnp.einsum('bhvk,bhk->bhv', S, r_t) if with_outputs else None
        return S, o

    S, o = lax.scan(step, state0, xs, reverse=reverse)
    if with_outputs:
        return S, jnp.moveaxis(o, 0, 1).astype(r.dtype)
    return S, None


def _rwkv_bidirectional(fc, fx, need_ctx):
    B = fx["r"].shape[0]
    zeros = jnp.zeros((B, RWKV_HEADS, RWKV_HEAD, RWKV_HEAD), jnp.float32)
    o_x, o_c = [], []
    for d in range(2):
        rev = d == 1
        s_c, oc = _rwkv7_scan(zeros, fc["r"], fc["w"][:, :, d], fc["k"][:, :, d], fc["v"], fc["kk"],
                              fc["a"][:, :, d], rev, need_ctx)
        _, ox = _rwkv7_scan(s_c, fx["r"], fx["w"][:, :, d], fx["k"][:, :, d], fx["v"], fx["kk"],
                            fx["a"][:, :, d], rev, True)
        o_x.append(ox)
        o_c.append(oc)
    return o_x, o_c


def _rwkv_readout(f, o_pair, lp):
    B, T = f["r"].shape[:2]
    o = (o_pair[0] + o_pair[1]).astype(jnp.float32)
    mean = jnp.mean(o, axis=-1, keepdims=True)
    var = jnp.mean(jnp.square(o - mean), axis=-1, keepdims=True)
    o = ((o - mean) * lax.rsqrt(var + RWKV_GN_EPS)).reshape(B, T, BRANCH_W)
    o = (o * lp["rwkv_ln_g"] + lp["rwkv_ln_b"]).astype(f["r"].dtype)
    bonus_w = jnp.sum(f["r"][:, :, None] * f["k"] * lp["rwkv_r_k"], axis=(2, 4), keepdims=False)
    bonus = (bonus_w[..., None] * f["v"]).reshape(B, T, BRANCH_W)
    return (o + bonus) * f["g"]


def _stream_features(h, lp, rope, need_out):
    B, T, _ = h.shape
    p_mla, p_rwkv, p_conv, p_diff = jnp.split(h @ lp["w_in"], IN_SPLIT, axis=-1)
    cq, ckv, k_rope = jnp.split(p_mla, MLA_SPLIT, axis=-1)
    kv = (_rmsnorm(ckv, lp["mla_kv_norm_g"]) @ lp["mla_w_ukv"]).reshape(B, T, MLA_HEADS, MLA_NOPE + MLA_V)
    dq, dk, dv = jnp.split(p_diff, DIFF_SPLIT, axis=-1)
    dk = dk.reshape(B, T, DIFF_HEADS, 2, DIFF_QK)
    if rope is not None:
        k_rope = _apply_rope(k_rope, *rope[0])
        dk = _apply_rope(dk, *rope[1])
    f = dict(k_nope=kv[..., :MLA_NOPE], v_mla=kv[..., MLA_NOPE:], k_rope=k_rope, dk=dk,
             dv=dv.reshape(B, T, DIFF_HEADS, DIFF_V))
    f.update(_rwkv_prepare(p_rwkv, lp, need_out))
    if need_out:
        q = (_rmsnorm(cq, lp["mla_q_norm_g"]) @ lp["mla_w_uq"]).reshape(B, T, MLA_HEADS, MLA_NOPE + MLA_ROPE)
        q_nope, q_rope = q[..., :MLA_NOPE], q[..., MLA_NOPE:]
        dq = dq.reshape(B, T, DIFF_HEADS, 2, DIFF_QK)
        if rope is not None:
            q_rope = _apply_rope(q_rope, *rope[0])
            dq = _apply_rope(dq, *rope[1])
        f.update(q_nope=q_nope, q_rope=q_rope, dq=dq, conv=_short_conv_mixer(p_conv, lp["conv_w"]))
    return f


def _merge(h, branches, lp):
    gl = h @ lp["gate_down"]
    acc = None
    for i, y in enumerate(branches):
        gate = jax.nn.sigmoid(gl @ lp["gate_up"][:, i] + lp["gate_b"][i])
        term = gate * (y @ lp["w_branch"][i])
        acc = term if acc is None else acc + term
    return acc @ lp["w_out"]


def _mixer_output(h, f, key_feats, rwkv_pair, lp, lam, lam_init):
    B, T, _ = h.shape

    def cat(name):
        return jnp.concatenate([kf[name] for kf in key_feats], axis=1)

    y_mla = _mla_attend(f["q_nope"], f["q_rope"], cat("k_nope"), cat("k_rope"), cat("v_mla"))
    y_mla = y_mla.reshape(B, T, BRANCH_W)
    y_rwkv = _rwkv_readout(f, rwkv_pair, lp)
    y_conv = f["conv"]
    o = _diff_attend(f["dq"], cat("dk"), cat("dv"), lam)
    y_diff = (_rmsnorm(o, lp["diff_norm_g"]) * (1.0 - lam_init)).reshape(B, T, BRANCH_W)
    return _merge(h, (y_mla, y_rwkv, y_conv, y_diff), lp)


def _sq_relu_mlp(h, w1, w2):
    return jnp.square(jax.nn.relu(h @ w1)) @ w2


def setup_inputs(seed: int = 0) -> dict:
    key = jax.random.key(seed)
    ks = iter(jax.random.split(key, 48))
    L, D, BW = DEPTH, D_MODEL, BRANCH_W

    def nrm(shape, scale):
        return jax.random.normal(next(ks), shape, jnp.float32) * scale

    def gain(shape):
        return 1.0 + nrm(shape, 0.02)

    def unif(shape, lo, hi):
        return jax.random.uniform(next(ks), shape, jnp.float32, lo, hi)

    return {
        "x": nrm((BATCH, SEQ, D), 1.0),
        "c": nrm((BATCH, D), 1.0),
        "ctx": nrm((BATCH, CTX_LEN, D), 1.0),
        "c_ctx": nrm((D,), 1.0),
        "norm1_g": gain((L, D)),
        "norm2_g": gain((L, D)),
        "mod_down": nrm((L, D, MOD_RANK), D ** -0.5),
        "mod_up": nrm((L, MOD_RANK, N_MOD * D), 0.5 * MOD_RANK ** -0.5),
        "mod_b": nrm((L, N_MOD * D), 0.02),
        "w_in": nrm((L, D, IN_COLS), D ** -0.5),
        "mla_q_norm_g": gain((L, MLA_Q_LORA)),
        "mla_w_uq": nrm((L, MLA_Q_LORA, MLA_HEADS * (MLA_NOPE + MLA_ROPE)), MLA_Q_LORA ** -0.5),
        "mla_kv_norm_g": gain((L, MLA_KV_LORA)),
        "mla_w_ukv": nrm((L, MLA_KV_LORA, MLA_HEADS * (MLA_NOPE + MLA_V)), MLA_KV_LORA ** -0.5),
        "rwkv_mu": unif((L, 2, RWKV_COLS), 0.0, 0.5),
        "rwkv_w0": nrm((L, 2, BW), 0.5),
        "rwkv_w2": nrm((L, 2, RWKV_DECAY_LORA, BW), RWKV_DECAY_LORA ** -0.5),
        "rwkv_a0": nrm((L, 2, BW), 0.5),
        "rwkv_a2": nrm((L, 2, RWKV_A_LORA, BW), RWKV_A_LORA ** -0.5),
        "rwkv_g2": nrm((L, RWKV_GATE_LORA, BW), RWKV_GATE_LORA ** -0.5),
        "rwkv_k_k": 0.85 + nrm((L, BW), 0.05),
        "rwkv_k_a": 1.0 + nrm((L, BW), 0.05),
        "rwkv_r_k": nrm((L, RWKV_HEADS, RWKV_HEAD), 0.1),
        "rwkv_ln_g": gain((L, BW)),
        "rwkv_ln_b": nrm((L, BW), 0.02),
        "conv_w": nrm((L, CONV_W, BW), CONV_W ** -0.5),
        "diff_lambda": nrm((L, 4, DIFF_QK), 0.1),
        "diff_norm_g": gain((L, DIFF_V)),
        "w_branch": nrm((L, N_BRANCH, BW, D), BW ** -0.5),
        "gate_down": nrm((L, D, GATE_RANK), D ** -0.5),
        "gate_up": nrm((L, GATE_RANK, N_BRANCH, D), GATE_RANK ** -0.5),
        "gate_b": nrm((L, N_BRANCH, D), 0.1),
        "w_out": nrm((L, D, D), D ** -0.5),
        "mlp_w1": nrm((L, D, D_FF), D ** -0.5),
        "mlp_w2": nrm((L, D_FF, D), D_FF ** -0.5),
        "final_norm_g": gain((D,)),
    }


def reference(x, c, ctx, c_ctx, norm1_g, norm2_g, mod_down, mod_up, mod_b, w_in, mla_q_norm_g, mla_w_uq,
              mla_kv_norm_g, mla_w_ukv, rwkv_mu, rwkv_w0, rwkv_w2, rwkv_a0, rwkv_a2, rwkv_g2, rwkv_k_k,
              rwkv_k_a, rwkv_r_k, rwkv_ln_g, rwkv_ln_b, conv_w, diff_lambda, diff_norm_g, w_branch, gate_down,
              gate_up, gate_b, w_out, mlp_w1, mlp_w2, final_norm_g):
    T = x.shape[1]
    rope = (_axial_rope_tables(T, MLA_ROPE), _axial_rope_tables(T, DIFF_QK))
    for l in range(DEPTH):
        need_ctx = l < DEPTH - 1
        lp = dict(w_in=w_in[l], mla_q_norm_g=mla_q_norm_g[l], mla_w_uq=mla_w_uq[l],
                  mla_kv_norm_g=mla_kv_norm_g[l], mla_w_ukv=mla_w_ukv[l], rwkv_mu=rwkv_mu[l],
                  rwkv_w0=rwkv_w0[l], rwkv_w2=rwkv_w2[l], rwkv_a0=rwkv_a0[l], rwkv_a2=rwkv_a2[l],
                  rwkv_g2=rwkv_g2[l], rwkv_k_k=rwkv_k_k[l], rwkv_k_a=rwkv_k_a[l], rwkv_r_k=rwkv_r_k[l],
                  rwkv_ln_g=rwkv_ln_g[l], rwkv_ln_b=rwkv_ln_b[l], conv_w=conv_w[l],
                  diff_norm_g=diff_norm_g[l], w_branch=w_branch[l], gate_down=gate_down[l],
                  gate_up=gate_up[l], gate_b=gate_b[l], w_out=w_out[l])
        mx = _modulation(c, mod_down[l], mod_up[l], mod_b[l])[:, :, None, :]
        mc = _modulation(c_ctx, mod_down[l], mod_up[l], mod_b[l])

        hx = _rmsnorm(x, norm1_g[l]) * (1.0 + mx[:, 1]) + mx[:, 0]
        hc = _rmsnorm(ctx, norm1_g[l]) * (1.0 + mc[1]) + mc[0]
        fx = _stream_features(hx, lp, rope, True)
        fc = _stream_features(hc, lp, None, need_ctx)
        o_x, o_c = _rwkv_bidirectional(fc, fx, need_ctx)
        lq1, lk1, lq2, lk2 = diff_lambda[l].astype(jnp.float32)
        lam_init = 0.8 - 0.6 * math.exp(-0.3 * l)
        lam = jnp.exp(jnp.sum(lq1 * lk1)) - jnp.exp(jnp.sum(lq2 * lk2)) + lam_init
        x = x + mx[:, 2] * _mixer_output(hx, fx, (fc, fx), o_x, lp, lam, lam_init)
        if need_ctx:
            ctx = ctx + mc[2] * _mixer_output(hc, fc, (fc,), o_c, lp, lam, lam_init)

        h2 = _rmsnorm(x, norm2_g[l]) * (1.0 + mx[:, 4]) + mx[:, 3]
        x = x + mx[:, 5] * _sq_relu_mlp(h2, mlp_w1[l], mlp_w2[l])
        if need_ctx:
            h2c = _rmsnorm(ctx, norm2_g[l]) * (1.0 + mc[4]) + mc[3]
            ctx = ctx + mc[5] * _sq_relu_mlp(h2c, mlp_w1[l], mlp_w2[l])
    return _rmsnorm(x, final_norm_g)
```

```python
import numpy as np
from contextlib import ExitStack
import concourse.bass as bass
import concourse.mybir as mybir

F32 = mybir.dt.float32
BF16 = mybir.dt.bfloat16
ALU = mybir.AluOpType
ACT = mybir.ActivationFunctionType
AX = mybir.AxisListType
ENGS = ("tensor", "vector", "scalar", "gpsimd", "sync")


class Res:
    def __init__(self, name, t):
        self.name = name
        self.t = t
        self.w = {}
        self.r = {}
        self.sem = None

    def __getitem__(self, k):
        return self.t[k]


class Prog:
    def __init__(self, nc):
        self.nc = nc
        self.es = ExitStack()
        self.semes = ExitStack()
        self.q = {e: [] for e in ENGS}
        self.cnt = {}
        self.sems = {}
        self.known = {e: {} for e in ENGS}
        for e in ENGS:
            self._mksem("E_" + e)
        self.nres = 0
        self.scopes = []
        self.depth = 0
        self.free_keys = []
        self.scope_keys = []
        self.ninstr = 0

    def _mksem(self, key):
        self.sems[key] = self.semes.enter_context(self.nc.semaphore(key))
        self.cnt[key] = 0
        return key

    def sb(self, name, shape, dt=F32):
        self.nres += 1
        name = "%s_%d" % (name, self.nres)
        t = self.es.enter_context(self.nc.sbuf_tensor(name, list(shape), dt))
        r = Res(name, t)
        r.scoped = self.depth > 0
        return r

    def ps(self, name, shape, dt=F32):
        self.nres += 1
        name = "%s_%d" % (name, self.nres)
        t = self.es.enter_context(self.nc.psum_tensor(name, list(shape), dt))
        r = Res(name, t)
        r.scoped = self.depth > 0
        return r

    def dram(self, name, shape, dt=F32, kind="Internal"):
        t = self.nc.dram_tensor(name, list(shape), dt, kind=kind)
        return Res(name, t)

    def _deps(self, eng, reads, writes, own):
        need = {}
        own_raw = 0
        for r in reads:
            for k, v in r.w.items():
                need[k] = max(need.get(k, 0), v)
                if k == own:
                    own_raw = max(own_raw, v)
        for w in writes:
            for k, v in w.w.items():
                need[k] = max(need.get(k, 0), v)
                if k == own:
                    own_raw = max(own_raw, v)
            for k, v in w.r.items():
                need[k] = max(need.get(k, 0), v)
        kn = self.known[eng]
        out = []
        for k, v in need.items():
            if k == own:
                if eng in ("vector", "scalar", "gpsimd") and own is not None and own.startswith("E_") \
                        and own_raw > kn.get(k, 0):
                    kn[k] = own_raw
                    out.append((k, own_raw))
                continue
            if kn.get(k, 0) >= v:
                continue
            kn[k] = v
            out.append((k, v))
        return out

    def _commit(self, reads, writes, key, val):
        for r in reads:
            r.r[key] = max(r.r.get(key, 0), val)
        for w in writes:
            w.w = {key: val}
            w.r = {}

    def op(self, eng, fn, reads=(), writes=()):
        key = "E_" + eng
        waits = self._deps(eng, reads, writes, key)
        self.cnt[key] += 1
        val = self.cnt[key]
        sems = self.sems
        sem = sems[key]

        e = getattr(self.nc, eng)
        for k, v in waits:
            e.wait_ge(sems[k], v)
        fn(e).then_inc(sem, 1)
        self._commit(reads, writes, key, val)
        self.ninstr += 1

    def dma(self, eng, out_res, out_ap, in_res, in_ap, **kw):
        if out_res.sem is None:
            if getattr(out_res, "scoped", False):
                if self.free_keys:
                    out_res.sem = self.free_keys.pop()
                else:
                    out_res.sem = self._mksem("DS_%d" % len(self.sems))
                self.scope_keys[-1].append(out_res.sem)
            else:
                out_res.sem = self._mksem("D_%d_%s" % (len(self.sems), out_res.name))
        key = out_res.sem
        waits = self._deps(eng, [in_res], [out_res], key)
        self.cnt[key] += 16
        val = self.cnt[key]
        sems = self.sems
        sem = sems[key]

        e = getattr(self.nc, eng)
        for k, v in waits:
            e.wait_ge(sems[k], v)
        e.dma_start(out=out_ap, in_=in_ap, **kw).then_inc(sem, 16)
        in_res.r[key] = max(in_res.r.get(key, 0), val)
        w = dict(out_res.w)
        w[key] = val
        out_res.w = {key: val}
        out_res.r = {}
        self.ninstr += 1

    def wait_all(self, eng, resources):
        waits = self._deps(eng, resources, [], None)
        sems = self.sems

        e = getattr(self.nc, eng)
        for k, v in waits:
            e.wait_ge(sems[k], v)

    def barrier(self):
        snap = dict(self.cnt)
        sems = self.sems
        for eng in ENGS:
            kn = self.known[eng]
            waits = [(k, v) for k, v in snap.items() if v > 0 and k != "E_" + eng and kn.get(k, 0) < v]
            for k, v in waits:
                kn[k] = v

            e = getattr(self.nc, eng)
            for k, v in waits:
                e.wait_ge(sems[k], v)

    def scope(self):
        prog = self

        class _S:
            def __enter__(s):
                prog.barrier()
                s.old = prog.es
                prog.es = ExitStack()
                prog.depth += 1
                prog.scope_keys.append([])
                return s

            def __exit__(s, *a):
                prog.barrier()
                prog.es.close()
                prog.es = s.old
                prog.depth -= 1
                prog.free_keys.extend(prog.scope_keys.pop())
                return False
        return _S()

    def finish(self, outs):
        for eng in ("sync", "gpsimd", "vector", "scalar", "tensor"):
            self.wait_all(eng, outs)
        self.barrier()
        self.es.close()


class Pool:
    def __init__(self, items):
        self.items = items
        self.i = 0

    def get(self):
        r = self.items[self.i % len(self.items)]
        self.i += 1
        return r

import math
import numpy as np
import concourse.bass as bass


NORM_EPS = 1e-6
GN_EPS = 64e-5
WCOLS = 1024


def cdiv(a, b):
    return (a + b - 1) // b


def pad128(n):
    return cdiv(n, 128) * 128


class Cfg:
    def __init__(self, D=4096, T=2048, C=256, DEPTH=4, GRID_W=64, ncores=8):
        self.D, self.T, self.C, self.DEPTH, self.GRID_W, self.ncores = D, T, C, DEPTH, GRID_W, ncores
        self.NT = C + T
        self.BW = D // 4
        self.H_MLA = self.BW // 128
        self.H_RW = self.BW // 64
        self.H_DF = self.BW // 128
        self.QL = 3 * D // 16
        self.KVL = D // 16
        self.DFF = 4 * D
        self.GR = 256
        self.MR = 256
        self.DC = D // 128
        self.BC = self.BW // 128
        BW = self.BW
        segs = [("cq", pad128(self.QL)), ("ckv", pad128(self.KVL)), ("kr", 128),
                ("r", BW), ("k", BW), ("v", BW), ("wd", 128), ("ad", 128), ("gd", 256),
                ("cb", BW), ("cc", BW), ("cu", BW), ("dq", BW), ("dk", BW), ("dv", BW),
                ("dqr", BW), ("dkr", BW)]
        self.seg = {}
        o = 0
        for n, s in segs:
            self.seg[n] = (o, s)
            o += s
        self.PCOLS = o
        self.tiles = []
        for a, b in ((0, C), (C, C + T)):
            n = a
            while n < b:
                s = min(512, b - n)
                self.tiles.append((n, s))
                n += s
        self.segs_tok = ((0, C), (C, C + T))
        QLp, KVLp = pad128(self.QL), pad128(self.KVL)
        self.wshapes = [
            ("mod_down", D, self.MR), ("mod_up", self.MR, 6 * D), ("w_in", D, self.PCOLS),
            ("w_uq", QLp, self.H_MLA * 256), ("w_ukv", KVLp, self.H_MLA * 256),
            ("w2_0", 64, BW), ("w2_1", 64, BW), ("a2_0", 64, BW), ("a2_1", 64, BW), ("g2", 256, BW),
            ("wb_0", BW, D), ("wb_1", BW, D), ("wb_2", BW, D), ("wb_3", BW, D),
            ("gate_down", D, self.GR), ("gu_0", self.GR, D), ("gu_1", self.GR, D), ("gu_2", self.GR, D),
            ("gu_3", self.GR, D), ("w_out", D, D), ("mlp_w1", D, self.DFF), ("mlp_w2", self.DFF, D),
        ]
        self.wgroup_of = {}
        for n, K, M in self.wshapes:
            self.wgroup_of[n] = 1 if n == "mlp_w1" else (2 if n == "mlp_w2" else 0)
        self.NG = 3
        self.woff = {}
        o = [0] * self.NG
        for n, K, M in self.wshapes:
            g = self.wgroup_of[n]
            self.woff[n] = (o[g], K, M)
            o[g] += pad128(K) * pad128(M)
        self.wrows = [cdiv(cdiv(o[g], WCOLS), 8 * 16) * 8 * 16 for g in range(self.NG)]

    def cond_of(self, n0):
        return 0 if n0 < self.C else 1


class Model:
    def __init__(self, P, cfg):
        self.P = P
        self.cfg = cfg
        self.nc = P.nc
        self.rr = 0

    def wview(self, wfull, name, mc):
        off, K, M = self.cfg.woff[name]
        kw = pad128(K)
        o = off + mc * 128 * kw
        return bass.AP(wfull[self.cfg.wgroup_of[name]].t, o, [[kw, 128], [1, kw]])

    def evac(self, eng, out_ap, in_ap, reads, writes, func=None, bias=None, scale=1.0):
        P = self.P
        if eng == "scalar":
            f = func if func is not None else ACT.Copy
            if bias is None:
                P.op("scalar", lambda e: e.activation(out_ap, in_ap, f, scale=scale), reads=reads, writes=writes)
            else:
                P.op("scalar", lambda e: e.activation(out_ap, in_ap, f, bias=bias, scale=scale),
                     reads=reads, writes=writes)
        else:
            P.op("vector", lambda e: e.tensor_copy(out_ap, in_ap), reads=reads, writes=writes)

    def alt(self):
        self.rr += 1
        return "scalar" if self.rr % 2 else "vector"

    def alloc_linear_bufs(self):
        P = self.P
        self.xpool = Pool([P.sb("xb%d" % i, [128, 16384], BF16) for i in range(2)])
        self.wpool = Pool([P.sb("wb%d" % i, [128, 16384], BF16) for i in range(2)])
        self.pspool = Pool([P.ps("ps%d" % i, [128, 512], F32) for i in range(4)])
        self.opool = Pool([P.sb("ob%d" % i, [128, 512], F32) for i in range(4)])
        self.opoolb = Pool([P.sb("obb%d" % i, [128, 512], BF16) for i in range(4)])
        self.accpool = Pool([P.sb("acc%d" % i, [128, 512], F32) for i in range(2)])

    def linear(self, xT, K, wfull, wname, M, epi, tiles=None, nmax=512, row0=0):
        P = self.P
        KC = cdiv(K, 128)
        MC = cdiv(M, 128)
        if tiles is None:
            tiles = self.cfg.tiles
        sub = []
        for n0, ns in tiles:
            o = 0
            while o < ns:
                s = min(nmax, ns - o)
                sub.append((n0 + o, s))
                o += s
        for n0, nsz in sub:
            xt = self.xpool.get()
            flat = xt.t
            kfull = K // 128
            if kfull:
                P.dma("sync", xt, flat[:, 0:kfull * nsz].rearrange("p (kc n) -> p kc n", n=nsz),
                      xT, xT.t[row0:row0 + kfull * 128, n0:n0 + nsz].rearrange("(kc p) n -> p kc n", p=128))
            krem = K - kfull * 128
            if krem:
                P.dma("sync", xt, flat[0:krem, kfull * nsz:(kfull + 1) * nsz],
                      xT, xT.t[row0 + kfull * 128:row0 + K, n0:n0 + nsz])
            for mc in range(MC):
                m0 = mc * 128
                msz = min(128, M - m0)
                wt = self.wpool.get()
                P.dma("sync", wt, wt.t[:, 0:KC * 128], wfull[self.cfg.wgroup_of[wname]], self.wview(wfull, wname, mc)[:, 0:KC * 128])
                ps = self.pspool.get()
                for kc in range(KC):
                    ksz = min(128, K - kc * 128)
                    P.op("tensor", lambda e, kc=kc, ksz=ksz: e.matmul(
                        ps.t[0:msz, 0:nsz], wt.t[0:ksz, kc * 128:kc * 128 + msz],
                        flat[0:ksz, kc * nsz:(kc + 1) * nsz], start=(kc == 0), stop=(kc == KC - 1)),
                        reads=[wt, xt], writes=[ps])
                epi(ps, m0, msz, n0, nsz)

    def epi_store(self, dst32=None, dst16=None, func=None, bias_fn=None, row0=0, square=False, mul=None):
        P = self.P

        def epi(ps, m0, msz, n0, nsz):
            eng = "scalar" if (func is not None or bias_fn is not None) else self.alt()
            bias = bias_fn(m0, msz, n0) if bias_fn is not None else None
            if dst32 is not None:
                o = self.opool.get()
                self.evac(eng, o.t[0:msz, 0:nsz], ps.t[0:msz, 0:nsz], [ps], [o], func=func, bias=bias)
                if square:
                    P.op("vector", lambda e: e.tensor_tensor(o.t[0:msz, 0:nsz], o.t[0:msz, 0:nsz],
                                                             o.t[0:msz, 0:nsz], ALU.mult), reads=[o], writes=[o])
                if mul is not None:
                    P.op("vector", lambda e: e.tensor_scalar(o.t[0:msz, 0:nsz], o.t[0:msz, 0:nsz], mul, None, ALU.mult),
                         reads=[o], writes=[o])
                P.dma("gpsimd", dst32, dst32.t[row0 + m0:row0 + m0 + msz, n0:n0 + nsz], o, o.t[0:msz, 0:nsz])
                if dst16 is not None:
                    ob = self.opoolb.get()
                    P.op("vector", lambda e: e.tensor_copy(ob.t[0:msz, 0:nsz], o.t[0:msz, 0:nsz]),
                         reads=[o], writes=[ob])
                    P.dma("gpsimd", dst16, dst16.t[row0 + m0:row0 + m0 + msz, n0:n0 + nsz], ob, ob.t[0:msz, 0:nsz])
            else:
                ob = self.opoolb.get()
                if square:
                    o = self.opool.get()
                    self.evac(eng, o.t[0:msz, 0:nsz], ps.t[0:msz, 0:nsz], [ps], [o], func=func, bias=bias)
                    P.op("vector", lambda e: e.tensor_tensor(ob.t[0:msz, 0:nsz], o.t[0:msz, 0:nsz],
                                                             o.t[0:msz, 0:nsz], ALU.mult), reads=[o], writes=[ob])
                else:
                    self.evac(eng, ob.t[0:msz, 0:nsz], ps.t[0:msz, 0:nsz], [ps], [ob], func=func, bias=bias)
                P.dma("gpsimd", dst16, dst16.t[row0 + m0:row0 + m0 + msz, n0:n0 + nsz], ob, ob.t[0:msz, 0:nsz])
        return epi

    def rmsnorm(self, src, srow0, F, dst16, drow0, A_fn, B_fn=None, deps=(), ones=None, tiles=None, dcol=0):
        P = self.P
        cfg = self.cfg
        FC = cdiv(F, 128)
        nmax = max(128, min(512, (8192 // FC) // 128 * 128))
        sub = []
        for a, ns in (tiles if tiles is not None else cfg.tiles):
            o = 0
            while o < ns:
                s_ = min(nmax, ns - o)
                sub.append((a + o, s_))
                o += s_
        for n0, nsz in sub:
            ci = cfg.cond_of(n0)
            xt = self.nx.get()
            v = xt.t[:, 0:FC * nsz].rearrange("p (c n) -> p c n", n=nsz)
            P.dma("sync", xt, v, src, src.t[srow0:srow0 + FC * 128, n0:n0 + nsz].rearrange("(c p) n -> p c n", p=128))
            sq = self.nsq.get()
            sv = sq.t[:, 0:FC * nsz].rearrange("p (c n) -> p c n", n=nsz)
            P.op("scalar", lambda e: e.activation(sv, v, ACT.Square), reads=[xt], writes=[sq])
            ps = self.pspool.get()
            for c in range(FC):
                P.op("tensor", lambda e, c=c: e.matmul(ps.t[:, 0:nsz], ones.t[:, :], sq.t[:, c * nsz:(c + 1) * nsz],
                                                       start=(c == 0), stop=(c == FC - 1)), reads=[ones, sq], writes=[ps])
            rs = self.nrs.get()
            P.op("scalar", lambda e: e.activation(rs.t[:, 0:nsz], ps.t[:, 0:nsz], ACT.Sqrt, bias=self.eps_t.t[:, 0:1],
                                                  scale=1.0 / F), reads=[ps, self.eps_t], writes=[rs])
            P.op("vector", lambda e: e.reciprocal(rs.t[:, 0:nsz], rs.t[:, 0:nsz]), reads=[rs], writes=[rs])
            ot = self.no.get()
            ov = ot.t[:, 0:FC * nsz].rearrange("p (c n) -> p c n", n=nsz)
            P.op("vector", lambda e: e.tensor_tensor(v, v, rs.t[:, 0:nsz].unsqueeze(1).broadcast_to([128, FC, nsz]),
                                                     ALU.mult), reads=[xt, rs], writes=[xt])
            for c in range(FC):
                a_ap = A_fn(c, ci)
                if B_fn is not None:
                    b_ap = B_fn(c, ci)
                    P.op("scalar", lambda e, c=c, a_ap=a_ap, b_ap=b_ap: e.activation(
                        ov[:, c, :], v[:, c, :], ACT.Identity, bias=b_ap, scale=a_ap),
                        reads=[xt] + list(deps), writes=[ot])
                else:
                    P.op("vector", lambda e, c=c, a_ap=a_ap: e.tensor_scalar(
                        ov[:, c, :], v[:, c, :], a_ap, None, ALU.mult), reads=[xt] + list(deps), writes=[ot])
            P.dma("gpsimd", dst16, dst16.t[drow0:drow0 + FC * 128, n0 - dcol:n0 - dcol + nsz].rearrange("(c p) n -> p c n", p=128),
                  ot, ov)

    def alloc_norm_bufs(self, out_dt=BF16):
        P = self.P
        self.nx = Pool([P.sb("nx%d" % i, [128, 8192], F32) for i in range(2 if out_dt == BF16 else 1)])
        self.nsq = Pool([P.sb("nsq%d" % i, [128, 8192], F32) for i in range(1)])
        self.nrs = Pool([P.sb("nrs%d" % i, [128, 512], F32) for i in range(2)])
        self.no = Pool([P.sb("no%d" % i, [128, 8192], out_dt) for i in range(2 if out_dt == BF16 else 1)])

    def ld(self, pool, src, r0, nrows, n0, nsz, q="sync"):
        t = pool.get()
        self.P.dma(q, t, t.t[0:nrows, 0:nsz], src, src.t[r0:r0 + nrows, n0:n0 + nsz])
        return t

    def st(self, dst, r0, nrows, n0, nsz, t, q="gpsimd"):
        self.P.dma(q, dst, dst.t[r0:r0 + nrows, n0:n0 + nsz], t, t.t[0:nrows, 0:nsz])

    def rope(self, src, srow, rrow, nrows_total, dst16, drow):
        P = self.P
        cfg = self.cfg
        done = 0
        while done < nrows_total:
            nr = min(128, nrows_total - done)
            for n0, nsz in cfg.tiles:
                a = self.ld(self.epool, src, srow + done, nr, n0, nsz)
                b = self.ld(self.epool, src, rrow + done, nr, n0, nsz)
                P.op("vector", lambda e: e.tensor_tensor(a.t[0:nr, 0:nsz], a.t[0:nr, 0:nsz],
                                                         self.cos2.t[0:nr, n0:n0 + nsz], ALU.mult),
                     reads=[a, self.cos2], writes=[a])
                P.op("gpsimd", lambda e: e.tensor_tensor(b.t[0:nr, 0:nsz], b.t[0:nr, 0:nsz],
                                                         self.sin2.t[0:nr, n0:n0 + nsz], ALU.mult),
                     reads=[b, self.sin2], writes=[b])
                o = self.epoolb.get()
                P.op("vector", lambda e: e.tensor_tensor(o.t[0:nr, 0:nsz], a.t[0:nr, 0:nsz], b.t[0:nr, 0:nsz], ALU.add),
                     reads=[a, b], writes=[o])
                self.st(dst16, drow + done, nr, n0, nsz, o)
            done += nr

    def cast_rows(self, src, srow, nrows_total, dst16, drow):
        P = self.P
        done = 0
        while done < nrows_total:
            nr = min(128, nrows_total - done)
            for n0, nsz in self.cfg.tiles:
                a = self.ld(self.epool, src, srow + done, nr, n0, nsz)
                o = self.epoolb.get()
                P.op("vector", lambda e: e.tensor_copy(o.t[0:nr, 0:nsz], a.t[0:nr, 0:nsz]), reads=[a], writes=[o])
                self.st(dst16, drow + done, nr, n0, nsz, o)
            done += nr

    def alloc_ew_bufs(self):
        P = self.P
        self.epool = Pool([P.sb("ew%d" % i, [128, 512], F32) for i in range(6)])
        self.epoolb = Pool([P.sb("ewb%d" % i, [128, 512], BF16) for i in range(3)])

    def alloc_attn_bufs(self):
        P = self.P
        NT = self.cfg.NT
        NK = NT // 128
        self.a_q = [P.sb("a_q%d" % i, [128, NT], BF16) for i in range(2)]
        self.a_k = [P.sb("a_k%d" % i, [128, NT], BF16) for i in range(2)]
        self.a_vf = P.sb("a_vf", [128, NT], BF16)
        self.a_vt = P.sb("a_vt", [128, NK, 128], BF16)
        self.a_pb = P.sb("a_pb", [128, NT], BF16)
        self.a_pt = P.sb("a_pt", [128, NK, 128], BF16)
        self.a_y = P.sb("a_y", [128, NT], BF16)
        self.a_sc = P.ps("a_sc", [128, 512 * cdiv(NT, 512)], F32)
        self.a_pT = P.ps("a_pT", [128, 512], F32)
        self.a_pO = [P.ps("a_pO%d" % i, [128, 512], F32) for i in range(2)]
        self.a_small = Pool([P.sb("a_sm%d" % i, [128, 8], F32) for i in range(4)])
        self.a_o = Pool([P.sb("a_o%d" % i, [128, 128], F32) for i in range(3)])
        self.a_ob = Pool([P.sb("a_ob%d" % i, [128, 128], BF16) for i in range(2)])

    def v_to_tokmajor(self):
        P = self.P
        NK = self.cfg.NT // 128
        for c0 in range(0, NK, 4):
            nb = min(4, NK - c0)
            for j in range(nb):
                c = c0 + j
                P.op("tensor", lambda e, c=c, j=j: e.matmul(self.a_pT.t[:, j * 128:(j + 1) * 128],
                                                            self.a_vf.t[:, c * 128:(c + 1) * 128], self.identb.t[:, :],
                                                            start=True, stop=True),
                     reads=[self.a_vf, self.identb], writes=[self.a_pT])
            self.evac(self.alt(), self.a_vt.t[:, c0:c0 + nb, :].rearrange("p c d -> p (c d)"),
                      self.a_pT.t[:, 0:nb * 128], [self.a_pT], [self.a_vt])

    def softmax_pv(self, qparts, kparts, q0, nk, scale, pO):
        P = self.P
        sc = self.a_sc
        npart = len(qparts)
        for k0 in range(0, nk, 512):
            ks = min(512, nk - k0)
            for i, ((qr, qa, qb), (kr, ka, kb)) in enumerate(zip(qparts, kparts)):
                P.op("tensor", lambda e, qr=qr, qa=qa, qb=qb, kr=kr, ka=ka, kb=kb, i=i: e.matmul(
                    sc.t[:, k0:k0 + ks], qr.t[qa:qb, q0:q0 + 128], kr.t[ka:kb, k0:k0 + ks],
                    start=(i == 0), stop=(i == npart - 1)), reads=[qr, kr], writes=[sc])
        sm = self.a_small.get()
        P.op("vector", lambda e: e.reduce_max(sm.t[:, 0:1], sc.t[:, 0:nk], AX.X), reads=[sc], writes=[sm])
        P.op("vector", lambda e: e.tensor_scalar(sm.t[:, 1:2], sm.t[:, 0:1], -scale, None, ALU.mult), reads=[sm], writes=[sm])
        P.op("scalar", lambda e: e.activation(self.a_pb.t[:, 0:nk], sc.t[:, 0:nk], ACT.Exp, bias=sm.t[:, 1:2],
                                              scale=scale, accum_out=sm.t[:, 2:3]), reads=[sc, sm], writes=[self.a_pb, sm])
        P.op("vector", lambda e: e.reciprocal(sm.t[:, 3:4], sm.t[:, 2:3]), reads=[sm], writes=[sm])
        nc_ = nk // 128
        for c0 in range(0, nc_, 4):
            nb = min(4, nc_ - c0)
            for j in range(nb):
                c = c0 + j
                P.op("tensor", lambda e, c=c, j=j: e.matmul(self.a_pT.t[:, j * 128:(j + 1) * 128],
                                                            self.a_pb.t[:, c * 128:(c + 1) * 128], self.identb.t[:, :],
                                                            start=True, stop=True),
                     reads=[self.a_pb, self.identb], writes=[self.a_pT])
            self.evac(self.alt(), self.a_pt.t[:, c0:c0 + nb, :].rearrange("p c d -> p (c d)"),
                      self.a_pT.t[:, 0:nb * 128], [self.a_pT], [self.a_pt])
        for c in range(nc_):
            P.op("tensor", lambda e, c=c: e.matmul(pO.t[:, 0:128], self.a_pt.t[:, c, :], self.a_vt.t[:, c, :],
                                                   start=(c == 0), stop=(c == nc_ - 1)),
                 reads=[self.a_pt, self.a_vt], writes=[pO])
        return sm

    def out_transpose(self, ob, q0):
        P = self.P
        P.op("tensor", lambda e: e.matmul(self.a_pT.t[:, 0:128], ob.t[:, :], self.identb.t[:, :], start=True, stop=True),
             reads=[ob, self.identb], writes=[self.a_pT])
        self.evac(self.alt(), self.a_y.t[:, q0:q0 + 128], self.a_pT.t[:, 0:128], [self.a_pT], [self.a_y])

    def load_rows(self, t, src, r0, nr, p0=0):
        NT = self.cfg.NT
        self.P.dma("sync", t, t.t[p0:p0 + nr, 0:NT], src, src.t[r0:r0 + nr, 0:NT])

    def mla_attention(self, QN, QR, KN, KR, VM, Y, yrow0):
        P = self.P
        cfg = self.cfg
        scale = (128 + 64) ** -0.5
        self.load_rows(self.a_k[1], KR, 0, 64)
        for h in range(cfg.H_MLA):
            self.load_rows(self.a_q[0], QN, h * 128, 128)
            self.load_rows(self.a_q[1], QR, h * 64, 64)
            self.load_rows(self.a_k[0], KN, h * 128, 128)
            self.load_rows(self.a_vf, VM, h * 128, 128)
            self.v_to_tokmajor()
            for q0 in range(0, cfg.NT, 128):
                nk = cfg.C if q0 < cfg.C else cfg.NT
                sm = self.softmax_pv([(self.a_q[0], 0, 128), (self.a_q[1], 0, 64)],
                                     [(self.a_k[0], 0, 128), (self.a_k[1], 0, 64)], q0, nk, scale, self.a_pO[0])
                ob = self.a_ob.get()
                P.op("scalar", lambda e: e.activation(ob.t[:, :], self.a_pO[0].t[:, 0:128], ACT.Copy, scale=sm.t[:, 3:4]),
                     reads=[self.a_pO[0], sm], writes=[ob])
                self.out_transpose(ob, q0)
            P.dma("gpsimd", Y, Y.t[yrow0 + h * 128:yrow0 + (h + 1) * 128, 0:cfg.NT], self.a_y, self.a_y.t[:, 0:cfg.NT])

    def diff_attention(self, DQ, DK, DV, Y, yrow0, lam_col, gtile, lam_init):
        P = self.P
        cfg = self.cfg
        scale = 64 ** -0.5
        for h in range(cfg.H_DF):
            self.load_rows(self.a_q[0], DQ, h * 128, 128)
            self.load_rows(self.a_k[0], DK, h * 128, 128)
            self.load_rows(self.a_vf, DV, h * 128, 128)
            self.v_to_tokmajor()
            for q0 in range(0, cfg.NT, 128):
                nk = cfg.C if q0 < cfg.C else cfg.NT
                sm1 = self.softmax_pv([(self.a_q[0], 0, 64)], [(self.a_k[0], 0, 64)], q0, nk, scale, self.a_pO[0])
                sm2 = self.softmax_pv([(self.a_q[0], 64, 128)], [(self.a_k[0], 64, 128)], q0, nk, scale, self.a_pO[1])
                o1 = self.a_o.get()
                P.op("scalar", lambda e: e.activation(o1.t[:, :], self.a_pO[0].t[:, 0:128], ACT.Copy, scale=sm1.t[:, 3:4]),
                     reads=[self.a_pO[0], sm1], writes=[o1])
                P.op("vector", lambda e: e.tensor_tensor(sm2.t[:, 4:5], sm2.t[:, 3:4], lam_col.t[:, 0:1], ALU.mult),
                     reads=[sm2, lam_col], writes=[sm2])
                o = self.a_o.get()
                P.op("vector", lambda e: e.scalar_tensor_tensor(o.t[:, :], self.a_pO[1].t[:, 0:128], sm2.t[:, 4:5],
                                                                o1.t[:, :], ALU.mult, ALU.add),
                     reads=[self.a_pO[1], sm2, o1], writes=[o])
                sq = self.a_o.get()
                P.op("vector", lambda e: e.tensor_tensor(sq.t[:, :], o.t[:, :], o.t[:, :], ALU.mult), reads=[o], writes=[sq])
                P.op("vector", lambda e: e.reduce_sum(sm2.t[:, 5:6], sq.t[:, :], AX.X), reads=[sq], writes=[sm2])
                P.op("scalar", lambda e: e.activation(sm2.t[:, 6:7], sm2.t[:, 5:6], ACT.Sqrt, bias=self.eps_t.t[:, 0:1],
                                                      scale=1.0 / 128), reads=[sm2, self.eps_t], writes=[sm2])
                P.op("vector", lambda e: e.reciprocal(sm2.t[:, 6:7], sm2.t[:, 6:7]), reads=[sm2], writes=[sm2])
                ob = self.a_ob.get()
                if 0 == 1:
                    P.op("vector", lambda e: e.tensor_copy(ob.t[:, :], o.t[:, :]), reads=[o], writes=[ob])
                elif 0 == 3:
                    P.op("vector", lambda e: e.tensor_scalar(ob.t[:, :], o.t[:, :], sm2.t[:, 6:7], None, ALU.mult), reads=[o, sm2], writes=[ob])
                elif 0 == 4:
                    P.op("vector", lambda e: e.tensor_tensor(ob.t[:, :], o.t[:, :], gtile.t[:, :], ALU.mult), reads=[o, gtile], writes=[ob])
                elif 0 == 5:
                    P.op("scalar", lambda e: e.activation(ob.t[:, :], self.a_pO[1].t[:, 0:128], ACT.Copy, scale=sm2.t[:, 3:4]),
                         reads=[self.a_pO[1], sm2], writes=[ob])
                elif 0 == 6:
                    P.op("scalar", lambda e: e.activation(ob.t[:, :], self.a_pO[1].t[:, 0:128], ACT.Copy, scale=sm2.t[:, 4:5]),
                         reads=[self.a_pO[1], sm2], writes=[ob])
                elif 0 == 7:
                    P.op("vector", lambda e: e.tensor_copy(ob.t[:, 0:2], lam_col.t[:, 0:2]), reads=[lam_col], writes=[ob])
                    P.op("vector", lambda e: e.tensor_copy(ob.t[:, 2:8], sm2.t[:, 2:8]), reads=[sm2], writes=[ob])
                elif 0 == 2:
                    P.op("vector", lambda e: e.tensor_copy(ob.t[:, :], o1.t[:, :]), reads=[o1], writes=[ob])
                else:
                    P.op("vector", lambda e: e.scalar_tensor_tensor(ob.t[:, :], o.t[:, :], sm2.t[:, 6:7], gtile.t[:, :],
                                                                    ALU.mult, ALU.mult), reads=[o, sm2, gtile], writes=[ob])
                self.out_transpose(ob, q0)
            P.dma("gpsimd", Y, Y.t[yrow0 + h * 128:yrow0 + (h + 1) * 128, 0:cfg.NT], self.a_y, self.a_y.t[:, 0:cfg.NT])

    def alloc_pad_bufs(self, nload, nwork):
        P = self.P
        NP = self.cfg.NT + 4
        self.NP = NP
        self.lpool = Pool([P.sb("pl%d" % i, [128, NP], F32) for i in range(nload)])
        for t in self.lpool.items:
            P.op("gpsimd", lambda e, t=t: e.memset(t.t[:, :], 0.0), writes=[t])
        self.wk = Pool([P.sb("pw%d" % i, [128, NP], F32) for i in range(nwork)])
        self.wkb = Pool([P.sb("pwb%d" % i, [128, NP], BF16) for i in range(2)])

    def ldpad(self, src, row0, nr=128):
        cfg = self.cfg
        t = self.lpool.get()
        C, T = cfg.C, cfg.T
        self.P.dma("sync", t, t.t[0:nr, 1:1 + C], src, src.t[row0:row0 + nr, 0:C])
        self.P.dma("sync", t, t.t[0:nr, C + 3:C + 3 + T], src, src.t[row0:row0 + nr, C:C + T])
        return t

    def stpad(self, dst, row0, t, nr=128):
        cfg = self.cfg
        C, T = cfg.C, cfg.T
        self.P.dma("gpsimd", dst, dst.t[row0:row0 + nr, 0:C], t, t.t[0:nr, 1:1 + C])
        self.P.dma("gpsimd", dst, dst.t[row0:row0 + nr, C:C + T], t, t.t[0:nr, C + 3:C + 3 + T])

    def tshift(self, x, mu0, mu1, o):
        P = self.P
        NP = self.NP
        cur, prev, nxt = x.t[:, 1:NP - 1], x.t[:, 0:NP - 2], x.t[:, 2:NP]
        d1 = self.wk.get()
        P.op("vector", lambda e: e.tensor_tensor(d1.t[:, 1:NP - 1], prev, cur, ALU.subtract), reads=[x], writes=[d1])
        d2 = self.wk.get()
        P.op("gpsimd", lambda e: e.tensor_tensor(d2.t[:, 1:NP - 1], nxt, cur, ALU.subtract), reads=[x], writes=[d2])
        P.op("vector", lambda e: e.scalar_tensor_tensor(d1.t[:, 1:NP - 1], d1.t[:, 1:NP - 1], mu0, cur, ALU.mult, ALU.add),
             reads=[d1, x, self.PVt], writes=[d1])
        P.op("vector", lambda e: e.scalar_tensor_tensor(o.t[:, 1:NP - 1], d2.t[:, 1:NP - 1], mu1, d1.t[:, 1:NP - 1],
                                                        ALU.mult, ALU.add), reads=[d1, d2, self.PVt], writes=[o])
        return o

    def blocksum(self, src, dst_fn):
        P = self.P
        NP = self.NP
        for c0 in range(1, NP - 1, 512):
            cs = min(512, NP - 1 - c0)
            ps = self.pspool.get()
            P.op("tensor", lambda e: e.matmul(ps.t[:, 0:cs], self.blk1.t[:, :], src.t[:, c0:c0 + cs], start=True, stop=True),
                 reads=[self.blk1, src], writes=[ps])
            dst_fn(ps, c0, cs)

    def rwkv_prep_a(self, P32, D_):
        P = self.P
        cfg = self.cfg
        NP = self.NP
        BC = cfg.BC
        pv = self.pv
        col = lambda name, c: self.PVt.t[:, pv[name][0] + c:pv[name][0] + c + 1]
        segs = [("wd", 0, ACT.Tanh), ("ad", 0, None), ("gd", 0, ACT.Sigmoid), ("gd", 128, ACT.Sigmoid)]
        for i, (sn, off, fn) in enumerate(segs):
            x = self.ldpad(P32, cfg.seg[sn][0] + off)
            s = self.tshift(x, col("mu0", 3 * BC + i), col("mu1", 3 * BC + i), self.wk.get())
            ob = self.wkb.get()
            if fn is None:
                P.op("vector", lambda e: e.tensor_copy(ob.t[:, 1:NP - 1], s.t[:, 1:NP - 1]), reads=[s], writes=[ob])
            else:
                P.op("scalar", lambda e, fn=fn: e.activation(ob.t[:, 1:NP - 1], s.t[:, 1:NP - 1], fn), reads=[s], writes=[ob])
            self.stpad(D_["LORA16"], i * 128, ob)

    def rwkv_prep_b(self, wfull, D_):
        cfg = self.cfg
        pv = self.pv
        col = lambda name, c: self.PVt.t[:, pv[name][0] + c:pv[name][0] + c + 1]
        for d in range(2):
            self.linear(D_["LORA16"], 64, wfull, "w2_%d" % d, cfg.BW,
                        self.epi_store(dst32=D_["LW%d" % d], func=ACT.Sigmoid,
                                       bias_fn=lambda m0, msz, n0, d=d: col("w0_%d" % d, m0 // 128)[0:msz], mul=-0.6065306597126334),
                        row0=d * 64)
            self.linear(D_["LORA16"], 64, wfull, "a2_%d" % d, cfg.BW,
                        self.epi_store(dst32=D_["A%d" % d], func=ACT.Sigmoid,
                                       bias_fn=lambda m0, msz, n0, d=d: col("a0_%d" % d, m0 // 128)[0:msz]),
                        row0=128 + d * 64)
        self.linear(D_["LORA16"], 256, wfull, "g2", cfg.BW, self.epi_store(dst32=D_["G"]), row0=256)

    def rwkv_prep_c(self, P32, D_):
        P = self.P
        cfg = self.cfg
        NP = self.NP
        BC = cfg.BC
        pv = self.pv
        PVt = self.PVt
        col = lambda name, c: PVt.t[:, pv[name][0] + c:pv[name][0] + c + 1]
        L = {n: P.sb("rp_" + n, [128, NP], F32) for n in ("r", "k", "v", "kk", "kd0", "kd1")}
        for c in range(BC):
            xr = self.ldpad(P32, cfg.seg["r"][0] + c * 128)
            r_s = self.tshift(xr, col("mu0", c), col("mu1", c), L["r"])
            self.stpad(D_["R"], c * 128, r_s)
            xk = self.ldpad(P32, cfg.seg["k"][0] + c * 128)
            k_s = self.tshift(xk, col("mu0", BC + c), col("mu1", BC + c), L["k"])
            xv = self.ldpad(P32, cfg.seg["v"][0] + c * 128)
            v_s = self.tshift(xv, col("mu0", 2 * BC + c), col("mu1", 2 * BC + c), L["v"])
            self.stpad(D_["V"], c * 128, v_s)
            kkr = self.wk.get()
            P.op("vector", lambda e: e.tensor_scalar(kkr.t[:, 1:NP - 1], k_s.t[:, 1:NP - 1], col("kk", c), None, ALU.mult),
                 reads=[k_s, PVt], writes=[kkr])
            sq = self.wk.get()
            P.op("gpsimd", lambda e: e.tensor_tensor(sq.t[:, 1:NP - 1], kkr.t[:, 1:NP - 1], kkr.t[:, 1:NP - 1], ALU.mult),
                 reads=[kkr], writes=[sq])
            rn = self.wk.get()

            def f1(ps, c0, cs):
                P.op("scalar", lambda e: e.activation(rn.t[:, c0:c0 + cs], ps.t[:, 0:cs], ACT.Sqrt, bias=self.eps12.t[:, 0:1]),
                     reads=[ps, self.eps12], writes=[rn])
            self.blocksum(sq, f1)
            P.op("vector", lambda e: e.reciprocal(rn.t[:, 1:NP - 1], rn.t[:, 1:NP - 1]), reads=[rn], writes=[rn])
            kk = L["kk"]
            P.op("vector", lambda e: e.tensor_tensor(kk.t[:, 1:NP - 1], kkr.t[:, 1:NP - 1], rn.t[:, 1:NP - 1], ALU.mult),
                 reads=[kkr, rn], writes=[kk])
            self.stpad(D_["KK"], c * 128, kk)
            kds = []
            for d in range(2):
                a = self.ldpad(D_["A%d" % d], c * 128)
                t = self.wk.get()
                P.op("vector", lambda e: e.tensor_scalar(t.t[:, 1:NP - 1], a.t[:, 1:NP - 1], col("ka", c), col("c1", c), ALU.mult, ALU.add),
                     reads=[a, PVt], writes=[t])
                kd = L["kd%d" % d]
                P.op("vector", lambda e: e.tensor_tensor(kd.t[:, 1:NP - 1], k_s.t[:, 1:NP - 1], t.t[:, 1:NP - 1], ALU.mult),
                     reads=[k_s, t], writes=[kd])
                self.stpad(D_["KD%d" % d], c * 128, kd)
                kds.append(kd)
                nb = self.wk.get()
                P.op("vector", lambda e: e.scalar_tensor_tensor(nb.t[:, 1:NP - 1], kk.t[:, 1:NP - 1], -1.0, a.t[:, 1:NP - 1],
                                                                ALU.mult, ALU.mult), reads=[kk, a], writes=[nb])
                self.stpad(D_["NB%d" % d], c * 128, nb)
                lw = self.ldpad(D_["LW%d" % d], c * 128)
                w = self.wk.get()
                P.op("scalar", lambda e: e.activation(w.t[:, 1:NP - 1], lw.t[:, 1:NP - 1], ACT.Exp), reads=[lw], writes=[w])
                self.stpad(D_["W%d" % d], c * 128, w)
            s_ = self.wk.get()
            P.op("gpsimd", lambda e: e.tensor_tensor(s_.t[:, 1:NP - 1], kds[0].t[:, 1:NP - 1], kds[1].t[:, 1:NP - 1], ALU.add),
                 reads=kds, writes=[s_])
            pr = self.wk.get()
            P.op("vector", lambda e: e.scalar_tensor_tensor(pr.t[:, 1:NP - 1], s_.t[:, 1:NP - 1], col("rk", c), r_s.t[:, 1:NP - 1],
                                                            ALU.mult, ALU.mult), reads=[s_, r_s, PVt], writes=[pr])
            bon = self.wk.get()

            def f2(ps, c0, cs):
                P.op("vector", lambda e: e.tensor_tensor(bon.t[:, c0:c0 + cs], ps.t[:, 0:cs], v_s.t[:, c0:c0 + cs], ALU.mult),
                     reads=[ps, v_s], writes=[bon])
            self.blocksum(pr, f2)
            self.stpad(D_["BON"], c * 128, bon)

    def rwkv_scan(self, D_, TC=32):
        P = self.P
        cfg = self.cfg
        G = cfg.BC
        GV = G * 64
        H = cfg.H_RW
        C, NT = cfg.C, cfg.NT
        S = []
        for d in range(2):
            s = {}
            s["M"] = P.sb("sM%d" % d, [128, GV], F32)
            P.op("vector", lambda e, s=s: e.memset(s["M"].t[:, :], 0.0), writes=[s["M"]])
            for nm in ("KK", "R", "KD", "NB", "W", "V"):
                s[nm] = P.sb("s%s%d" % (nm, d), [128, G, TC], F32)
            s["LK"] = P.sb("sLK%d" % d, [128, TC, 16 * ((H + 15) // 16)], F32)
            s["LR"] = P.sb("sLR%d" % d, [128, TC, 16 * ((H + 15) // 16)], F32)
            for nm in ("LK", "LR"):
                P.op("gpsimd", lambda e, t=s[nm]: e.memset(t.t[:, :, :], 0.0), writes=[s[nm]])
            s["VT"] = P.sb("sVT%d" % d, [TC, G * 128], F32)
            s["SA"] = P.sb("sSA%d" % d, [16, GV], F32)
            s["T1"] = P.sb("sT1%d" % d, [128, GV], F32)
            s["T2"] = P.sb("sT2%d" % d, [128, GV], F32)
            s["OM"] = P.sb("sOM%d" % d, [16, 8, GV], F32)
            s["OR"] = P.sb("sOR%d" % d, [16, TC, 64], F32)
            s["pA"] = P.ps("pA%d" % d, [128, 512], F32)
            s["pB"] = P.ps("pB%d" % d, [128, 512], F32)
            s["pS"] = P.ps("pS%d" % d, [128, 512], F32)
            s["pV"] = P.ps("pV%d" % d, [128, 512], F32)
            S.append(s)
        HP = H
        assert HP <= 16 and GV <= 512

        def chunks(d):
            out = []
            for a, b in ((0, C), (C, NT)):
                cs = [(t0, min(TC, b - t0)) for t0 in range(a, b, TC)]
                if d == 1:
                    cs = cs[::-1]
                out += cs
            return out
        ch = [chunks(0), chunks(1)]
        src = [dict(KK="KK", R="R", KD="KD0", NB="NB0", W="W0", V="V"), dict(KK="KK", R="R", KD="KD1", NB="NB1", W="W1", V="V")]

        def bcast(t, ti):
            return t.t[:, :, ti:ti + 1].broadcast_to([128, G, 64])

        def load_chunk(d, t0, tn):
            s = S[d]
            for nm in ("KK", "R", "KD", "NB", "W", "V"):
                dr = D_[src[d][nm]]
                P.dma("sync", s[nm], s[nm].t[:, :, 0:tn], dr, dr.t[:, t0:t0 + tn].rearrange("(g p) t -> p g t", p=128))
            for nm, L in (("KK", "LK"), ("R", "LR")):
                for par in range(2):
                    lo = par * 64
                    outv = s[L].t[lo:lo + 64, 0:tn, 0:2 * G].rearrange("p t (g two) -> p t g two", two=2)[:, :, :, par]
                    inv = s[nm].t[lo:lo + 64, :, 0:tn].rearrange("p g t -> p t g")
                    P.op("gpsimd", lambda e, outv=outv, inv=inv: e.tensor_copy(outv, inv), reads=[s[nm]], writes=[s[L]])
            for g0 in range(0, G, 4):
                nb = min(4, G - g0)
                for j in range(nb):
                    g = g0 + j
                    P.op("tensor", lambda e, g=g, j=j: e.matmul(s["pB"].t[0:tn, j * 128:(j + 1) * 128], s["V"].t[:, g, 0:tn],
                                                                self.ident32.t[:, :], start=True, stop=True),
                         reads=[s["V"], self.ident32], writes=[s["pB"]])
                self.evac("scalar", s["VT"].t[0:tn, g0 * 128:(g0 + nb) * 128], s["pB"].t[0:tn, 0:nb * 128], [s["pB"]], [s["VT"]])

        def mm_sa(d, ti):
            s = S[d]
            P.op("tensor", lambda e: e.matmul(s["pA"].t[0:HP, 0:GV], s["LK"].t[:, ti, 0:HP], s["M"].t[:, :], start=True, stop=True),
                 reads=[s["LK"], s["M"]], writes=[s["pA"]])

        def mm_o(d, ti):
            s = S[d]
            P.op("tensor", lambda e: e.matmul(s["pB"].t[0:HP, 0:GV], s["LR"].t[:, ti, 0:HP], s["M"].t[:, :], start=True, stop=True),
                 reads=[s["LR"], s["M"]], writes=[s["pB"]])
            P.op("vector", lambda e: e.tensor_tensor(s["OM"].t[0:HP, ti % 8, :], s["pB"].t[0:HP, 0:GV], self.mask16.t[0:HP, 0:GV], ALU.mult),
                 reads=[s["pB"], self.mask16], writes=[s["OM"]])
            if (d == 0 and ti % 8 == 7) or (d == 1 and ti % 8 == 0):
                t8 = (ti // 8) * 8
                P.op("vector", lambda e: e.tensor_reduce(
                    s["OR"].t[0:HP, t8:t8 + 8, :], s["OM"].t[0:HP, 0:8, :].rearrange("p t (g v) -> p t v g", v=64), AX.X, ALU.add),
                    reads=[s["OM"]], writes=[s["OR"]])

        def step(d, ti, tn):
            s = S[d]
            M3 = s["M"].t[:, :].rearrange("p (g v) -> p g v", v=64)
            for par in range(2):
                rhs = s["VT"].t[0:tn, :].rearrange("t (g q v) -> t g q v", q=2, v=64)[:, :, par, :]
                P.op("tensor", lambda e, par=par, rhs=rhs: e.matmul(
                    s["pV"].t[par * 64:(par + 1) * 64, 0:GV].rearrange("p (g v) -> p g v", v=64),
                    self.onehot.t[0:tn, ti, :], rhs, start=True, stop=True),
                    reads=[self.onehot, s["VT"]], writes=[s["pV"]])
            P.op("vector", lambda e: e.tensor_tensor(s["T2"].t[:, :].rearrange("p (g v) -> p g v", v=64),
                                                     s["pV"].t[:, 0:GV].rearrange("p (g v) -> p g v", v=64),
                                                     bcast(s["KD"], ti), ALU.mult), reads=[s["pV"], s["KD"]], writes=[s["T2"]])
            mm_sa(d, ti)
            P.op("vector", lambda e: e.tensor_tensor(s["SA"].t[0:HP, :], s["pA"].t[0:HP, 0:GV], self.mask16.t[0:HP, 0:GV], ALU.mult),
                 reads=[s["pA"], self.mask16], writes=[s["SA"]])
            P.op("tensor", lambda e: e.matmul(s["pS"].t[:, 0:GV], self.esel.t[0:HP, :], s["SA"].t[0:HP, :], start=True, stop=True),
                 reads=[self.esel, s["SA"]], writes=[s["pS"]])
            P.op("vector", lambda e: e.tensor_tensor(s["T1"].t[:, :].rearrange("p (g v) -> p g v", v=64),
                                                     s["pS"].t[:, 0:GV].rearrange("p (g v) -> p g v", v=64),
                                                     bcast(s["NB"], ti), ALU.mult), reads=[s["pS"], s["NB"]], writes=[s["T1"]])
            P.op("gpsimd", lambda e: e.tensor_tensor(M3, M3, bcast(s["W"], ti), ALU.mult), reads=[s["M"], s["W"]], writes=[s["M"]])
            P.op("gpsimd", lambda e: e.tensor_tensor(s["M"].t[:, :], s["M"].t[:, :], s["T2"].t[:, :], ALU.add),
                 reads=[s["M"], s["T2"]], writes=[s["M"]])
            P.op("vector", lambda e: e.tensor_tensor(s["M"].t[:, :], s["M"].t[:, :], s["T1"].t[:, :], ALU.add),
                 reads=[s["M"], s["T1"]], writes=[s["M"]])
            mm_o(d, ti)

        def flush(d, t0, tn):
            s = S[d]
            assert tn % 8 == 0
            dst = D_["O%d" % d]
            P.dma("gpsimd", dst, dst.t[t0:t0 + tn, :].rearrange("t (h v) -> h t v", v=64), s["OR"], s["OR"].t[0:HP, 0:tn, :])

        nch = len(ch[0])
        for ci in range(nch):
            for d in range(2):
                t0, tn = ch[d][ci]
                load_chunk(d, t0, tn)
            order = [list(range(ch[0][ci][1])), list(range(ch[1][ci][1]))[::-1]]
            for si in range(max(len(order[0]), len(order[1]))):
                for d in range(2):
                    if si < len(order[d]):
                        step(d, order[d][si], ch[d][ci][1])
            for d in range(2):
                flush(d, *ch[d][ci])

    def rwkv_readout(self, D_, Y, yrow0, lng, lnb):
        P = self.P
        cfg = self.cfg
        BW, H = cfg.BW, cfg.H_RW
        tk = Pool([P.sb("ro%d" % i, [128, BW], F32) for i in range(5)])
        sm = Pool([P.sb("rs%d" % i, [128, H], F32) for i in range(4)])
        fm = Pool([P.sb("rf%d" % i, [128, 128], F32) for i in range(4)])
        fmb = Pool([P.sb("rfb%d" % i, [128, 128], BF16) for i in range(2)])
        for t0 in range(0, cfg.NT, 128):
            a = tk.get()
            b = tk.get()
            P.dma("sync", a, a.t[:, :], D_["O0"], D_["O0"].t[t0:t0 + 128, :])
            P.dma("sync", b, b.t[:, :], D_["O1"], D_["O1"].t[t0:t0 + 128, :])
            P.op("vector", lambda e: e.tensor_tensor(a.t[:, :], a.t[:, :], b.t[:, :], ALU.add), reads=[a, b], writes=[a])
            a3 = a.t[:, :].rearrange("p (h v) -> p h v", v=64)
            mean = sm.get()
            P.op("vector", lambda e: e.tensor_reduce(mean.t[:, :], a3, AX.X, ALU.add), reads=[a], writes=[mean])
            P.op("vector", lambda e: e.tensor_scalar(mean.t[:, :], mean.t[:, :], -1.0 / 64, None, ALU.mult), reads=[mean], writes=[mean])
            P.op("vector", lambda e: e.tensor_tensor(a3, a3, mean.t[:, :].unsqueeze(2).broadcast_to([128, H, 64]), ALU.add),
                 reads=[a, mean], writes=[a])
            sq = tk.get()
            P.op("gpsimd", lambda e: e.tensor_tensor(sq.t[:, :], a.t[:, :], a.t[:, :], ALU.mult), reads=[a], writes=[sq])
            var = sm.get()
            P.op("vector", lambda e: e.tensor_reduce(var.t[:, :], sq.t[:, :].rearrange("p (h v) -> p h v", v=64), AX.X, ALU.add),
                 reads=[sq], writes=[var])
            P.op("scalar", lambda e: e.activation(var.t[:, :], var.t[:, :], ACT.Sqrt, bias=self.epsgn.t[:, 0:1], scale=1.0 / 64),
                 reads=[var, self.epsgn], writes=[var])
            P.op("vector", lambda e: e.reciprocal(var.t[:, :], var.t[:, :]), reads=[var], writes=[var])
            P.op("vector", lambda e: e.tensor_tensor(a3, a3, var.t[:, :].unsqueeze(2).broadcast_to([128, H, 64]), ALU.mult),
                 reads=[a, var], writes=[a])
            P.op("vector", lambda e: e.tensor_tensor(a.t[:, :], a.t[:, :], lng.t[:, :], ALU.mult), reads=[a, lng], writes=[a])
            P.op("gpsimd", lambda e: e.tensor_tensor(a.t[:, :], a.t[:, :], lnb.t[:, :], ALU.add), reads=[a, lnb], writes=[a])
            for c in range(cfg.BC):
                ps = self.pspool.get()
                P.op("tensor", lambda e, c=c: e.matmul(ps.t[:, 0:128], a.t[:, c * 128:(c + 1) * 128], self.ident32.t[:, :],
                                                       start=True, stop=True), reads=[a, self.ident32], writes=[ps])
                bon = fm.get()
                g = fm.get()
                P.dma("sync", bon, bon.t[:, :], D_["BON"], D_["BON"].t[c * 128:(c + 1) * 128, t0:t0 + 128])
                P.dma("sync", g, g.t[:, :], D_["G"], D_["G"].t[c * 128:(c + 1) * 128, t0:t0 + 128])
                P.op("vector", lambda e: e.tensor_tensor(bon.t[:, :], ps.t[:, 0:128], bon.t[:, :], ALU.add), reads=[ps, bon], writes=[bon])
                ob = fmb.get()
                P.op("vector", lambda e: e.tensor_tensor(ob.t[:, :], bon.t[:, :], g.t[:, :], ALU.mult), reads=[bon, g], writes=[ob])
                P.dma("gpsimd", Y, Y.t[yrow0 + c * 128:yrow0 + (c + 1) * 128, t0:t0 + 128], ob, ob.t[:, :])

    def conv(self, P32, Y, yrow0):
        P = self.P
        cfg = self.cfg
        NP = self.NP
        pv = self.pv
        col = lambda name, c: self.PVt.t[:, pv[name][0] + c:pv[name][0] + c + 1]
        for c in range(cfg.BC):
            xb = self.ldpad(P32, cfg.seg["cb"][0] + c * 128)
            xc = self.ldpad(P32, cfg.seg["cc"][0] + c * 128)
            xu = self.ldpad(P32, cfg.seg["cu"][0] + c * 128)
            z = self.wk.get()
            P.op("vector", lambda e: e.tensor_tensor(z.t[:, :], xc.t[:, :], xu.t[:, :], ALU.mult), reads=[xc, xu], writes=[z])
            y = self.wk.get()
            P.op("vector", lambda e: e.tensor_scalar(y.t[:, 1:NP - 1], z.t[:, 0:NP - 2], col("cw0", c), None, ALU.mult),
                 reads=[z, self.PVt], writes=[y])
            P.op("vector", lambda e: e.scalar_tensor_tensor(y.t[:, 1:NP - 1], z.t[:, 1:NP - 1], col("cw1", c), y.t[:, 1:NP - 1],
                                                            ALU.mult, ALU.add), reads=[z, y, self.PVt], writes=[y])
            P.op("vector", lambda e: e.scalar_tensor_tensor(y.t[:, 1:NP - 1], z.t[:, 2:NP], col("cw2", c), y.t[:, 1:NP - 1],
                                                            ALU.mult, ALU.add), reads=[z, y, self.PVt], writes=[y])
            ob = self.wkb.get()
            P.op("vector", lambda e: e.tensor_tensor(ob.t[:, 1:NP - 1], y.t[:, 1:NP - 1], xb.t[:, 1:NP - 1], ALU.mult),
                 reads=[y, xb], writes=[ob])
            self.stpad(Y, yrow0 + c * 128, ob)

    def merge(self, GL, Y, wfull, ACC16):
        P = self.P
        cfg = self.cfg
        pv = self.pv
        BWC = cfg.BC
        nmax = max(128, min(512, (16384 // (2 + 4 * BWC)) // 128 * 128))
        sub = []
        for a_, ns_ in cfg.tiles:
            o_ = 0
            while o_ < ns_:
                s_ = min(nmax, ns_ - o_)
                sub.append((a_ + o_, s_))
                o_ += s_
        for n0, nsz in sub:
            xt = self.xpool.get()
            flat = xt.t
            P.dma("sync", xt, flat[:, 0:2 * nsz].rearrange("p (kc n) -> p kc n", n=nsz),
                  GL, GL.t[0:256, n0:n0 + nsz].rearrange("(kc p) n -> p kc n", p=128))
            P.dma("sync", xt, flat[:, 2 * nsz:(2 + 4 * BWC) * nsz].rearrange("p (kc n) -> p kc n", n=nsz),
                  Y, Y.t[0:4 * cfg.BW, n0:n0 + nsz].rearrange("(kc p) n -> p kc n", p=128))
            for mc in range(cfg.DC):
                acc = self.accpool.get()
                for i in range(4):
                    wg = self.wpool.get()
                    P.dma("sync", wg, wg.t[:, 0:256], wfull[0], self.wview(wfull, "gu_%d" % i, mc)[:, 0:256])
                    P.dma("sync", wg, wg.t[:, 256:256 + BWC * 128], wfull[0], self.wview(wfull, "wb_%d" % i, mc)[:, 0:BWC * 128])
                    pg = self.pspool.get()
                    for kc in range(2):
                        P.op("tensor", lambda e, kc=kc: e.matmul(pg.t[:, 0:nsz], wg.t[:, kc * 128:(kc + 1) * 128],
                                                                 flat[:, kc * nsz:(kc + 1) * nsz], start=(kc == 0), stop=(kc == 1)),
                             reads=[wg, xt], writes=[pg])
                    gt = self.opool.get()
                    gb = self.PVt.t[:, pv["gb_%d" % i][0] + mc:pv["gb_%d" % i][0] + mc + 1]
                    P.op("scalar", lambda e: e.activation(gt.t[:, 0:nsz], pg.t[:, 0:nsz], ACT.Sigmoid, bias=gb),
                         reads=[pg, self.PVt], writes=[gt])
                    pb = self.pspool.get()
                    for kc in range(BWC):
                        xc = 2 + i * BWC + kc
                        P.op("tensor", lambda e, kc=kc, xc=xc: e.matmul(pb.t[:, 0:nsz], wg.t[:, 256 + kc * 128:256 + (kc + 1) * 128],
                                                                        flat[:, xc * nsz:(xc + 1) * nsz], start=(kc == 0), stop=(kc == BWC - 1)),
                             reads=[wg, xt], writes=[pb])
                    if i == 0:
                        P.op("vector", lambda e: e.tensor_tensor(acc.t[:, 0:nsz], pb.t[:, 0:nsz], gt.t[:, 0:nsz], ALU.mult),
                             reads=[pb, gt], writes=[acc])
                    else:
                        P.op("vector", lambda e: e.tensor_tensor(gt.t[:, 0:nsz], pb.t[:, 0:nsz], gt.t[:, 0:nsz], ALU.mult),
                             reads=[pb, gt], writes=[gt])
                        P.op("gpsimd", lambda e: e.tensor_tensor(acc.t[:, 0:nsz], acc.t[:, 0:nsz], gt.t[:, 0:nsz], ALU.add),
                             reads=[acc, gt], writes=[acc])
                ob = self.opoolb.get()
                P.op("vector", lambda e: e.tensor_copy(ob.t[:, 0:nsz], acc.t[:, 0:nsz]), reads=[acc], writes=[ob])
                P.dma("gpsimd", ACC16, ACC16.t[mc * 128:(mc + 1) * 128, n0:n0 + nsz], ob, ob.t[:, 0:nsz])

    def epi_residual(self, xold, xnew, j):
        P = self.P
        cfg = self.cfg

        def epi(ps, m0, msz, n0, nsz):
            ci = cfg.cond_of(n0)
            xt = self.ld(self.opool, xold, m0, msz, n0, nsz)
            mcol = self.modS.t[0:msz, j * cfg.DC + m0 // 128, ci:ci + 1]
            P.op("vector", lambda e: e.scalar_tensor_tensor(xt.t[0:msz, 0:nsz], ps.t[0:msz, 0:nsz], mcol, xt.t[0:msz, 0:nsz],
                                                            ALU.mult, ALU.add), reads=[ps, xt, self.modS], writes=[xt])
            self.st(xnew, m0, msz, n0, nsz, xt)
        return epi

    def modulation(self, COND16, wfull, T16):
        P = self.P
        cfg = self.cfg
        pv = self.pv
        self.linear(COND16, cfg.D, wfull, "mod_down", cfg.MR, self.epi_store(dst16=T16), tiles=[(0, 2)])

        def epi(ps, m0, msz, n0, nsz):
            mc = m0 // 128
            b = self.PVt.t[:, pv["modb"][0] + mc:pv["modb"][0] + mc + 1]
            P.op("scalar", lambda e: e.activation(self.modS.t[:, mc, 0:2], ps.t[:, 0:2], ACT.Identity, bias=b),
                 reads=[ps, self.PVt], writes=[self.modS])
        self.linear(T16, cfg.MR, wfull, "mod_up", 6 * cfg.D, epi, tiles=[(0, 2)])
        DC = cfg.DC
        for k, (gname, j) in enumerate((("n1g", 1), ("n2g", 4))):
            g = self.PVt.t[:, pv[gname][0]:pv[gname][0] + DC]
            out = self.modA.t[:, k * DC:(k + 1) * DC, :]
            P.op("vector", lambda e, out=out, j=j: e.tensor_scalar(out, self.modS.t[:, j * DC:(j + 1) * DC, :], 1.0, None, ALU.add),
                 reads=[self.modS], writes=[self.modA])
            P.op("vector", lambda e, out=out, g=g: e.tensor_tensor(out, out, g.unsqueeze(2).broadcast_to([128, DC, 2]), ALU.mult),
                 reads=[self.modA, self.PVt], writes=[self.modA])

import math
import numpy as np
import concourse.bass as bass
import concourse.mybir as mybir


TC_SCAN = 32


def pv_layout(cfg):
    DC, BC = cfg.DC, cfg.BC
    QC, KVC = pad128(cfg.QL) // 128, pad128(cfg.KVL) // 128
    RC = 3 * BC + 4
    items = [("n1g", DC), ("n2g", DC), ("modb", 6 * DC), ("qg", QC), ("kvg", KVC), ("mu0", RC), ("mu1", RC),
             ("w0_0", BC), ("w0_1", BC), ("a0_0", BC), ("a0_1", BC), ("kk", BC), ("ka", BC), ("c1", BC), ("rk", BC),
             ("cw0", BC), ("cw1", BC), ("cw2", BC), ("gb_0", DC), ("gb_1", DC), ("gb_2", DC), ("gb_3", DC)]
    pv = {}
    o = 0
    for n, k in items:
        pv[n] = (o, k)
        o += k
    return pv, o


def build_program(cfg, debug_outs=()):
    nc = bass.Bass("TRN2", target_bir_lowering=False)
    P = Prog(nc)
    m = Model(P, cfg)
    D, NT, C, T, BW, L = cfg.D, cfg.NT, cfg.C, cfg.T, cfg.BW, cfg.DEPTH
    DC, BC = cfg.DC, cfg.BC
    H = cfg.H_MLA
    QLp, KVLp = pad128(cfg.QL), pad128(cfg.KVL)
    pv, PVN = pv_layout(cfg)
    m.pv = pv

    def inp(name, shape):
        return P.dram(name, shape, F32, kind="ExternalInput")
    xT = inp("xT", [D, NT])
    cond = inp("cond", [D, 2])
    pvec = inp("pvec", [L * 128, PVN])
    bcin = inp("bcin", [L * 128, 2 * BW + 128])
    dlin = inp("dlin", [L, 256])
    fng = inp("fng", [128, DC])
    c_ident = inp("c_ident", [128, 128])
    c_blk1 = inp("c_blk1", [128, 128])
    c_mask16 = inp("c_mask16", [16, 512])
    c_esel = inp("c_esel", [16, 128])
    c_onehot = inp("c_onehot", [TC_SCAN, TC_SCAN * 64])
    c_cos = inp("c_cos", [128, NT])
    c_sin = inp("c_sin", [128, NT])
    NG = cfg.NG
    shard_rows = [cfg.wrows[g] // cfg.ncores for g in range(NG)]
    wsh = [inp("wflat%d" % g, [L * shard_rows[g], WCOLS]) for g in range(NG)]
    yT = P.dram("yT", [D, T], F32, kind="ExternalOutput")
    dbg = {}

    wfull = []
    for l in range(L):
        grp = []
        for g in range(NG):
            wf = P.dram("wfull%d_%d" % (l, g), [cfg.wrows[g], WCOLS], BF16)
            sr = shard_rows[g]
            if cfg.ncores == 1:
                tgt = wf
            else:
                tgt = P.dram("wsh16_%d_%d" % (l, g), [sr, WCOLS], BF16)
            for r0 in range(0, sr, 4096):
                rn = min(4096, sr - r0)
                P.dma("gpsimd", tgt, tgt.t[r0:r0 + rn, :], wsh[g], wsh[g].t[l * sr + r0:l * sr + r0 + rn, :])
            if cfg.ncores > 1:
                if wf.sem is None:
                    wf.sem = P._mksem("D_cc_%d_%d" % (l, g))
                key = wf.sem
                waits = P._deps("gpsimd", [tgt], [wf], key)
                for k, v in waits:
                    nc.gpsimd.wait_ge(P.sems[k], v)
                P.cnt[key] += 1
                nc.gpsimd.collective_compute("AllGather", ALU.bypass, replica_groups=[list(range(cfg.ncores))],
                                             ins=[tgt.t[:, :]], outs=[wf.t[:, :]]).then_inc(P.sems[key], 1)
                tgt.r[key] = P.cnt[key]
                wf.w = {key: P.cnt[key]}
                wf.r = {}
            grp.append(wf)
        wfull.append(grp)

    def const_tile(name, src, shape, dt=F32):
        t = P.sb(name, shape, dt)
        P.dma("sync", t, t.t[:], src, src.t[:])
        return t
    m.ident32 = const_tile("ident32", c_ident, [128, 128])
    m.blk1 = const_tile("blk1", c_blk1, [128, 128])
    m.mask16 = const_tile("mask16", c_mask16, [16, 512])
    m.esel = const_tile("esel", c_esel, [16, 128])
    oh = P.sb("onehot", [TC_SCAN, TC_SCAN, 64], F32)
    P.dma("sync", oh, oh.t[:].rearrange("p a b -> p (a b)"), c_onehot, c_onehot.t[:])
    m.onehot = oh
    m.cos2 = const_tile("cos2", c_cos, [128, NT])
    m.sin2 = const_tile("sin2", c_sin, [128, NT])
    m.identb = P.sb("identb", [128, 128], BF16)
    P.op("vector", lambda e: e.tensor_copy(m.identb.t[:], m.ident32.t[:]), reads=[m.ident32], writes=[m.identb])
    m.ones = P.sb("ones", [128, 128], F32)
    P.op("vector", lambda e: e.memset(m.ones.t[:], 1.0), writes=[m.ones])
    for nm, val in (("eps_t", NORM_EPS), ("eps12", 1e-12), ("epsgn", GN_EPS)):
        t = P.sb(nm, [128, 1], F32)
        P.op("vector", lambda e, t=t, val=val: e.memset(t.t[:], val), writes=[t])
        setattr(m, nm, t)
    fngt = const_tile("fngt", fng, [128, DC])
    m.PVt = P.sb("PVt", [128, PVN], F32)
    m.modS = P.sb("modS", [128, 6 * DC, 2], F32)
    m.modA = P.sb("modA", [128, 2 * DC, 2], F32)
    bct = P.sb("bct", [128, 2 * BW + 128], F32)
    lam_row = P.sb("lam_row", [1, 260], F32)
    lam_col = P.sb("lam_col", [128, 2], F32)
    gtile = P.sb("gtile", [128, 128], F32)

    def dr(name, shape, dt=F32):
        return P.dram(name, shape, dt)
    xres = [xT, dr("xresA", [D, NT]), dr("xresB", [D, NT])]
    hB = dr("hB", [D, NT], BF16)
    P32 = dr("P32", [cfg.PCOLS, NT])
    CQN = dr("CQN", [QLp, NT], BF16)
    CKVN = dr("CKVN", [KVLp, NT], BF16)
    Q32 = dr("Q32", [H * 128, NT])
    QN = dr("QN", [H * 128, NT], BF16)
    QR = dr("QR", [H * 64, NT], BF16)
    KN = dr("KN", [H * 128, NT], BF16)
    VM = dr("VM", [H * 128, NT], BF16)
    KR = dr("KR", [64, NT], BF16)
    DQ = dr("DQ", [BW, NT], BF16)
    DK = dr("DK", [BW, NT], BF16)
    DV = dr("DV", [BW, NT], BF16)
    Y = dr("Y", [4 * BW, NT], BF16)
    RW = {n: dr("RW_" + n, [BW, NT]) for n in ("R", "V", "KK", "KD0", "KD1", "NB0", "NB1", "W0", "W1", "BON", "G",
                                               "LW0", "LW1", "A0", "A1")}
    RW["LORA16"] = dr("LORA16", [512, NT], BF16)
    RW["O0"] = dr("RW_O0", [NT, BW])
    RW["O1"] = dr("RW_O1", [NT, BW])
    GL = dr("GL", [256, NT], BF16)
    ACC16 = dr("ACC16", [D, NT], BF16)
    HID16 = dr("HID16", [cfg.DFF, NT], BF16)
    COND16 = dr("COND16", [D, 2], BF16)
    T16 = dr("T16", [cfg.MR, 2], BF16)

    with P.scope():
        ct = P.sb("condt", [128, DC, 2], F32)
        cb = P.sb("condb", [128, DC, 2], BF16)
        P.dma("sync", ct, ct.t[:], cond, cond.t[:, :].rearrange("(c p) n -> p c n", p=128))
        P.op("scalar", lambda e: e.activation(cb.t[:], ct.t[:], ACT.Silu), reads=[ct], writes=[cb])
        P.dma("gpsimd", COND16, COND16.t[:, :].rearrange("(c p) n -> p c n", p=128), cb, cb.t[:])

    col = lambda name, c: m.PVt.t[:, pv[name][0] + c:pv[name][0] + c + 1]
    cur = 0
    for l in range(L):
        wf = wfull[l]
        lam_init = 0.8 - 0.6 * math.exp(-0.3 * l)
        P.dma("sync", m.PVt, m.PVt.t[:, :], pvec, pvec.t[l * 128:(l + 1) * 128, :])
        P.dma("sync", bct, bct.t[:, :], bcin, bcin.t[l * 128:(l + 1) * 128, :])
        P.dma("sync", lam_row, lam_row.t[0:1, 0:256], dlin, dlin.t[l:l + 1, :])
        ka = m.PVt.t[:, pv["ka"][0]:pv["ka"][0] + BC]
        P.op("vector", lambda e: e.tensor_scalar(m.PVt.t[:, pv["c1"][0]:pv["c1"][0] + BC], ka, -1.0, 1.0, ALU.mult, ALU.add),
             reads=[m.PVt], writes=[m.PVt])
        P.op("vector", lambda e: e.tensor_tensor(lam_row.t[0:1, 0:64], lam_row.t[0:1, 0:64], lam_row.t[0:1, 64:128], ALU.mult),
             reads=[lam_row], writes=[lam_row])
        P.op("vector", lambda e: e.tensor_tensor(lam_row.t[0:1, 128:192], lam_row.t[0:1, 128:192], lam_row.t[0:1, 192:256], ALU.mult),
             reads=[lam_row], writes=[lam_row])
        P.op("vector", lambda e: e.reduce_sum(lam_row.t[0:1, 256:257], lam_row.t[0:1, 0:64], AX.X), reads=[lam_row], writes=[lam_row])
        P.op("vector", lambda e: e.reduce_sum(lam_row.t[0:1, 257:258], lam_row.t[0:1, 128:192], AX.X), reads=[lam_row], writes=[lam_row])
        P.op("scalar", lambda e: e.activation(lam_row.t[0:1, 256:258], lam_row.t[0:1, 256:258], ACT.Exp), reads=[lam_row], writes=[lam_row])
        P.op("vector", lambda e: e.tensor_tensor(lam_row.t[0:1, 258:259], lam_row.t[0:1, 257:258], lam_row.t[0:1, 256:257], ALU.subtract),
             reads=[lam_row], writes=[lam_row])
        P.op("vector", lambda e: e.tensor_scalar(lam_row.t[0:1, 258:259], lam_row.t[0:1, 258:259], -lam_init, None, ALU.add),
             reads=[lam_row], writes=[lam_row])
        P.op("vector", lambda e: e.tensor_copy(lam_row.t[0:1, 259:260], lam_row.t[0:1, 258:259]), reads=[lam_row], writes=[lam_row])
        with P.scope():
            pl = P.ps("pl", [128, 512], F32)
            P.op("tensor", lambda e: e.matmul(pl.t[:, 0:2], m.ones.t[0:1, :], lam_row.t[0:1, 258:260], start=True, stop=True),
                 reads=[m.ones, lam_row], writes=[pl])
            P.op("vector", lambda e: e.tensor_copy(lam_col.t[:, 0:2], pl.t[:, 0:2]), reads=[pl], writes=[lam_col])
        P.op("vector", lambda e: e.tensor_scalar(gtile.t[:, :], bct.t[:, 2 * BW:2 * BW + 128], 1.0 - lam_init, None, ALU.mult),
             reads=[bct], writes=[gtile])
        with P.scope():
            m.alloc_linear_bufs()
            m.modulation(COND16, wf, T16)
        xin = xres[cur]
        xmid = xres[(cur + 1) % 3]
        xout = xres[(cur + 2) % 3]
        with P.scope():
            m.pspool = Pool([P.ps("ps%d" % i, [128, 512], F32) for i in range(4)])
            m.alloc_norm_bufs()
            m.rmsnorm(xin, 0, D, hB, 0, lambda c, ci: m.modA.t[:, c, ci:ci + 1], lambda c, ci: m.modS.t[:, c, ci:ci + 1],
                      deps=[m.modA, m.modS], ones=m.ones)
        with P.scope():
            m.alloc_linear_bufs()
            m.linear(hB, D, wf, "w_in", cfg.PCOLS, m.epi_store(dst32=P32))
            m.linear(hB, D, wf, "gate_down", cfg.GR, m.epi_store(dst16=GL))
        if "P32" in debug_outs and l == 0:
            dbg["P32"] = P32
        with P.scope():
            m.pspool = Pool([P.ps("ps%d" % i, [128, 512], F32) for i in range(4)])
            m.alloc_norm_bufs()
            m.rmsnorm(P32, cfg.seg["cq"][0], cfg.QL, CQN, 0, lambda c, ci: col("qg", c), deps=[m.PVt], ones=m.ones)
            m.rmsnorm(P32, cfg.seg["ckv"][0], cfg.KVL, CKVN, 0, lambda c, ci: col("kvg", c), deps=[m.PVt], ones=m.ones)
        with P.scope():
            m.alloc_linear_bufs()
            m.alloc_ew_bufs()
            e_qn = m.epi_store(dst16=QN)
            e_q32 = m.epi_store(dst32=Q32, row0=-H * 128)

            def epi_q(ps, m0, msz, n0, nsz):
                (e_qn if m0 < H * 128 else e_q32)(ps, m0, msz, n0, nsz)
            m.linear(CQN, QLp, wf, "w_uq", H * 256, epi_q)
            e_kn = m.epi_store(dst16=KN)
            e_vm = m.epi_store(dst16=VM, row0=-H * 128)

            def epi_kv(ps, m0, msz, n0, nsz):
                (e_kn if m0 < H * 128 else e_vm)(ps, m0, msz, n0, nsz)
            m.linear(CKVN, KVLp, wf, "w_ukv", H * 256, epi_kv)
            m.rope(Q32, 0, H * 64, H * 64, QR, 0)
            m.rope(P32, cfg.seg["kr"][0], cfg.seg["kr"][0] + 64, 64, KR, 0)
            m.rope(P32, cfg.seg["dq"][0], cfg.seg["dqr"][0], BW, DQ, 0)
            m.rope(P32, cfg.seg["dk"][0], cfg.seg["dkr"][0], BW, DK, 0)
            m.cast_rows(P32, cfg.seg["dv"][0], BW, DV, 0)
        with P.scope():
            m.alloc_attn_bufs()
            m.mla_attention(QN, QR, KN, KR, VM, Y, 0)
            m.diff_attention(DQ, DK, DV, Y, 3 * BW, lam_col, gtile, lam_init)
        with P.scope():
            m.alloc_pad_bufs(3, 4)
            m.rwkv_prep_a(P32, RW)
        with P.scope():
            m.alloc_linear_bufs()
            m.rwkv_prep_b(wf, RW)
        with P.scope():
            m.pspool = Pool([P.ps("ps%d" % i, [128, 512], F32) for i in range(4)])
            m.alloc_pad_bufs(6, 4)
            m.rwkv_prep_c(P32, RW)
        with P.scope():
            m.rwkv_scan(RW, TC=TC_SCAN)
        with P.scope():
            m.pspool = Pool([P.ps("ps%d" % i, [128, 512], F32) for i in range(4)])
            lng = Res("lng", bct.t[:, 0:BW])
            lnb = Res("lnb", bct.t[:, BW:2 * BW])
            lng.w = lnb.w = bct.w
            lng.r = lnb.r = bct.r
            m.rwkv_readout(RW, Y, BW, lng, lnb)
        with P.scope():
            m.alloc_pad_bufs(4, 4)
            m.conv(P32, Y, 2 * BW)
        if l == 0:
            for n_ in debug_outs:
                if n_ == "Y":
                    dbg["Y"] = Y
                if n_ in RW:
                    dbg[n_] = RW[n_]
        with P.scope():
            m.alloc_linear_bufs()
            m.merge(GL, Y, wf, ACC16)
            m.linear(ACC16, D, wf, "w_out", D, m.epi_residual(xin, xmid, 2))
        with P.scope():
            m.pspool = Pool([P.ps("ps%d" % i, [128, 512], F32) for i in range(4)])
            m.alloc_norm_bufs()
            m.rmsnorm(xmid, 0, D, hB, 0, lambda c, ci: m.modA.t[:, DC + c, ci:ci + 1], lambda c, ci: m.modS.t[:, 3 * DC + c, ci:ci + 1],
                      deps=[m.modA, m.modS], ones=m.ones)
        with P.scope():
            m.alloc_linear_bufs()
            m.linear(hB, D, wf, "mlp_w1", cfg.DFF, m.epi_store(dst16=HID16, func=ACT.Relu, square=True))
            m.linear(HID16, cfg.DFF, wf, "mlp_w2", D, m.epi_residual(xmid, xout, 5), nmax=(128 if cfg.DFF > 4096 else 512))
        cur = (cur + 2) % 3
        if l == 0 and "X1" in debug_outs:
            dbg["X1"] = xres[cur]
        if l == 0 and "XMID" in debug_outs:
            dbg["XMID"] = xmid
    with P.scope():
        m.pspool = Pool([P.ps("ps%d" % i, [128, 512], F32) for i in range(4)])
        m.alloc_norm_bufs(out_dt=F32)
        xt_tiles = [(a, s) for a, s in cfg.tiles if a >= C]
        m.rmsnorm(xres[cur], 0, D, yT, 0, lambda c, ci: fngt.t[:, c:c + 1], deps=[fngt], ones=m.ones, tiles=xt_tiles, dcol=C)
    outs = [yT]
    dbg_out = {}
    for n_, r in dbg.items():
        shp = list(r.t.shape)
        o = P.dram("dbg_" + n_, shp, F32, kind="ExternalOutput")
        P.dma("gpsimd", o, o.t[:, :], r, r.t[:, :])
        outs.append(o)
        dbg_out[n_] = "dbg_" + n_
    P.finish(outs)
    return nc, P, dbg_out


def tile_weight(W):
    K, M = W.shape
    KC, MC = cdiv(K, 128), cdiv(M, 128)
    Wp = np.zeros((KC * 128, MC * 128), np.float32)
    Wp[:K, :M] = W
    return np.ascontiguousarray(Wp.reshape(KC, 128, MC, 128).transpose(2, 1, 0, 3)).reshape(-1)


def pp(v, nchunks=None):
    v = np.asarray(v, np.float32).reshape(-1)
    k = cdiv(v.size, 128) if nchunks is None else nchunks
    o = np.zeros(k * 128, np.float32)
    o[:v.size] = v
    return np.ascontiguousarray(o.reshape(k, 128).T)


def rot64(idx):
    idx = np.asarray(idx).reshape(-1, 64)
    return np.concatenate([idx[:, 32:], idx[:, :32]], axis=1).reshape(-1)


def host_layer_weights(cfg, inp, l):
    D, BW, H = cfg.D, cfg.BW, cfg.H_MLA
    QL, KVL = cfg.QL, cfg.KVL
    w_in = inp["w_in"][l]
    o_cq, o_ckv, o_kr = 0, QL, QL + KVL
    o_rw = QL + KVL + 64
    o_r, o_k, o_v = o_rw, o_rw + BW, o_rw + 2 * BW
    o_wd = o_rw + 3 * BW
    o_ad = o_wd + 128
    o_gd = o_ad + 128
    o_cv = o_gd + 160
    o_df = o_cv + 3 * BW
    ext = np.zeros((D, cfg.PCOLS), np.float32)

    def put(name, cols):
        a, s = cfg.seg[name]
        ext[:, a:a + len(cols)] = w_in[:, cols]
    put("cq", np.arange(o_cq, o_cq + QL))
    put("ckv", np.arange(o_ckv, o_ckv + KVL))
    kr = np.arange(o_kr, o_kr + 64)
    put("kr", np.concatenate([kr, rot64(kr)]))
    put("r", np.arange(o_r, o_r + BW))
    put("k", np.arange(o_k, o_k + BW))
    put("v", np.arange(o_v, o_v + BW))
    put("wd", np.arange(o_wd, o_wd + 128))
    put("ad", np.arange(o_ad, o_ad + 128))
    put("gd", np.arange(o_gd, o_gd + 160))
    put("cb", np.arange(o_cv, o_cv + BW))
    put("cc", np.arange(o_cv + BW, o_cv + 2 * BW))
    put("cu", np.arange(o_cv + 2 * BW, o_cv + 3 * BW))
    dq = np.arange(o_df, o_df + BW)
    dk = np.arange(o_df + BW, o_df + 2 * BW)
    put("dq", dq)
    put("dk", dk)
    put("dv", np.arange(o_df + 2 * BW, o_df + 3 * BW))
    put("dqr", rot64(dq))
    put("dkr", rot64(dk))
    wq = inp["mla_w_uq"][l]
    qn = np.concatenate([np.arange(h * 192, h * 192 + 128) for h in range(H)])
    qr = np.concatenate([np.arange(h * 192 + 128, h * 192 + 192) for h in range(H)])
    wq_ext = wq[:, np.concatenate([qn, qr, rot64(qr)])]
    wkv = inp["mla_w_ukv"][l]
    kn = np.concatenate([np.arange(h * 256, h * 256 + 128) for h in range(H)])
    vv = np.concatenate([np.arange(h * 256 + 128, h * 256 + 256) for h in range(H)])
    wkv_ext = wkv[:, np.concatenate([kn, vv])]
    g2 = np.zeros((256, BW), np.float32)
    g2[:160] = inp["rwkv_g2"][l]
    ws = {
        "mod_down": inp["mod_down"][l], "mod_up": inp["mod_up"][l], "w_in": ext, "w_uq": wq_ext, "w_ukv": wkv_ext,
        "w2_0": inp["rwkv_w2"][l, 0], "w2_1": inp["rwkv_w2"][l, 1], "a2_0": inp["rwkv_a2"][l, 0], "a2_1": inp["rwkv_a2"][l, 1],
        "g2": g2, "gate_down": inp["gate_down"][l], "w_out": inp["w_out"][l], "mlp_w1": inp["mlp_w1"][l], "mlp_w2": inp["mlp_w2"][l],
    }
    for i in range(4):
        ws["wb_%d" % i] = inp["w_branch"][l, i]
        ws["gu_%d" % i] = inp["gate_up"][l][:, i, :]
    flats = [np.zeros(cfg.wrows[g] * WCOLS, np.float32) for g in range(cfg.NG)]
    for n, K, M in cfg.wshapes:
        off = cfg.woff[n][0]
        w = ws[n]
        wp = np.zeros((K, M), np.float32)
        wp[:w.shape[0], :w.shape[1]] = w
        t = tile_weight(wp)
        flats[cfg.wgroup_of[n]][off:off + t.size] = t
    return [flats[g].reshape(cfg.wrows[g], WCOLS) for g in range(cfg.NG)]


def host_pvec(cfg, inp, l):
    pv, PVN = pv_layout(cfg)
    BW, BC = cfg.BW, cfg.BC
    out = np.zeros((128, PVN), np.float32)

    def put(name, arr):
        a, k = pv[name]
        out[:, a:a + k] = pp(arr, k)
    put("n1g", inp["norm1_g"][l])
    put("n2g", inp["norm2_g"][l])
    put("modb", inp["mod_b"][l])
    put("qg", inp["mla_q_norm_g"][l])
    put("kvg", inp["mla_kv_norm_g"][l])
    for d in range(2):
        mu = inp["rwkv_mu"][l, d]
        mup = np.zeros(3 * BW + 512, np.float32)
        mup[:3 * BW + 256] = mu[:3 * BW + 256]
        mup[3 * BW + 256:3 * BW + 256 + 160] = mu[3 * BW + 256:]
        put("mu%d" % d, mup)
        put("w0_%d" % d, inp["rwkv_w0"][l, d])
        put("a0_%d" % d, inp["rwkv_a0"][l, d])
    put("kk", inp["rwkv_k_k"][l])
    put("ka", inp["rwkv_k_a"][l])
    put("rk", inp["rwkv_r_k"][l].reshape(-1))
    for j in range(3):
        put("cw%d" % j, inp["conv_w"][l, j])
    for i in range(4):
        put("gb_%d" % i, inp["gate_b"][l, i])
    return out


def rope_tables(cfg):
    T, C, GW = cfg.T, cfg.C, cfg.GRID_W
    rows = T // GW
    row = np.repeat(np.arange(rows, dtype=np.float32), GW)
    colv = np.tile(np.arange(GW, dtype=np.float32), rows)
    nf = 16
    inv = (10000.0 ** (-np.arange(nf, dtype=np.float32) / nf)).astype(np.float32)
    ang = np.concatenate([row[:, None] * inv, colv[:, None] * inv], axis=-1)
    cos, sin = np.cos(ang).T.astype(np.float32), np.sin(ang).T.astype(np.float32)
    cos64 = np.concatenate([cos, cos], 0)
    sin64 = np.concatenate([-sin, sin], 0)
    c = np.ones((128, cfg.NT), np.float32)
    s = np.zeros((128, cfg.NT), np.float32)
    c[:, C:] = np.concatenate([cos64, cos64], 0)
    s[:, C:] = np.concatenate([sin64, sin64], 0)
    return c, s


def host_consts(cfg):
    G = cfg.BC
    mask16 = np.zeros((16, 512), np.float32)
    esel = np.zeros((16, 128), np.float32)
    for h in range(16):
        g, par = h // 2, h % 2
        if g < G:
            mask16[h, g * 64:(g + 1) * 64] = 1.0
        esel[h, par * 64:(par + 1) * 64] = 1.0
    blk1 = np.zeros((128, 128), np.float32)
    blk1[:64, :64] = 1.0
    blk1[64:, 64:] = 1.0
    oh = np.zeros((TC_SCAN, TC_SCAN, 64), np.float32)
    for t in range(TC_SCAN):
        oh[t, t, :] = 1.0
    c, s = rope_tables(cfg)
    return dict(c_ident=np.eye(128, dtype=np.float32), c_blk1=blk1, c_mask16=mask16, c_esel=esel,
                c_onehot=oh.reshape(TC_SCAN, TC_SCAN * 64), c_cos=c, c_sin=s)


def host_inputs(cfg, inp, ncores, batch_of_core):
    L, BW = cfg.DEPTH, cfg.BW
    consts = host_consts(cfg)
    pvec = np.concatenate([host_pvec(cfg, inp, l) for l in range(L)], 0)
    bc = np.zeros((L * 128, 2 * BW + 128), np.float32)
    for l in range(L):
        bc[l * 128:(l + 1) * 128, 0:BW] = np.tile(inp["rwkv_ln_g"][l][None, :], (128, 1))
        bc[l * 128:(l + 1) * 128, BW:2 * BW] = np.tile(inp["rwkv_ln_b"][l][None, :], (128, 1))
        bc[l * 128:(l + 1) * 128, 2 * BW:] = np.tile(inp["diff_norm_g"][l][None, :], (128, 1))
    dl = np.ascontiguousarray(inp["diff_lambda"].reshape(L, 256))
    fng = pp(inp["final_norm_g"])
    NG = cfg.NG
    sr = [cfg.wrows[g] // ncores for g in range(NG)]
    wsh = [[np.zeros((L * sr[g], WCOLS), np.float32) for g in range(NG)] for _ in range(ncores)]
    for l in range(L):
        flats = host_layer_weights(cfg, inp, l)
        for g in range(NG):
            for c in range(ncores):
                wsh[c][g][l * sr[g]:(l + 1) * sr[g]] = flats[g][c * sr[g]:(c + 1) * sr[g]]
        del flats
    maps = []
    for c in range(ncores):
        b = batch_of_core[c]
        xT = np.ascontiguousarray(np.concatenate([inp["ctx"][b], inp["x"][b]], 0).T)
        cond = np.ascontiguousarray(np.stack([inp["c_ctx"], inp["c"][b]], 1))
        d = dict(xT=xT, cond=cond, pvec=pvec, bcin=bc, dlin=dl, fng=fng)
        for g in range(NG):
            d["wflat%d" % g] = wsh[c][g]
        d.update(consts)
        maps.append(d)
    return maps


from concourse.bass_utils import run_bass_kernel_spmd


def kernel(**inputs):
    inputs = {k: np.asarray(v) for k, v in inputs.items()}
    B, T, D = inputs["x"].shape
    C = inputs["ctx"].shape[1]
    L = inputs["w_in"].shape[0]
    ncores = 8
    cfg = Cfg(D=D, T=T, C=C, DEPTH=L, ncores=ncores)
    nc, P, _ = build_program(cfg)
    batch_of_core = [c % B for c in range(ncores)]
    maps = host_inputs(cfg, inputs, ncores, batch_of_core)
    res = run_bass_kernel_spmd(nc, maps, core_ids=list(range(ncores)))
    first = {}
    for c in range(ncores):
        first.setdefault(batch_of_core[c], c)
    out = np.stack([np.ascontiguousarray(res.results[first[b]]["yT"].T) for b in range(B)], 0)
    return out.astype(np.float32)
```

```python
import numpy as np
from contextlib import ExitStack
import concourse.bass as bass
import concourse.mybir as mybir

F32 = mybir.dt.float32
BF16 = mybir.dt.bfloat16
ALU = mybir.AluOpType
ACT = mybir.ActivationFunctionType
AX = mybir.AxisListType
ENGS = ("tensor", "vector", "scalar", "gpsimd", "sync")


class Res:
    def __init__(self, name, t):
        self.name = name
        self.t = t
        self.w = {}
        self.r = {}
        self.sem = None

    def __getitem__(self, k):
        return self.t[k]


class Prog:
    def __init__(self, nc):
        self.nc = nc
        self.es = ExitStack()
        self.semes = ExitStack()
        self.q = {e: [] for e in ENGS}
        self.cnt = {}
        self.sems = {}
        self.known = {e: {} for e in ENGS}
        for e in ENGS:
            self._mksem("E_" + e)
        self.nres = 0
        self.scopes = []
        self.depth = 0
        self.free_keys = []
        self.scope_keys = []
        self.ninstr = 0

    def _mksem(self, key):
        self.sems[key] = self.semes.enter_context(self.nc.semaphore(key))
        self.cnt[key] = 0
        return key

    def sb(self, name, shape, dt=F32):
        self.nres += 1
        name = "%s_%d" % (name, self.nres)
        t = self.es.enter_context(self.nc.sbuf_tensor(name, list(shape), dt))
        r = Res(name, t)
        r.scoped = self.depth > 0
        return r

    def ps(self, name, shape, dt=F32):
        self.nres += 1
        name = "%s_%d" % (name, self.nres)
        t = self.es.enter_context(self.nc.psum_tensor(name, list(shape), dt))
        r = Res(name, t)
        r.scoped = self.depth > 0
        return r

    def dram(self, name, shape, dt=F32, kind="Internal"):
        t = self.nc.dram_tensor(name, list(shape), dt, kind=kind)
        return Res(name, t)

    def _deps(self, eng, reads, writes, own):
        need = {}
        own_raw = 0
        for r in reads:
            for k, v in r.w.items():
                need[k] = max(need.get(k, 0), v)
                if k == own:
                    own_raw = max(own_raw, v)
        for w in writes:
            for k, v in w.w.items():
                need[k] = max(need.get(k, 0), v)
                if k == own:
                    own_raw = max(own_raw, v)
            for k, v in w.r.items():
                need[k] = max(need.get(k, 0), v)
        kn = self.known[eng]
        out = []
        for k, v in need.items():
            if k == own:
                if eng in ("vector", "scalar", "gpsimd") and own is not None and own.startswith("E_") \
                        and own_raw > kn.get(k, 0):
                    kn[k] = own_raw
                    out.append((k, own_raw))
                continue
            if kn.get(k, 0) >= v:
                continue
            kn[k] = v
            out.append((k, v))
        return out

    def _commit(self, reads, writes, key, val):
        for r in reads:
            r.r[key] = max(r.r.get(key, 0), val)
        for w in writes:
            w.w = {key: val}
            w.r = {}

    def op(self, eng, fn, reads=(), writes=()):
        key = "E_" + eng
        waits = self._deps(eng, reads, writes, key)
        self.cnt[key] += 1
        val = self.cnt[key]
        sems = self.sems
        sem = sems[key]

        e = getattr(self.nc, eng)
        for k, v in waits:
            e.wait_ge(sems[k], v)
        fn(e).then_inc(sem, 1)
        self._commit(reads, writes, key, val)
        self.ninstr += 1

    def dma(self, eng, out_res, out_ap, in_res, in_ap, **kw):
        if out_res.sem is None:
            if getattr(out_res, "scoped", False):
                if self.free_keys:
                    out_res.sem = self.free_keys.pop()
                else:
                    out_res.sem = self._mksem("DS_%d" % len(self.sems))
                self.scope_keys[-1].append(out_res.sem)
            else:
                out_res.sem = self._mksem("D_%d_%s" % (len(self.sems), out_res.name))
        key = out_res.sem
        waits = self._deps(eng, [in_res], [out_res], key)
        self.cnt[key] += 16
        val = self.cnt[key]
        sems = self.sems
        sem = sems[key]

        e = getattr(self.nc, eng)
        for k, v in waits:
            e.wait_ge(sems[k], v)
        e.dma_start(out=out_ap, in_=in_ap, **kw).then_inc(sem, 16)
        in_res.r[key] = max(in_res.r.get(key, 0), val)
        w = dict(out_res.w)
        w[key] = val
        out_res.w = {key: val}
        out_res.r = {}
        self.ninstr += 1

    def collective(self, kind, op, groups, in_res, in_ap, out_res, out_ap):
        if out_res.sem is None:
            out_res.sem = self._mksem("C_%d_%s" % (len(self.sems), out_res.name))
        key = out_res.sem
        waits = self._deps("gpsimd", [in_res], [out_res], key)
        e = self.nc.gpsimd
        for k, v in waits:
            e.wait_ge(self.sems[k], v)
        self.cnt[key] += 1
        val = self.cnt[key]
        e.collective_compute(kind, op, replica_groups=groups, ins=[in_ap], outs=[out_ap]).then_inc(self.sems[key], 1)
        in_res.r[key] = max(in_res.r.get(key, 0), val)
        out_res.w = {key: val}
        out_res.r = {}
        self.ninstr += 1

    def wait_all(self, eng, resources):
        waits = self._deps(eng, resources, [], None)
        sems = self.sems

        e = getattr(self.nc, eng)
        for k, v in waits:
            e.wait_ge(sems[k], v)

    def barrier(self):
        snap = dict(self.cnt)
        sems = self.sems
        for eng in ENGS:
            kn = self.known[eng]
            waits = [(k, v) for k, v in snap.items() if v > 0 and k != "E_" + eng and kn.get(k, 0) < v]
            for k, v in waits:
                kn[k] = v

            e = getattr(self.nc, eng)
            for k, v in waits:
                e.wait_ge(sems[k], v)

    def scope(self):
        prog = self

        class _S:
            def __enter__(s):
                prog.barrier()
                s.old = prog.es
                prog.es = ExitStack()
                prog.depth += 1
                prog.scope_keys.append([])
                return s

            def __exit__(s, *a):
                prog.barrier()
                prog.es.close()
                prog.es = s.old
                prog.depth -= 1
                prog.free_keys.extend(prog.scope_keys.pop())
                return False
        return _S()

    def finish(self, outs):
        for eng in ("sync", "gpsimd", "vector", "scalar", "tensor"):
            self.wait_all(eng, outs)
        self.barrier()
        self.es.close()


class Pool:
    def __init__(self, items):
        self.items = items
        self.i = 0

    def get(self):
        r = self.items[self.i % len(self.items)]
        self.i += 1
        return r

import math
import numpy as np
import concourse.bass as bass


NORM_EPS = 1e-6
GN_EPS = 64e-5
WCOLS = 1024
GCH = 512


def cdiv(a, b):
    return (a + b - 1) // b


def pad128(n):
    return cdiv(n, 128) * 128


class Cfg:
    def __init__(self, D=4096, T=2048, C=256, DEPTH=4, GRID_W=64, ncores=8, split=1):
        self.D, self.T, self.C, self.DEPTH, self.GRID_W, self.ncores = D, T, C, DEPTH, GRID_W, ncores
        self.NT = C + T
        self.split = split
        self.BWG = D // 4
        self.BW = self.BWG // split
        self.H_MLA = self.BW // 128
        self.H_RW = self.BW // 64
        self.H_DF = self.BW // 128
        self.QL = 3 * D // 16
        self.KVL = D // 16
        self.DFF = 4 * D
        self.DFFL = self.DFF // split
        self.GR = 256
        self.MR = 256
        self.DC = D // 128
        self.BC = self.BW // 128
        BW = self.BW
        segs = [("cq", pad128(self.QL)), ("ckv", pad128(self.KVL)), ("kr", 128),
                ("r", BW), ("k", BW), ("v", BW), ("wd", 128), ("ad", 128), ("gd", 256),
                ("cb", BW), ("cc", BW), ("cu", BW), ("dq", BW), ("dk", BW), ("dv", BW),
                ("dqr", BW), ("dkr", BW)]
        self.seg = {}
        o = 0
        for n, s in segs:
            self.seg[n] = (o, s)
            o += s
        self.PCOLS = o
        self.tiles = []
        for a, b in ((0, C), (C, C + T)):
            n = a
            while n < b:
                s = min(512, b - n)
                self.tiles.append((n, s))
                n += s
        self.segs_tok = ((0, C), (C, C + T))
        QLp, KVLp = pad128(self.QL), pad128(self.KVL)
        self.wshapes = [
            ("mod_down", D, self.MR), ("mod_up", self.MR, 6 * D), ("w_in", D, self.PCOLS),
            ("w_uq", QLp, self.H_MLA * 256), ("w_ukv", KVLp, self.H_MLA * 256),
            ("w2_0", 64, BW), ("w2_1", 64, BW), ("a2_0", 64, BW), ("a2_1", 64, BW), ("g2", 256, BW),
            ("wb_0", self.BWG, D), ("wb_1", self.BWG, D), ("wb_2", self.BWG, D), ("wb_3", self.BWG, D),
            ("gate_down", D, self.GR), ("gu_0", self.GR, D), ("gu_1", self.GR, D), ("gu_2", self.GR, D),
            ("gu_3", self.GR, D), ("w_out", D, D), ("mlp_w1", D, self.DFFL), ("mlp_w2", self.DFFL, D),
        ]
        self.wgroup_of = {}
        for n, K, M in self.wshapes:
            self.wgroup_of[n] = 1 if n == "mlp_w1" else (2 if n == "mlp_w2" else 0)
        self.NG = 3
        self.woff = {}
        o = [0] * self.NG
        for n, K, M in self.wshapes:
            g = self.wgroup_of[n]
            self.woff[n] = (o[g], K, M)
            o[g] += pad128(K) * pad128(M)
        self.wrows = [cdiv(cdiv(o[g], WCOLS), 8 * GCH) * 8 * GCH for g in range(self.NG)]

    def cond_of(self, n0):
        return 0 if n0 < self.C else 1


class Model:
    def __init__(self, P, cfg):
        self.P = P
        self.cfg = cfg
        self.nc = P.nc
        self.rr = 0

    def wview(self, wfull, name, mc):
        off, K, M = self.cfg.woff[name]
        kw = pad128(K)
        o = off + mc * 128 * kw
        return bass.AP(wfull[self.cfg.wgroup_of[name]].t, o, [[kw, 128], [1, kw]])

    def evac(self, eng, out_ap, in_ap, reads, writes, func=None, bias=None, scale=1.0):
        P = self.P
        if eng == "scalar":
            f = func if func is not None else ACT.Copy
            if bias is None:
                P.op("scalar", lambda e: e.activation(out_ap, in_ap, f, scale=scale), reads=reads, writes=writes)
            else:
                P.op("scalar", lambda e: e.activation(out_ap, in_ap, f, bias=bias, scale=scale),
                     reads=reads, writes=writes)
        else:
            P.op("vector", lambda e: e.tensor_copy(out_ap, in_ap), reads=reads, writes=writes)

    def alt(self):
        self.rr += 1
        return "scalar" if self.rr % 2 else "vector"

    def alloc_linear_bufs(self):
        P = self.P
        self.xpool = Pool([P.sb("xb%d" % i, [128, 16384], BF16) for i in range(2)])
        wmax = max(pad128(K) for _, K, _ in self.cfg.wshapes)
        nwb = max(2, 32768 // wmax)
        self.wpool = Pool([P.sb("wb%d" % i, [128, wmax], BF16) for i in range(nwb)])
        self.pspool = Pool([P.ps("ps%d" % i, [128, 512], F32) for i in range(4)])
        self.opool = Pool([P.sb("ob%d" % i, [128, 512], F32) for i in range(4)])
        self.opoolb = Pool([P.sb("obb%d" % i, [128, 512], BF16) for i in range(4)])
        self.accpool = Pool([P.sb("acc%d" % i, [128, 512], F32) for i in range(2)])

    def linear(self, xT, K, wfull, wname, M, epi, tiles=None, nmax=512, row0=0):
        P = self.P
        KC = cdiv(K, 128)
        MC = cdiv(M, 128)
        if tiles is None:
            tiles = self.cfg.tiles
        sub = []
        for n0, ns in tiles:
            o = 0
            while o < ns:
                s = min(nmax, ns - o)
                sub.append((n0 + o, s))
                o += s
        for n0, nsz in sub:
            xt = self.xpool.get()
            flat = xt.t
            kfull = K // 128
            if kfull:
                P.dma("sync", xt, flat[:, 0:kfull * nsz].rearrange("p (kc n) -> p kc n", n=nsz),
                      xT, xT.t[row0:row0 + kfull * 128, n0:n0 + nsz].rearrange("(kc p) n -> p kc n", p=128))
            krem = K - kfull * 128
            if krem:
                P.dma("sync", xt, flat[0:krem, kfull * nsz:(kfull + 1) * nsz],
                      xT, xT.t[row0 + kfull * 128:row0 + K, n0:n0 + nsz])
            for mc in range(MC):
                m0 = mc * 128
                msz = min(128, M - m0)
                wt = self.wpool.get()
                P.dma("sync", wt, wt.t[:, 0:KC * 128], wfull[self.cfg.wgroup_of[wname]], self.wview(wfull, wname, mc)[:, 0:KC * 128])
                ps = self.pspool.get()
                for kc in range(KC):
                    ksz = min(128, K - kc * 128)
                    P.op("tensor", lambda e, kc=kc, ksz=ksz: e.matmul(
                        ps.t[0:msz, 0:nsz], wt.t[0:ksz, kc * 128:kc * 128 + msz],
                        flat[0:ksz, kc * nsz:(kc + 1) * nsz], start=(kc == 0), stop=(kc == KC - 1)),
                        reads=[wt, xt], writes=[ps])
                epi(ps, m0, msz, n0, nsz)

    def epi_store(self, dst32=None, dst16=None, func=None, bias_fn=None, row0=0, square=False, mul=None):
        P = self.P

        def epi(ps, m0, msz, n0, nsz):
            eng = "scalar" if (func is not None or bias_fn is not None) else self.alt()
            bias = bias_fn(m0, msz, n0) if bias_fn is not None else None
            if dst32 is not None:
                o = self.opool.get()
                self.evac(eng, o.t[0:msz, 0:nsz], ps.t[0:msz, 0:nsz], [ps], [o], func=func, bias=bias)
                if square:
                    P.op("vector", lambda e: e.tensor_tensor(o.t[0:msz, 0:nsz], o.t[0:msz, 0:nsz],
                                                             o.t[0:msz, 0:nsz], ALU.mult), reads=[o], writes=[o])
                if mul is not None:
                    P.op("vector", lambda e: e.tensor_scalar(o.t[0:msz, 0:nsz], o.t[0:msz, 0:nsz], mul, None, ALU.mult),
                         reads=[o], writes=[o])
                P.dma("gpsimd", dst32, dst32.t[row0 + m0:row0 + m0 + msz, n0:n0 + nsz], o, o.t[0:msz, 0:nsz])
                if dst16 is not None:
                    ob = self.opoolb.get()
                    P.op("vector", lambda e: e.tensor_copy(ob.t[0:msz, 0:nsz], o.t[0:msz, 0:nsz]),
                         reads=[o], writes=[ob])
                    P.dma("gpsimd", dst16, dst16.t[row0 + m0:row0 + m0 + msz, n0:n0 + nsz], ob, ob.t[0:msz, 0:nsz])
            else:
                ob = self.opoolb.get()
                if square:
                    o = self.opool.get()
                    self.evac(eng, o.t[0:msz, 0:nsz], ps.t[0:msz, 0:nsz], [ps], [o], func=func, bias=bias)
                    P.op("vector", lambda e: e.tensor_tensor(ob.t[0:msz, 0:nsz], o.t[0:msz, 0:nsz],
                                                             o.t[0:msz, 0:nsz], ALU.mult), reads=[o], writes=[ob])
                else:
                    self.evac(eng, ob.t[0:msz, 0:nsz], ps.t[0:msz, 0:nsz], [ps], [ob], func=func, bias=bias)
                P.dma("gpsimd", dst16, dst16.t[row0 + m0:row0 + m0 + msz, n0:n0 + nsz], ob, ob.t[0:msz, 0:nsz])
        return epi

    def rmsnorm(self, src, srow0, F, dst16, drow0, A_fn, B_fn=None, deps=(), ones=None, tiles=None, dcol=0):
        P = self.P
        cfg = self.cfg
        FC = cdiv(F, 128)
        nmax = max(128, min(512, (8192 // FC) // 128 * 128))
        sub = []
        for a, ns in (tiles if tiles is not None else cfg.tiles):
            o = 0
            while o < ns:
                s_ = min(nmax, ns - o)
                sub.append((a + o, s_))
                o += s_
        for n0, nsz in sub:
            ci = cfg.cond_of(n0)
            xt = self.nx.get()
            v = xt.t[:, 0:FC * nsz].rearrange("p (c n) -> p c n", n=nsz)
            P.dma("sync", xt, v, src, src.t[srow0:srow0 + FC * 128, n0:n0 + nsz].rearrange("(c p) n -> p c n", p=128))
            sq = self.nsq.get()
            sv = sq.t[:, 0:FC * nsz].rearrange("p (c n) -> p c n", n=nsz)
            P.op("scalar", lambda e: e.activation(sv, v, ACT.Square), reads=[xt], writes=[sq])
            ps = self.pspool.get()
            for c in range(FC):
                P.op("tensor", lambda e, c=c: e.matmul(ps.t[:, 0:nsz], ones.t[:, :], sq.t[:, c * nsz:(c + 1) * nsz],
                                                       start=(c == 0), stop=(c == FC - 1)), reads=[ones, sq], writes=[ps])
            rs = self.nrs.get()
            P.op("scalar", lambda e: e.activation(rs.t[:, 0:nsz], ps.t[:, 0:nsz], ACT.Sqrt, bias=self.eps_t.t[:, 0:1],
                                                  scale=1.0 / F), reads=[ps, self.eps_t], writes=[rs])
            P.op("vector", lambda e: e.reciprocal(rs.t[:, 0:nsz], rs.t[:, 0:nsz]), reads=[rs], writes=[rs])
            ot = self.no.get()
            ov = ot.t[:, 0:FC * nsz].rearrange("p (c n) -> p c n", n=nsz)
            P.op("vector", lambda e: e.tensor_tensor(v, v, rs.t[:, 0:nsz].unsqueeze(1).broadcast_to([128, FC, nsz]),
                                                     ALU.mult), reads=[xt, rs], writes=[xt])
            for c in range(FC):
                a_ap = A_fn(c, ci)
                if B_fn is not None:
                    b_ap = B_fn(c, ci)
                    P.op("scalar", lambda e, c=c, a_ap=a_ap, b_ap=b_ap: e.activation(
                        ov[:, c, :], v[:, c, :], ACT.Identity, bias=b_ap, scale=a_ap),
                        reads=[xt] + list(deps), writes=[ot])
                else:
                    P.op("vector", lambda e, c=c, a_ap=a_ap: e.tensor_scalar(
                        ov[:, c, :], v[:, c, :], a_ap, None, ALU.mult), reads=[xt] + list(deps), writes=[ot])
            P.dma("gpsimd", dst16, dst16.t[drow0:drow0 + FC * 128, n0 - dcol:n0 - dcol + nsz].rearrange("(c p) n -> p c n", p=128),
                  ot, ov)

    def alloc_norm_bufs(self, out_dt=BF16):
        P = self.P
        self.nx = Pool([P.sb("nx%d" % i, [128, 8192], F32) for i in range(2 if out_dt == BF16 else 1)])
        self.nsq = Pool([P.sb("nsq%d" % i, [128, 8192], F32) for i in range(1)])
        self.nrs = Pool([P.sb("nrs%d" % i, [128, 512], F32) for i in range(2)])
        self.no = Pool([P.sb("no%d" % i, [128, 8192], out_dt) for i in range(2 if out_dt == BF16 else 1)])

    def ld(self, pool, src, r0, nrows, n0, nsz, q="sync"):
        t = pool.get()
        self.P.dma(q, t, t.t[0:nrows, 0:nsz], src, src.t[r0:r0 + nrows, n0:n0 + nsz])
        return t

    def st(self, dst, r0, nrows, n0, nsz, t, q="gpsimd"):
        self.P.dma(q, dst, dst.t[r0:r0 + nrows, n0:n0 + nsz], t, t.t[0:nrows, 0:nsz])

    def rope(self, src, srow, rrow, nrows_total, dst16, drow):
        P = self.P
        cfg = self.cfg
        done = 0
        while done < nrows_total:
            nr = min(128, nrows_total - done)
            for n0, nsz in cfg.tiles:
                a = self.ld(self.epool, src, srow + done, nr, n0, nsz)
                b = self.ld(self.epool, src, rrow + done, nr, n0, nsz)
                P.op("vector", lambda e: e.tensor_tensor(a.t[0:nr, 0:nsz], a.t[0:nr, 0:nsz],
                                                         self.cos2.t[0:nr, n0:n0 + nsz], ALU.mult),
                     reads=[a, self.cos2], writes=[a])
                P.op("gpsimd", lambda e: e.tensor_tensor(b.t[0:nr, 0:nsz], b.t[0:nr, 0:nsz],
                                                         self.sin2.t[0:nr, n0:n0 + nsz], ALU.mult),
                     reads=[b, self.sin2], writes=[b])
                o = self.epoolb.get()
                P.op("vector", lambda e: e.tensor_tensor(o.t[0:nr, 0:nsz], a.t[0:nr, 0:nsz], b.t[0:nr, 0:nsz], ALU.add),
                     reads=[a, b], writes=[o])
                self.st(dst16, drow + done, nr, n0, nsz, o)
            done += nr

    def cast_rows(self, src, srow, nrows_total, dst16, drow):
        P = self.P
        done = 0
        while done < nrows_total:
            nr = min(128, nrows_total - done)
            for n0, nsz in self.cfg.tiles:
                a = self.ld(self.epool, src, srow + done, nr, n0, nsz)
                o = self.epoolb.get()
                P.op("vector", lambda e: e.tensor_copy(o.t[0:nr, 0:nsz], a.t[0:nr, 0:nsz]), reads=[a], writes=[o])
                self.st(dst16, drow + done, nr, n0, nsz, o)
            done += nr

    def alloc_ew_bufs(self):
        P = self.P
        self.epool = Pool([P.sb("ew%d" % i, [128, 512], F32) for i in range(6)])
        self.epoolb = Pool([P.sb("ewb%d" % i, [128, 512], BF16) for i in range(3)])

    def alloc_attn_bufs(self):
        P = self.P
        NT = self.cfg.NT
        NK = NT // 128
        self.a_q = [P.sb("a_q%d" % i, [128, NT], BF16) for i in range(2)]
        self.a_k = [P.sb("a_k%d" % i, [128, NT], BF16) for i in range(2)]
        self.a_vf = P.sb("a_vf", [128, NT], BF16)
        self.a_vt = P.sb("a_vt", [128, NK, 128], BF16)
        self.a_pb = P.sb("a_pb", [128, NT], BF16)
        self.a_pt = P.sb("a_pt", [128, NK, 128], BF16)
        self.a_y = P.sb("a_y", [128, NT], BF16)
        self.a_sc = P.ps("a_sc", [128, 512 * cdiv(NT, 512)], F32)
        self.a_pT = P.ps("a_pT", [128, 512], F32)
        self.a_pO = [P.ps("a_pO%d" % i, [128, 512], F32) for i in range(2)]
        self.a_small = Pool([P.sb("a_sm%d" % i, [128, 8], F32) for i in range(4)])
        self.a_o = Pool([P.sb("a_o%d" % i, [128, 128], F32) for i in range(3)])
        self.a_ob = Pool([P.sb("a_ob%d" % i, [128, 128], BF16) for i in range(2)])

    def v_to_tokmajor(self):
        P = self.P
        NK = self.cfg.NT // 128
        for c0 in range(0, NK, 4):
            nb = min(4, NK - c0)
            for j in range(nb):
                c = c0 + j
                P.op("tensor", lambda e, c=c, j=j: e.matmul(self.a_pT.t[:, j * 128:(j + 1) * 128],
                                                            self.a_vf.t[:, c * 128:(c + 1) * 128], self.identb.t[:, :],
                                                            start=True, stop=True),
                     reads=[self.a_vf, self.identb], writes=[self.a_pT])
            self.evac(self.alt(), self.a_vt.t[:, c0:c0 + nb, :].rearrange("p c d -> p (c d)"),
                      self.a_pT.t[:, 0:nb * 128], [self.a_pT], [self.a_vt])

    def softmax_pv(self, qparts, kparts, q0, nk, scale, pO):
        P = self.P
        sc = self.a_sc
        npart = len(qparts)
        for k0 in range(0, nk, 512):
            ks = min(512, nk - k0)
            for i, ((qr, qa, qb), (kr, ka, kb)) in enumerate(zip(qparts, kparts)):
                P.op("tensor", lambda e, qr=qr, qa=qa, qb=qb, kr=kr, ka=ka, kb=kb, i=i: e.matmul(
                    sc.t[:, k0:k0 + ks], qr.t[qa:qb, q0:q0 + 128], kr.t[ka:kb, k0:k0 + ks],
                    start=(i == 0), stop=(i == npart - 1)), reads=[qr, kr], writes=[sc])
        sm = self.a_small.get()
        P.op("vector", lambda e: e.reduce_max(sm.t[:, 0:1], sc.t[:, 0:nk], AX.X), reads=[sc], writes=[sm])
        P.op("vector", lambda e: e.tensor_scalar(sm.t[:, 1:2], sm.t[:, 0:1], -scale, None, ALU.mult), reads=[sm], writes=[sm])
        P.op("scalar", lambda e: e.activation(self.a_pb.t[:, 0:nk], sc.t[:, 0:nk], ACT.Exp, bias=sm.t[:, 1:2],
                                              scale=scale, accum_out=sm.t[:, 2:3]), reads=[sc, sm], writes=[self.a_pb, sm])
        P.op("vector", lambda e: e.reciprocal(sm.t[:, 3:4], sm.t[:, 2:3]), reads=[sm], writes=[sm])
        nc_ = nk // 128
        for c0 in range(0, nc_, 4):
            nb = min(4, nc_ - c0)
            for j in range(nb):
                c = c0 + j
                P.op("tensor", lambda e, c=c, j=j: e.matmul(self.a_pT.t[:, j * 128:(j + 1) * 128],
                                                            self.a_pb.t[:, c * 128:(c + 1) * 128], self.identb.t[:, :],
                                                            start=True, stop=True),
                     reads=[self.a_pb, self.identb], writes=[self.a_pT])
            self.evac(self.alt(), self.a_pt.t[:, c0:c0 + nb, :].rearrange("p c d -> p (c d)"),
                      self.a_pT.t[:, 0:nb * 128], [self.a_pT], [self.a_pt])
        for c in range(nc_):
            P.op("tensor", lambda e, c=c: e.matmul(pO.t[:, 0:128], self.a_pt.t[:, c, :], self.a_vt.t[:, c, :],
                                                   start=(c == 0), stop=(c == nc_ - 1)),
                 reads=[self.a_pt, self.a_vt], writes=[pO])
        return sm

    def out_transpose(self, ob, q0):
        P = self.P
        P.op("tensor", lambda e: e.matmul(self.a_pT.t[:, 0:128], ob.t[:, :], self.identb.t[:, :], start=True, stop=True),
             reads=[ob, self.identb], writes=[self.a_pT])
        self.evac(self.alt(), self.a_y.t[:, q0:q0 + 128], self.a_pT.t[:, 0:128], [self.a_pT], [self.a_y])

    def load_rows(self, t, src, r0, nr, p0=0):
        NT = self.cfg.NT
        self.P.dma("sync", t, t.t[p0:p0 + nr, 0:NT], src, src.t[r0:r0 + nr, 0:NT])

    def mla_attention(self, QN, QR, KN, KR, VM, Y, yrow0):
        P = self.P
        cfg = self.cfg
        scale = (128 + 64) ** -0.5
        self.load_rows(self.a_k[1], KR, 0, 64)
        for h in range(cfg.H_MLA):
            self.load_rows(self.a_q[0], QN, h * 128, 128)
            self.load_rows(self.a_q[1], QR, h * 64, 64)
            self.load_rows(self.a_k[0], KN, h * 128, 128)
            self.load_rows(self.a_vf, VM, h * 128, 128)
            self.v_to_tokmajor()
            for q0 in range(0, cfg.NT, 128):
                nk = cfg.C if q0 < cfg.C else cfg.NT
                sm = self.softmax_pv([(self.a_q[0], 0, 128), (self.a_q[1], 0, 64)],
                                     [(self.a_k[0], 0, 128), (self.a_k[1], 0, 64)], q0, nk, scale, self.a_pO[0])
                ob = self.a_ob.get()
                P.op("scalar", lambda e: e.activation(ob.t[:, :], self.a_pO[0].t[:, 0:128], ACT.Copy, scale=sm.t[:, 3:4]),
                     reads=[self.a_pO[0], sm], writes=[ob])
                self.out_transpose(ob, q0)
            P.dma("gpsimd", Y, Y.t[yrow0 + h * 128:yrow0 + (h + 1) * 128, 0:cfg.NT], self.a_y, self.a_y.t[:, 0:cfg.NT])

    def diff_attention(self, DQ, DK, DV, Y, yrow0, lam_col, gtile, lam_init):
        P = self.P
        cfg = self.cfg
        scale = 64 ** -0.5
        for h in range(cfg.H_DF):
            self.load_rows(self.a_q[0], DQ, h * 128, 128)
            self.load_rows(self.a_k[0], DK, h * 128, 128)
            self.load_rows(self.a_vf, DV, h * 128, 128)
            self.v_to_tokmajor()
            for q0 in range(0, cfg.NT, 128):
                nk = cfg.C if q0 < cfg.C else cfg.NT
                sm1 = self.softmax_pv([(self.a_q[0], 0, 64)], [(self.a_k[0], 0, 64)], q0, nk, scale, self.a_pO[0])
                sm2 = self.softmax_pv([(self.a_q[0], 64, 128)], [(self.a_k[0], 64, 128)], q0, nk, scale, self.a_pO[1])
                o1 = self.a_o.get()
                P.op("scalar", lambda e: e.activation(o1.t[:, :], self.a_pO[0].t[:, 0:128], ACT.Copy, scale=sm1.t[:, 3:4]),
                     reads=[self.a_pO[0], sm1], writes=[o1])
                P.op("vector", lambda e: e.tensor_tensor(sm2.t[:, 4:5], sm2.t[:, 3:4], lam_col.t[:, 0:1], ALU.mult),
                     reads=[sm2, lam_col], writes=[sm2])
                o = self.a_o.get()
                P.op("vector", lambda e: e.scalar_tensor_tensor(o.t[:, :], self.a_pO[1].t[:, 0:128], sm2.t[:, 4:5],
                                                                o1.t[:, :], ALU.mult, ALU.add),
                     reads=[self.a_pO[1], sm2, o1], writes=[o])
                sq = self.a_o.get()
                P.op("vector", lambda e: e.tensor_tensor(sq.t[:, :], o.t[:, :], o.t[:, :], ALU.mult), reads=[o], writes=[sq])
                P.op("vector", lambda e: e.reduce_sum(sm2.t[:, 5:6], sq.t[:, :], AX.X), reads=[sq], writes=[sm2])
                P.op("scalar", lambda e: e.activation(sm2.t[:, 6:7], sm2.t[:, 5:6], ACT.Sqrt, bias=self.eps_t.t[:, 0:1],
                                                      scale=1.0 / 128), reads=[sm2, self.eps_t], writes=[sm2])
                P.op("vector", lambda e: e.reciprocal(sm2.t[:, 6:7], sm2.t[:, 6:7]), reads=[sm2], writes=[sm2])
                ob = self.a_ob.get()
                if 0 == 1:
                    P.op("vector", lambda e: e.tensor_copy(ob.t[:, :], o.t[:, :]), reads=[o], writes=[ob])
                elif 0 == 3:
                    P.op("vector", lambda e: e.tensor_scalar(ob.t[:, :], o.t[:, :], sm2.t[:, 6:7], None, ALU.mult), reads=[o, sm2], writes=[ob])
                elif 0 == 4:
                    P.op("vector", lambda e: e.tensor_tensor(ob.t[:, :], o.t[:, :], gtile.t[:, :], ALU.mult), reads=[o, gtile], writes=[ob])
                elif 0 == 5:
                    P.op("scalar", lambda e: e.activation(ob.t[:, :], self.a_pO[1].t[:, 0:128], ACT.Copy, scale=sm2.t[:, 3:4]),
                         reads=[self.a_pO[1], sm2], writes=[ob])
                elif 0 == 6:
                    P.op("scalar", lambda e: e.activation(ob.t[:, :], self.a_pO[1].t[:, 0:128], ACT.Copy, scale=sm2.t[:, 4:5]),
                         reads=[self.a_pO[1], sm2], writes=[ob])
                elif 0 == 7:
                    P.op("vector", lambda e: e.tensor_copy(ob.t[:, 0:2], lam_col.t[:, 0:2]), reads=[lam_col], writes=[ob])
                    P.op("vector", lambda e: e.tensor_copy(ob.t[:, 2:8], sm2.t[:, 2:8]), reads=[sm2], writes=[ob])
                elif 0 == 2:
                    P.op("vector", lambda e: e.tensor_copy(ob.t[:, :], o1.t[:, :]), reads=[o1], writes=[ob])
                else:
                    P.op("vector", lambda e: e.scalar_tensor_tensor(ob.t[:, :], o.t[:, :], sm2.t[:, 6:7], gtile.t[:, :],
                                                                    ALU.mult, ALU.mult), reads=[o, sm2, gtile], writes=[ob])
                self.out_transpose(ob, q0)
            P.dma("gpsimd", Y, Y.t[yrow0 + h * 128:yrow0 + (h + 1) * 128, 0:cfg.NT], self.a_y, self.a_y.t[:, 0:cfg.NT])

    def alloc_pad_bufs(self, nload, nwork):
        P = self.P
        NP = self.cfg.NT + 4
        self.NP = NP
        self.lpool = Pool([P.sb("pl%d" % i, [128, NP], F32) for i in range(nload)])
        for t in self.lpool.items:
            P.op("gpsimd", lambda e, t=t: e.memset(t.t[:, :], 0.0), writes=[t])
        self.wk = Pool([P.sb("pw%d" % i, [128, NP], F32) for i in range(nwork)])
        self.wkb = Pool([P.sb("pwb%d" % i, [128, NP], BF16) for i in range(2)])

    def ldpad(self, src, row0, nr=128):
        cfg = self.cfg
        t = self.lpool.get()
        C, T = cfg.C, cfg.T
        self.P.dma("sync", t, t.t[0:nr, 1:1 + C], src, src.t[row0:row0 + nr, 0:C])
        self.P.dma("sync", t, t.t[0:nr, C + 3:C + 3 + T], src, src.t[row0:row0 + nr, C:C + T])
        return t

    def stpad(self, dst, row0, t, nr=128):
        cfg = self.cfg
        C, T = cfg.C, cfg.T
        self.P.dma("gpsimd", dst, dst.t[row0:row0 + nr, 0:C], t, t.t[0:nr, 1:1 + C])
        self.P.dma("gpsimd", dst, dst.t[row0:row0 + nr, C:C + T], t, t.t[0:nr, C + 3:C + 3 + T])

    def tshift(self, x, mu0, mu1, o):
        P = self.P
        NP = self.NP
        cur, prev, nxt = x.t[:, 1:NP - 1], x.t[:, 0:NP - 2], x.t[:, 2:NP]
        d1 = self.wk.get()
        P.op("vector", lambda e: e.tensor_tensor(d1.t[:, 1:NP - 1], prev, cur, ALU.subtract), reads=[x], writes=[d1])
        d2 = self.wk.get()
        P.op("gpsimd", lambda e: e.tensor_tensor(d2.t[:, 1:NP - 1], nxt, cur, ALU.subtract), reads=[x], writes=[d2])
        P.op("vector", lambda e: e.scalar_tensor_tensor(d1.t[:, 1:NP - 1], d1.t[:, 1:NP - 1], mu0, cur, ALU.mult, ALU.add),
             reads=[d1, x, self.PVt], writes=[d1])
        P.op("vector", lambda e: e.scalar_tensor_tensor(o.t[:, 1:NP - 1], d2.t[:, 1:NP - 1], mu1, d1.t[:, 1:NP - 1],
                                                        ALU.mult, ALU.add), reads=[d1, d2, self.PVt], writes=[o])
        return o

    def blocksum(self, src, dst_fn):
        P = self.P
        NP = self.NP
        for c0 in range(1, NP - 1, 512):
            cs = min(512, NP - 1 - c0)
            ps = self.pspool.get()
            P.op("tensor", lambda e: e.matmul(ps.t[:, 0:cs], self.blk1.t[:, :], src.t[:, c0:c0 + cs], start=True, stop=True),
                 reads=[self.blk1, src], writes=[ps])
            dst_fn(ps, c0, cs)

    def rwkv_prep_a(self, P32, D_):
        P = self.P
        cfg = self.cfg
        NP = self.NP
        BC = cfg.BC
        pv = self.pv
        col = lambda name, c: self.PVt.t[:, pv[name][0] + c:pv[name][0] + c + 1]
        segs = [("wd", 0, ACT.Tanh), ("ad", 0, None), ("gd", 0, ACT.Sigmoid), ("gd", 128, ACT.Sigmoid)]
        for i, (sn, off, fn) in enumerate(segs):
            x = self.ldpad(P32, cfg.seg[sn][0] + off)
            s = self.tshift(x, col("mu0", 3 * BC + i), col("mu1", 3 * BC + i), self.wk.get())
            ob = self.wkb.get()
            if fn is None:
                P.op("vector", lambda e: e.tensor_copy(ob.t[:, 1:NP - 1], s.t[:, 1:NP - 1]), reads=[s], writes=[ob])
            else:
                P.op("scalar", lambda e, fn=fn: e.activation(ob.t[:, 1:NP - 1], s.t[:, 1:NP - 1], fn), reads=[s], writes=[ob])
            self.stpad(D_["LORA16"], i * 128, ob)

    def rwkv_prep_b(self, wfull, D_):
        cfg = self.cfg
        pv = self.pv
        col = lambda name, c: self.PVt.t[:, pv[name][0] + c:pv[name][0] + c + 1]
        for d in range(2):
            self.linear(D_["LORA16"], 64, wfull, "w2_%d" % d, cfg.BW,
                        self.epi_store(dst32=D_["LW%d" % d], func=ACT.Sigmoid,
                                       bias_fn=lambda m0, msz, n0, d=d: col("w0_%d" % d, m0 // 128)[0:msz], mul=-0.6065306597126334),
                        row0=d * 64)
            self.linear(D_["LORA16"], 64, wfull, "a2_%d" % d, cfg.BW,
                        self.epi_store(dst32=D_["A%d" % d], func=ACT.Sigmoid,
                                       bias_fn=lambda m0, msz, n0, d=d: col("a0_%d" % d, m0 // 128)[0:msz]),
                        row0=128 + d * 64)
        self.linear(D_["LORA16"], 256, wfull, "g2", cfg.BW, self.epi_store(dst32=D_["G"]), row0=256)

    def rwkv_prep_c(self, P32, D_):
        P = self.P
        cfg = self.cfg
        NP = self.NP
        BC = cfg.BC
        pv = self.pv
        PVt = self.PVt
        col = lambda name, c: PVt.t[:, pv[name][0] + c:pv[name][0] + c + 1]
        L = {n: P.sb("rp_" + n, [128, NP], F32) for n in ("r", "k", "v", "kk", "kd0", "kd1")}
        for c in range(BC):
            xr = self.ldpad(P32, cfg.seg["r"][0] + c * 128)
            r_s = self.tshift(xr, col("mu0", c), col("mu1", c), L["r"])
            self.stpad(D_["R"], c * 128, r_s)
            xk = self.ldpad(P32, cfg.seg["k"][0] + c * 128)
            k_s = self.tshift(xk, col("mu0", BC + c), col("mu1", BC + c), L["k"])
            xv = self.ldpad(P32, cfg.seg["v"][0] + c * 128)
            v_s = self.tshift(xv, col("mu0", 2 * BC + c), col("mu1", 2 * BC + c), L["v"])
            self.stpad(D_["V"], c * 128, v_s)
            kkr = self.wk.get()
            P.op("vector", lambda e: e.tensor_scalar(kkr.t[:, 1:NP - 1], k_s.t[:, 1:NP - 1], col("kk", c), None, ALU.mult),
                 reads=[k_s, PVt], writes=[kkr])
            sq = self.wk.get()
            P.op("gpsimd", lambda e: e.tensor_tensor(sq.t[:, 1:NP - 1], kkr.t[:, 1:NP - 1], kkr.t[:, 1:NP - 1], ALU.mult),
                 reads=[kkr], writes=[sq])
            rn = self.wk.get()

            def f1(ps, c0, cs):
                P.op("scalar", lambda e: e.activation(rn.t[:, c0:c0 + cs], ps.t[:, 0:cs], ACT.Sqrt, bias=self.eps12.t[:, 0:1]),
                     reads=[ps, self.eps12], writes=[rn])
            self.blocksum(sq, f1)
            P.op("vector", lambda e: e.reciprocal(rn.t[:, 1:NP - 1], rn.t[:, 1:NP - 1]), reads=[rn], writes=[rn])
            kk = L["kk"]
            P.op("vector", lambda e: e.tensor_tensor(kk.t[:, 1:NP - 1], kkr.t[:, 1:NP - 1], rn.t[:, 1:NP - 1], ALU.mult),
                 reads=[kkr, rn], writes=[kk])
            self.stpad(D_["KK"], c * 128, kk)
            kds = []
            for d in range(2):
                a = self.ldpad(D_["A%d" % d], c * 128)
                t = self.wk.get()
                P.op("vector", lambda e: e.tensor_scalar(t.t[:, 1:NP - 1], a.t[:, 1:NP - 1], col("ka", c), col("c1", c), ALU.mult, ALU.add),
                     reads=[a, PVt], writes=[t])
                kd = L["kd%d" % d]
                P.op("vector", lambda e: e.tensor_tensor(kd.t[:, 1:NP - 1], k_s.t[:, 1:NP - 1], t.t[:, 1:NP - 1], ALU.mult),
                     reads=[k_s, t], writes=[kd])
                self.stpad(D_["KD%d" % d], c * 128, kd)
                kds.append(kd)
                nb = self.wk.get()
                P.op("vector", lambda e: e.scalar_tensor_tensor(nb.t[:, 1:NP - 1], kk.t[:, 1:NP - 1], -1.0, a.t[:, 1:NP - 1],
                                                                ALU.mult, ALU.mult), reads=[kk, a], writes=[nb])
                self.stpad(D_["NB%d" % d], c * 128, nb)
                lw = self.ldpad(D_["LW%d" % d], c * 128)
                w = self.wk.get()
                P.op("scalar", lambda e: e.activation(w.t[:, 1:NP - 1], lw.t[:, 1:NP - 1], ACT.Exp), reads=[lw], writes=[w])
                self.stpad(D_["W%d" % d], c * 128, w)
            s_ = self.wk.get()
            P.op("gpsimd", lambda e: e.tensor_tensor(s_.t[:, 1:NP - 1], kds[0].t[:, 1:NP - 1], kds[1].t[:, 1:NP - 1], ALU.add),
                 reads=kds, writes=[s_])
            pr = self.wk.get()
            P.op("vector", lambda e: e.scalar_tensor_tensor(pr.t[:, 1:NP - 1], s_.t[:, 1:NP - 1], col("rk", c), r_s.t[:, 1:NP - 1],
                                                            ALU.mult, ALU.mult), reads=[s_, r_s, PVt], writes=[pr])
            bon = self.wk.get()

            def f2(ps, c0, cs):
                P.op("vector", lambda e: e.tensor_tensor(bon.t[:, c0:c0 + cs], ps.t[:, 0:cs], v_s.t[:, c0:c0 + cs], ALU.mult),
                     reads=[ps, v_s], writes=[bon])
            self.blocksum(pr, f2)
            self.stpad(D_["BON"], c * 128, bon)

    def rwkv_scan(self, D_, TC=32):
        P = self.P
        cfg = self.cfg
        G = cfg.BC
        GV = G * 64
        H = cfg.H_RW
        C, NT = cfg.C, cfg.NT
        S = []
        for d in range(2):
            s = {}
            s["M"] = P.sb("sM%d" % d, [128, GV], F32)
            P.op("vector", lambda e, s=s: e.memset(s["M"].t[:, :], 0.0), writes=[s["M"]])
            for nm in ("KK", "R", "KD", "NB", "W", "V"):
                s[nm] = P.sb("s%s%d" % (nm, d), [128, G, TC], F32)
            s["LK"] = P.sb("sLK%d" % d, [128, TC, 16 * ((H + 15) // 16)], F32)
            s["LR"] = P.sb("sLR%d" % d, [128, TC, 16 * ((H + 15) // 16)], F32)
            for nm in ("LK", "LR"):
                P.op("gpsimd", lambda e, t=s[nm]: e.memset(t.t[:, :, :], 0.0), writes=[s[nm]])
            s["VT"] = P.sb("sVT%d" % d, [TC, G * 128], F32)
            s["SA"] = P.sb("sSA%d" % d, [16, GV], F32)
            s["T1"] = P.sb("sT1%d" % d, [128, GV], F32)
            s["T2"] = P.sb("sT2%d" % d, [128, GV], F32)
            s["OM"] = P.sb("sOM%d" % d, [16, 8, GV], F32)
            s["OR"] = P.sb("sOR%d" % d, [16, TC, 64], F32)
            s["pA"] = P.ps("pA%d" % d, [128, 512], F32)
            s["pB"] = P.ps("pB%d" % d, [128, 512], F32)
            s["pS"] = P.ps("pS%d" % d, [128, 512], F32)
            s["pV"] = P.ps("pV%d" % d, [128, 512], F32)
            S.append(s)
        HP = H
        assert HP <= 16 and GV <= 512

        def chunks(d):
            out = []
            for a, b in ((0, C), (C, NT)):
                cs = [(t0, min(TC, b - t0)) for t0 in range(a, b, TC)]
                if d == 1:
                    cs = cs[::-1]
                out += cs
            return out
        ch = [chunks(0), chunks(1)]
        src = [dict(KK="KK", R="R", KD="KD0", NB="NB0", W="W0", V="V"), dict(KK="KK", R="R", KD="KD1", NB="NB1", W="W1", V="V")]

        def bcast(t, ti):
            return t.t[:, :, ti:ti + 1].broadcast_to([128, G, 64])

        def load_chunk(d, t0, tn):
            s = S[d]
            for nm in ("KK", "R", "KD", "NB", "W", "V"):
                dr = D_[src[d][nm]]
                P.dma("sync", s[nm], s[nm].t[:, :, 0:tn], dr, dr.t[:, t0:t0 + tn].rearrange("(g p) t -> p g t", p=128))
            for nm, L in (("KK", "LK"), ("R", "LR")):
                for par in range(2):
                    lo = par * 64
                    outv = s[L].t[lo:lo + 64, 0:tn, 0:2 * G].rearrange("p t (g two) -> p t g two", two=2)[:, :, :, par]
                    inv = s[nm].t[lo:lo + 64, :, 0:tn].rearrange("p g t -> p t g")
                    P.op("gpsimd", lambda e, outv=outv, inv=inv: e.tensor_copy(outv, inv), reads=[s[nm]], writes=[s[L]])
            for g0 in range(0, G, 4):
                nb = min(4, G - g0)
                for j in range(nb):
                    g = g0 + j
                    P.op("tensor", lambda e, g=g, j=j: e.matmul(s["pB"].t[0:tn, j * 128:(j + 1) * 128], s["V"].t[:, g, 0:tn],
                                                                self.ident32.t[:, :], start=True, stop=True),
                         reads=[s["V"], self.ident32], writes=[s["pB"]])
                self.evac("scalar", s["VT"].t[0:tn, g0 * 128:(g0 + nb) * 128], s["pB"].t[0:tn, 0:nb * 128], [s["pB"]], [s["VT"]])

        def mm_sa(d, ti):
            s = S[d]
            P.op("tensor", lambda e: e.matmul(s["pA"].t[0:HP, 0:GV], s["LK"].t[:, ti, 0:HP], s["M"].t[:, :], start=True, stop=True),
                 reads=[s["LK"], s["M"]], writes=[s["pA"]])

        def mm_o(d, ti):
            s = S[d]
            P.op("tensor", lambda e: e.matmul(s["pB"].t[0:HP, 0:GV], s["LR"].t[:, ti, 0:HP], s["M"].t[:, :], start=True, stop=True),
                 reads=[s["LR"], s["M"]], writes=[s["pB"]])
            P.op("vector", lambda e: e.tensor_tensor(s["OM"].t[0:HP, ti % 8, :], s["pB"].t[0:HP, 0:GV], self.mask16.t[0:HP, 0:GV], ALU.mult),
                 reads=[s["pB"], self.mask16], writes=[s["OM"]])
            if (d == 0 and ti % 8 == 7) or (d == 1 and ti % 8 == 0):
                t8 = (ti // 8) * 8
                P.op("vector", lambda e: e.tensor_reduce(
                    s["OR"].t[0:HP, t8:t8 + 8, :], s["OM"].t[0:HP, 0:8, :].rearrange("p t (g v) -> p t v g", v=64), AX.X, ALU.add),
                    reads=[s["OM"]], writes=[s["OR"]])

        def step(d, ti, tn):
            s = S[d]
            M3 = s["M"].t[:, :].rearrange("p (g v) -> p g v", v=64)
            for par in range(2):
                rhs = s["VT"].t[0:tn, :].rearrange("t (g q v) -> t g q v", q=2, v=64)[:, :, par, :]
                P.op("tensor", lambda e, par=par, rhs=rhs: e.matmul(
                    s["pV"].t[par * 64:(par + 1) * 64, 0:GV].rearrange("p (g v) -> p g v", v=64),
                    self.onehot.t[0:tn, ti, :], rhs, start=True, stop=True),
                    reads=[self.onehot, s["VT"]], writes=[s["pV"]])
            P.op("vector", lambda e: e.tensor_tensor(s["T2"].t[:, :].rearrange("p (g v) -> p g v", v=64),
                                                     s["pV"].t[:, 0:GV].rearrange("p (g v) -> p g v", v=64),
                                                     bcast(s["KD"], ti), ALU.mult), reads=[s["pV"], s["KD"]], writes=[s["T2"]])
            mm_sa(d, ti)
            P.op("vector", lambda e: e.tensor_tensor(s["SA"].t[0:HP, :], s["pA"].t[0:HP, 0:GV], self.mask16.t[0:HP, 0:GV], ALU.mult),
                 reads=[s["pA"], self.mask16], writes=[s["SA"]])
            P.op("tensor", lambda e: e.matmul(s["pS"].t[:, 0:GV], self.esel.t[0:HP, :], s["SA"].t[0:HP, :], start=True, stop=True),
                 reads=[self.esel, s["SA"]], writes=[s["pS"]])
            P.op("vector", lambda e: e.tensor_tensor(s["T1"].t[:, :].rearrange("p (g v) -> p g v", v=64),
                                                     s["pS"].t[:, 0:GV].rearrange("p (g v) -> p g v", v=64),
                                                     bcast(s["NB"], ti), ALU.mult), reads=[s["pS"], s["NB"]], writes=[s["T1"]])
            P.op("gpsimd", lambda e: e.tensor_tensor(M3, M3, bcast(s["W"], ti), ALU.mult), reads=[s["M"], s["W"]], writes=[s["M"]])
            P.op("gpsimd", lambda e: e.tensor_tensor(s["M"].t[:, :], s["M"].t[:, :], s["T2"].t[:, :], ALU.add),
                 reads=[s["M"], s["T2"]], writes=[s["M"]])
            P.op("vector", lambda e: e.tensor_tensor(s["M"].t[:, :], s["M"].t[:, :], s["T1"].t[:, :], ALU.add),
                 reads=[s["M"], s["T1"]], writes=[s["M"]])
            mm_o(d, ti)

        def flush(d, t0, tn):
            s = S[d]
            assert tn % 8 == 0
            dst = D_["O%d" % d]
            P.dma("gpsimd", dst, dst.t[t0:t0 + tn, :].rearrange("t (h v) -> h t v", v=64), s["OR"], s["OR"].t[0:HP, 0:tn, :])

        nch = len(ch[0])
        for ci in range(nch):
            for d in range(2):
                t0, tn = ch[d][ci]
                load_chunk(d, t0, tn)
            order = [list(range(ch[0][ci][1])), list(range(ch[1][ci][1]))[::-1]]
            for si in range(max(len(order[0]), len(order[1]))):
                for d in range(2):
                    if si < len(order[d]):
                        step(d, order[d][si], ch[d][ci][1])
            for d in range(2):
                flush(d, *ch[d][ci])

    def rwkv_readout(self, D_, Y, yrow0, lng, lnb):
        P = self.P
        cfg = self.cfg
        BW, H = cfg.BW, cfg.H_RW
        tk = Pool([P.sb("ro%d" % i, [128, BW], F32) for i in range(5)])
        sm = Pool([P.sb("rs%d" % i, [128, H], F32) for i in range(4)])
        fm = Pool([P.sb("rf%d" % i, [128, 128], F32) for i in range(4)])
        fmb = Pool([P.sb("rfb%d" % i, [128, 128], BF16) for i in range(2)])
        for t0 in range(0, cfg.NT, 128):
            a = tk.get()
            b = tk.get()
            P.dma("sync", a, a.t[:, :], D_["O0"], D_["O0"].t[t0:t0 + 128, :])
            P.dma("sync", b, b.t[:, :], D_["O1"], D_["O1"].t[t0:t0 + 128, :])
            P.op("vector", lambda e: e.tensor_tensor(a.t[:, :], a.t[:, :], b.t[:, :], ALU.add), reads=[a, b], writes=[a])
            a3 = a.t[:, :].rearrange("p (h v) -> p h v", v=64)
            mean = sm.get()
            P.op("vector", lambda e: e.tensor_reduce(mean.t[:, :], a3, AX.X, ALU.add), reads=[a], writes=[mean])
            P.op("vector", lambda e: e.tensor_scalar(mean.t[:, :], mean.t[:, :], -1.0 / 64, None, ALU.mult), reads=[mean], writes=[mean])
            P.op("vector", lambda e: e.tensor_tensor(a3, a3, mean.t[:, :].unsqueeze(2).broadcast_to([128, H, 64]), ALU.add),
                 reads=[a, mean], writes=[a])
            sq = tk.get()
            P.op("gpsimd", lambda e: e.tensor_tensor(sq.t[:, :], a.t[:, :], a.t[:, :], ALU.mult), reads=[a], writes=[sq])
            var = sm.get()
            P.op("vector", lambda e: e.tensor_reduce(var.t[:, :], sq.t[:, :].rearrange("p (h v) -> p h v", v=64), AX.X, ALU.add),
                 reads=[sq], writes=[var])
            P.op("scalar", lambda e: e.activation(var.t[:, :], var.t[:, :], ACT.Sqrt, bias=self.epsgn.t[:, 0:1], scale=1.0 / 64),
                 reads=[var, self.epsgn], writes=[var])
            P.op("vector", lambda e: e.reciprocal(var.t[:, :], var.t[:, :]), reads=[var], writes=[var])
            P.op("vector", lambda e: e.tensor_tensor(a3, a3, var.t[:, :].unsqueeze(2).broadcast_to([128, H, 64]), ALU.mult),
                 reads=[a, var], writes=[a])
            P.op("vector", lambda e: e.tensor_tensor(a.t[:, :], a.t[:, :], lng.t[:, :], ALU.mult), reads=[a, lng], writes=[a])
            P.op("gpsimd", lambda e: e.tensor_tensor(a.t[:, :], a.t[:, :], lnb.t[:, :], ALU.add), reads=[a, lnb], writes=[a])
            for c in range(cfg.BC):
                ps = self.pspool.get()
                P.op("tensor", lambda e, c=c: e.matmul(ps.t[:, 0:128], a.t[:, c * 128:(c + 1) * 128], self.ident32.t[:, :],
                                                       start=True, stop=True), reads=[a, self.ident32], writes=[ps])
                bon = fm.get()
                g = fm.get()
                P.dma("sync", bon, bon.t[:, :], D_["BON"], D_["BON"].t[c * 128:(c + 1) * 128, t0:t0 + 128])
                P.dma("sync", g, g.t[:, :], D_["G"], D_["G"].t[c * 128:(c + 1) * 128, t0:t0 + 128])
                P.op("vector", lambda e: e.tensor_tensor(bon.t[:, :], ps.t[:, 0:128], bon.t[:, :], ALU.add), reads=[ps, bon], writes=[bon])
                ob = fmb.get()
                P.op("vector", lambda e: e.tensor_tensor(ob.t[:, :], bon.t[:, :], g.t[:, :], ALU.mult), reads=[bon, g], writes=[ob])
                P.dma("gpsimd", Y, Y.t[yrow0 + c * 128:yrow0 + (c + 1) * 128, t0:t0 + 128], ob, ob.t[:, :])

    def conv(self, P32, Y, yrow0):
        P = self.P
        cfg = self.cfg
        NP = self.NP
        pv = self.pv
        col = lambda name, c: self.PVt.t[:, pv[name][0] + c:pv[name][0] + c + 1]
        for c in range(cfg.BC):
            xb = self.ldpad(P32, cfg.seg["cb"][0] + c * 128)
            xc = self.ldpad(P32, cfg.seg["cc"][0] + c * 128)
            xu = self.ldpad(P32, cfg.seg["cu"][0] + c * 128)
            z = self.wk.get()
            P.op("vector", lambda e: e.tensor_tensor(z.t[:, :], xc.t[:, :], xu.t[:, :], ALU.mult), reads=[xc, xu], writes=[z])
            y = self.wk.get()
            P.op("vector", lambda e: e.tensor_scalar(y.t[:, 1:NP - 1], z.t[:, 0:NP - 2], col("cw0", c), None, ALU.mult),
                 reads=[z, self.PVt], writes=[y])
            P.op("vector", lambda e: e.scalar_tensor_tensor(y.t[:, 1:NP - 1], z.t[:, 1:NP - 1], col("cw1", c), y.t[:, 1:NP - 1],
                                                            ALU.mult, ALU.add), reads=[z, y, self.PVt], writes=[y])
            P.op("vector", lambda e: e.scalar_tensor_tensor(y.t[:, 1:NP - 1], z.t[:, 2:NP], col("cw2", c), y.t[:, 1:NP - 1],
                                                            ALU.mult, ALU.add), reads=[z, y, self.PVt], writes=[y])
            ob = self.wkb.get()
            P.op("vector", lambda e: e.tensor_tensor(ob.t[:, 1:NP - 1], y.t[:, 1:NP - 1], xb.t[:, 1:NP - 1], ALU.mult),
                 reads=[y, xb], writes=[ob])
            self.stpad(Y, yrow0 + c * 128, ob)

    def merge(self, GL, Y, wfull, ACC16):
        P = self.P
        cfg = self.cfg
        pv = self.pv
        BWC = cfg.BWG // 128
        BWL = cfg.BW // 128
        nmax = max(128, min(512, (16384 // (2 + 4 * BWC)) // 128 * 128))
        sub = []
        for a_, ns_ in cfg.tiles:
            o_ = 0
            while o_ < ns_:
                s_ = min(nmax, ns_ - o_)
                sub.append((a_ + o_, s_))
                o_ += s_
        for n0, nsz in sub:
            xt = self.xpool.get()
            flat = xt.t
            P.dma("sync", xt, flat[:, 0:2 * nsz].rearrange("p (kc n) -> p kc n", n=nsz),
                  GL, GL.t[0:256, n0:n0 + nsz].rearrange("(kc p) n -> p kc n", p=128))
            for rk in range(cfg.split):
                for i in range(4):
                    for kl in range(BWL):
                        s0 = 2 + i * BWC + rk * BWL + kl
                        r0 = ((i * BWL + kl) * cfg.split + rk) * 128
                        P.dma("sync", xt, flat[:, s0 * nsz:(s0 + 1) * nsz], Y, Y.t[r0:r0 + 128, n0:n0 + nsz])
            for mc in range(cfg.DC):
                acc = self.accpool.get()
                for i in range(4):
                    wg = self.wpool.get()
                    P.dma("sync", wg, wg.t[:, 0:256], wfull[0], self.wview(wfull, "gu_%d" % i, mc)[:, 0:256])
                    P.dma("sync", wg, wg.t[:, 256:256 + BWC * 128], wfull[0], self.wview(wfull, "wb_%d" % i, mc)[:, 0:BWC * 128])
                    pg = self.pspool.get()
                    for kc in range(2):
                        P.op("tensor", lambda e, kc=kc: e.matmul(pg.t[:, 0:nsz], wg.t[:, kc * 128:(kc + 1) * 128],
                                                                 flat[:, kc * nsz:(kc + 1) * nsz], start=(kc == 0), stop=(kc == 1)),
                             reads=[wg, xt], writes=[pg])
                    gt = self.opool.get()
                    gb = self.PVt.t[:, pv["gb_%d" % i][0] + mc:pv["gb_%d" % i][0] + mc + 1]
                    P.op("scalar", lambda e: e.activation(gt.t[:, 0:nsz], pg.t[:, 0:nsz], ACT.Sigmoid, bias=gb),
                         reads=[pg, self.PVt], writes=[gt])
                    pb = self.pspool.get()
                    for kc in range(BWC):
                        xc = 2 + i * BWC + kc
                        P.op("tensor", lambda e, kc=kc, xc=xc: e.matmul(pb.t[:, 0:nsz], wg.t[:, 256 + kc * 128:256 + (kc + 1) * 128],
                                                                        flat[:, xc * nsz:(xc + 1) * nsz], start=(kc == 0), stop=(kc == BWC - 1)),
                             reads=[wg, xt], writes=[pb])
                    if i == 0:
                        P.op("vector", lambda e: e.tensor_tensor(acc.t[:, 0:nsz], pb.t[:, 0:nsz], gt.t[:, 0:nsz], ALU.mult),
                             reads=[pb, gt], writes=[acc])
                    else:
                        P.op("vector", lambda e: e.tensor_tensor(gt.t[:, 0:nsz], pb.t[:, 0:nsz], gt.t[:, 0:nsz], ALU.mult),
                             reads=[pb, gt], writes=[gt])
                        P.op("gpsimd", lambda e: e.tensor_tensor(acc.t[:, 0:nsz], acc.t[:, 0:nsz], gt.t[:, 0:nsz], ALU.add),
                             reads=[acc, gt], writes=[acc])
                ob = self.opoolb.get()
                P.op("vector", lambda e: e.tensor_copy(ob.t[:, 0:nsz], acc.t[:, 0:nsz]), reads=[acc], writes=[ob])
                P.dma("gpsimd", ACC16, ACC16.t[mc * 128:(mc + 1) * 128, n0:n0 + nsz], ob, ob.t[:, 0:nsz])

    def epi_residual(self, xold, xnew, j):
        P = self.P
        cfg = self.cfg

        def epi(ps, m0, msz, n0, nsz):
            ci = cfg.cond_of(n0)
            xt = self.ld(self.opool, xold, m0, msz, n0, nsz)
            mcol = self.modS.t[0:msz, j * cfg.DC + m0 // 128, ci:ci + 1]
            P.op("vector", lambda e: e.scalar_tensor_tensor(xt.t[0:msz, 0:nsz], ps.t[0:msz, 0:nsz], mcol, xt.t[0:msz, 0:nsz],
                                                            ALU.mult, ALU.add), reads=[ps, xt, self.modS], writes=[xt])
            self.st(xnew, m0, msz, n0, nsz, xt)
        return epi

    def modulation(self, COND16, wfull, T16):
        P = self.P
        cfg = self.cfg
        pv = self.pv
        self.linear(COND16, cfg.D, wfull, "mod_down", cfg.MR, self.epi_store(dst16=T16), tiles=[(0, 2)])

        def epi(ps, m0, msz, n0, nsz):
            mc = m0 // 128
            b = self.PVt.t[:, pv["modb"][0] + mc:pv["modb"][0] + mc + 1]
            P.op("scalar", lambda e: e.activation(self.modS.t[:, mc, 0:2], ps.t[:, 0:2], ACT.Identity, bias=b),
                 reads=[ps, self.PVt], writes=[self.modS])
        self.linear(T16, cfg.MR, wfull, "mod_up", 6 * cfg.D, epi, tiles=[(0, 2)])
        DC = cfg.DC
        for k, (gname, j) in enumerate((("n1g", 1), ("n2g", 4))):
            g = self.PVt.t[:, pv[gname][0]:pv[gname][0] + DC]
            out = self.modA.t[:, k * DC:(k + 1) * DC, :]
            P.op("vector", lambda e, out=out, j=j: e.tensor_scalar(out, self.modS.t[:, j * DC:(j + 1) * DC, :], 1.0, None, ALU.add),
                 reads=[self.modS], writes=[self.modA])
            P.op("vector", lambda e, out=out, g=g: e.tensor_tensor(out, out, g.unsqueeze(2).broadcast_to([128, DC, 2]), ALU.mult),
                 reads=[self.modA, self.PVt], writes=[self.modA])

import math
import numpy as np
import concourse.bass as bass
import concourse.mybir as mybir


TC_SCAN = 32


def pv_layout(cfg):
    DC, BC = cfg.DC, cfg.BC
    QC, KVC = pad128(cfg.QL) // 128, pad128(cfg.KVL) // 128
    RC = 3 * BC + 4
    items = [("n1g", DC), ("n2g", DC), ("modb", 6 * DC), ("qg", QC), ("kvg", KVC), ("mu0", RC), ("mu1", RC),
             ("w0_0", BC), ("w0_1", BC), ("a0_0", BC), ("a0_1", BC), ("kk", BC), ("ka", BC), ("c1", BC), ("rk", BC),
             ("cw0", BC), ("cw1", BC), ("cw2", BC), ("gb_0", DC), ("gb_1", DC), ("gb_2", DC), ("gb_3", DC)]
    pv = {}
    o = 0
    for n, k in items:
        pv[n] = (o, k)
        o += k
    return pv, o


def build_program(cfg, debug_outs=()):
    nc = bass.Bass("TRN2", target_bir_lowering=False)
    P = Prog(nc)
    m = Model(P, cfg)
    D, NT, C, T, BW, L = cfg.D, cfg.NT, cfg.C, cfg.T, cfg.BW, cfg.DEPTH
    DC, BC = cfg.DC, cfg.BC
    H = cfg.H_MLA
    QLp, KVLp = pad128(cfg.QL), pad128(cfg.KVL)
    pv, PVN = pv_layout(cfg)
    m.pv = pv

    def inp(name, shape):
        return P.dram(name, shape, F32, kind="ExternalInput")
    xT = inp("xT", [D, NT])
    cond = inp("cond", [D, 2])
    pvec = inp("pvec", [L * 128, PVN])
    bcin = inp("bcin", [L * 128, 2 * BW + 128])
    dlin = inp("dlin", [L, 256])
    fng = inp("fng", [128, DC])
    c_ident = inp("c_ident", [128, 128])
    c_blk1 = inp("c_blk1", [128, 128])
    c_mask16 = inp("c_mask16", [16, 512])
    c_esel = inp("c_esel", [16, 128])
    c_onehot = inp("c_onehot", [TC_SCAN, TC_SCAN * 64])
    c_cos = inp("c_cos", [128, NT])
    c_sin = inp("c_sin", [128, NT])
    NG = cfg.NG
    nshard = max(1, cfg.ncores // cfg.split)
    shard_rows = [cfg.wrows[g] // nshard for g in range(NG)]
    wsh = [inp("wflat%d" % g, [L * shard_rows[g], WCOLS]) for g in range(NG)]
    yT = P.dram("yT", [D, T], F32, kind="ExternalOutput")
    dbg = {}

    wfull = []
    for l in range(L):
        grp = []
        for g in range(NG):
            wf = P.dram("wfull%d_%d" % (l, g), [cfg.wrows[g], WCOLS], BF16)
            sr = shard_rows[g]
            if cfg.ncores == 1:
                tgt = wf
            else:
                tgt = P.dram("wsh16_%d_%d" % (l, g), [sr, WCOLS], BF16)
            for r0 in range(0, sr, 4096):
                rn = min(4096, sr - r0)
                P.dma("gpsimd", tgt, tgt.t[r0:r0 + rn, :], wsh[g], wsh[g].t[l * sr + r0:l * sr + r0 + rn, :])
            if cfg.ncores > 1:
                wgroups = [list(range(h * nshard, (h + 1) * nshard)) for h in range(cfg.split)]
                for j in range(sr // GCH):
                    P.collective("AllGather", ALU.bypass, wgroups, tgt, tgt.t[j * GCH:(j + 1) * GCH, :],
                                 wf, wf.t[j * nshard * GCH:(j + 1) * nshard * GCH, :])
            grp.append(wf)
        wfull.append(grp)

    def const_tile(name, src, shape, dt=F32):
        t = P.sb(name, shape, dt)
        P.dma("sync", t, t.t[:], src, src.t[:])
        return t
    m.ident32 = const_tile("ident32", c_ident, [128, 128])
    m.blk1 = const_tile("blk1", c_blk1, [128, 128])
    m.mask16 = const_tile("mask16", c_mask16, [16, 512])
    m.esel = const_tile("esel", c_esel, [16, 128])
    oh = P.sb("onehot", [TC_SCAN, TC_SCAN, 64], F32)
    P.dma("sync", oh, oh.t[:].rearrange("p a b -> p (a b)"), c_onehot, c_onehot.t[:])
    m.onehot = oh
    m.cos2 = const_tile("cos2", c_cos, [128, NT])
    m.sin2 = const_tile("sin2", c_sin, [128, NT])
    m.identb = P.sb("identb", [128, 128], BF16)
    P.op("vector", lambda e: e.tensor_copy(m.identb.t[:], m.ident32.t[:]), reads=[m.ident32], writes=[m.identb])
    m.ones = P.sb("ones", [128, 128], F32)
    P.op("vector", lambda e: e.memset(m.ones.t[:], 1.0), writes=[m.ones])
    for nm, val in (("eps_t", NORM_EPS), ("eps12", 1e-12), ("epsgn", GN_EPS)):
        t = P.sb(nm, [128, 1], F32)
        P.op("vector", lambda e, t=t, val=val: e.memset(t.t[:], val), writes=[t])
        setattr(m, nm, t)
    fngt = const_tile("fngt", fng, [128, DC])
    m.PVt = P.sb("PVt", [128, PVN], F32)
    m.modS = P.sb("modS", [128, 6 * DC, 2], F32)
    m.modA = P.sb("modA", [128, 2 * DC, 2], F32)
    bct = P.sb("bct", [128, 2 * BW + 128], F32)
    lam_row = P.sb("lam_row", [1, 260], F32)
    lam_col = P.sb("lam_col", [128, 2], F32)
    gtile = P.sb("gtile", [128, 128], F32)

    def dr(name, shape, dt=F32):
        return P.dram(name, shape, dt)
    xres = [xT, dr("xresA", [D, NT]), dr("xresB", [D, NT])]
    hB = dr("hB", [D, NT], BF16)
    P32 = dr("P32", [cfg.PCOLS, NT])
    CQN = dr("CQN", [QLp, NT], BF16)
    CKVN = dr("CKVN", [KVLp, NT], BF16)
    Q32 = dr("Q32", [H * 128, NT])
    QN = dr("QN", [H * 128, NT], BF16)
    QR = dr("QR", [H * 64, NT], BF16)
    KN = dr("KN", [H * 128, NT], BF16)
    VM = dr("VM", [H * 128, NT], BF16)
    KR = dr("KR", [64, NT], BF16)
    DQ = dr("DQ", [BW, NT], BF16)
    DK = dr("DK", [BW, NT], BF16)
    DV = dr("DV", [BW, NT], BF16)
    Y = dr("Y", [4 * BW, NT], BF16)
    YG = dr("YG", [cfg.split * 4 * BW, NT], BF16) if cfg.split > 1 else Y
    pairs = [[c, c + cfg.ncores // 2] for c in range(cfg.ncores // 2)] if cfg.split > 1 else None
    MP = dr("MP", [D, NT]) if cfg.split > 1 else None
    MPR = dr("MPR", [D, NT]) if cfg.split > 1 else None
    RW = {n: dr("RW_" + n, [BW, NT]) for n in ("R", "V", "KK", "KD0", "KD1", "NB0", "NB1", "W0", "W1", "BON", "G",
                                               "LW0", "LW1", "A0", "A1")}
    RW["LORA16"] = dr("LORA16", [512, NT], BF16)
    RW["O0"] = dr("RW_O0", [NT, BW])
    RW["O1"] = dr("RW_O1", [NT, BW])
    GL = dr("GL", [256, NT], BF16)
    ACC16 = dr("ACC16", [D, NT], BF16)
    HID16 = dr("HID16", [cfg.DFFL, NT], BF16)
    COND16 = dr("COND16", [D, 2], BF16)
    T16 = dr("T16", [cfg.MR, 2], BF16)

    with P.scope():
        ct = P.sb("condt", [128, DC, 2], F32)
        cb = P.sb("condb", [128, DC, 2], BF16)
        P.dma("sync", ct, ct.t[:], cond, cond.t[:, :].rearrange("(c p) n -> p c n", p=128))
        P.op("scalar", lambda e: e.activation(cb.t[:], ct.t[:], ACT.Silu), reads=[ct], writes=[cb])
        P.dma("gpsimd", COND16, COND16.t[:, :].rearrange("(c p) n -> p c n", p=128), cb, cb.t[:])

    col = lambda name, c: m.PVt.t[:, pv[name][0] + c:pv[name][0] + c + 1]
    cur = 0
    for l in range(L):
        wf = wfull[l]
        lam_init = 0.8 - 0.6 * math.exp(-0.3 * l)
        P.dma("sync", m.PVt, m.PVt.t[:, :], pvec, pvec.t[l * 128:(l + 1) * 128, :])
        P.dma("sync", bct, bct.t[:, :], bcin, bcin.t[l * 128:(l + 1) * 128, :])
        P.dma("sync", lam_row, lam_row.t[0:1, 0:256], dlin, dlin.t[l:l + 1, :])
        ka = m.PVt.t[:, pv["ka"][0]:pv["ka"][0] + BC]
        P.op("vector", lambda e: e.tensor_scalar(m.PVt.t[:, pv["c1"][0]:pv["c1"][0] + BC], ka, -1.0, 1.0, ALU.mult, ALU.add),
             reads=[m.PVt], writes=[m.PVt])
        P.op("vector", lambda e: e.tensor_tensor(lam_row.t[0:1, 0:64], lam_row.t[0:1, 0:64], lam_row.t[0:1, 64:128], ALU.mult),
             reads=[lam_row], writes=[lam_row])
        P.op("vector", lambda e: e.tensor_tensor(lam_row.t[0:1, 128:192], lam_row.t[0:1, 128:192], lam_row.t[0:1, 192:256], ALU.mult),
             reads=[lam_row], writes=[lam_row])
        P.op("vector", lambda e: e.reduce_sum(lam_row.t[0:1, 256:257], lam_row.t[0:1, 0:64], AX.X), reads=[lam_row], writes=[lam_row])
        P.op("vector", lambda e: e.reduce_sum(lam_row.t[0:1, 257:258], lam_row.t[0:1, 128:192], AX.X), reads=[lam_row], writes=[lam_row])
        P.op("scalar", lambda e: e.activation(lam_row.t[0:1, 256:258], lam_row.t[0:1, 256:258], ACT.Exp), reads=[lam_row], writes=[lam_row])
        P.op("vector", lambda e: e.tensor_tensor(lam_row.t[0:1, 258:259], lam_row.t[0:1, 257:258], lam_row.t[0:1, 256:257], ALU.subtract),
             reads=[lam_row], writes=[lam_row])
        P.op("vector", lambda e: e.tensor_scalar(lam_row.t[0:1, 258:259], lam_row.t[0:1, 258:259], -lam_init, None, ALU.add),
             reads=[lam_row], writes=[lam_row])
        P.op("vector", lambda e: e.tensor_copy(lam_row.t[0:1, 259:260], lam_row.t[0:1, 258:259]), reads=[lam_row], writes=[lam_row])
        with P.scope():
            pl = P.ps("pl", [128, 512], F32)
            P.op("tensor", lambda e: e.matmul(pl.t[:, 0:2], m.ones.t[0:1, :], lam_row.t[0:1, 258:260], start=True, stop=True),
                 reads=[m.ones, lam_row], writes=[pl])
            P.op("vector", lambda e: e.tensor_copy(lam_col.t[:, 0:2], pl.t[:, 0:2]), reads=[pl], writes=[lam_col])
        P.op("vector", lambda e: e.tensor_scalar(gtile.t[:, :], bct.t[:, 2 * BW:2 * BW + 128], 1.0 - lam_init, None, ALU.mult),
             reads=[bct], writes=[gtile])
        with P.scope():
            m.alloc_linear_bufs()
            m.modulation(COND16, wf, T16)
        xin = xres[cur]
        xmid = xres[(cur + 1) % 3]
        xout = xres[(cur + 2) % 3]
        with P.scope():
            m.pspool = Pool([P.ps("ps%d" % i, [128, 512], F32) for i in range(4)])
            m.alloc_norm_bufs()
            m.rmsnorm(xin, 0, D, hB, 0, lambda c, ci: m.modA.t[:, c, ci:ci + 1], lambda c, ci: m.modS.t[:, c, ci:ci + 1],
                      deps=[m.modA, m.modS], ones=m.ones)
        with P.scope():
            m.alloc_linear_bufs()
            m.linear(hB, D, wf, "w_in", cfg.PCOLS, m.epi_store(dst32=P32))
            m.linear(hB, D, wf, "gate_down", cfg.GR, m.epi_store(dst16=GL))
        if "P32" in debug_outs and l == 0:
            dbg["P32"] = P32
        with P.scope():
            m.pspool = Pool([P.ps("ps%d" % i, [128, 512], F32) for i in range(4)])
            m.alloc_norm_bufs()
            m.rmsnorm(P32, cfg.seg["cq"][0], cfg.QL, CQN, 0, lambda c, ci: col("qg", c), deps=[m.PVt], ones=m.ones)
            m.rmsnorm(P32, cfg.seg["ckv"][0], cfg.KVL, CKVN, 0, lambda c, ci: col("kvg", c), deps=[m.PVt], ones=m.ones)
        with P.scope():
            m.alloc_linear_bufs()
            m.alloc_ew_bufs()
            e_qn = m.epi_store(dst16=QN)
            e_q32 = m.epi_store(dst32=Q32, row0=-H * 128)

            def epi_q(ps, m0, msz, n0, nsz):
                (e_qn if m0 < H * 128 else e_q32)(ps, m0, msz, n0, nsz)
            m.linear(CQN, QLp, wf, "w_uq", H * 256, epi_q)
            e_kn = m.epi_store(dst16=KN)
            e_vm = m.epi_store(dst16=VM, row0=-H * 128)

            def epi_kv(ps, m0, msz, n0, nsz):
                (e_kn if m0 < H * 128 else e_vm)(ps, m0, msz, n0, nsz)
            m.linear(CKVN, KVLp, wf, "w_ukv", H * 256, epi_kv)
            m.rope(Q32, 0, H * 64, H * 64, QR, 0)
            m.rope(P32, cfg.seg["kr"][0], cfg.seg["kr"][0] + 64, 64, KR, 0)
            m.rope(P32, cfg.seg["dq"][0], cfg.seg["dqr"][0], BW, DQ, 0)
            m.rope(P32, cfg.seg["dk"][0], cfg.seg["dkr"][0], BW, DK, 0)
            m.cast_rows(P32, cfg.seg["dv"][0], BW, DV, 0)
        with P.scope():
            m.alloc_attn_bufs()
            m.mla_attention(QN, QR, KN, KR, VM, Y, 0)
            m.diff_attention(DQ, DK, DV, Y, 3 * BW, lam_col, gtile, lam_init)
        with P.scope():
            m.alloc_pad_bufs(3, 4)
            m.rwkv_prep_a(P32, RW)
        with P.scope():
            m.alloc_linear_bufs()
            m.rwkv_prep_b(wf, RW)
        with P.scope():
            m.pspool = Pool([P.ps("ps%d" % i, [128, 512], F32) for i in range(4)])
            m.alloc_pad_bufs(6, 4)
            m.rwkv_prep_c(P32, RW)
        with P.scope():
            m.rwkv_scan(RW, TC=TC_SCAN)
        with P.scope():
            m.pspool = Pool([P.ps("ps%d" % i, [128, 512], F32) for i in range(4)])
            lng = Res("lng", bct.t[:, 0:BW])
            lnb = Res("lnb", bct.t[:, BW:2 * BW])
            lng.w = lnb.w = bct.w
            lng.r = lnb.r = bct.r
            m.rwkv_readout(RW, Y, BW, lng, lnb)
        with P.scope():
            m.alloc_pad_bufs(4, 4)
            m.conv(P32, Y, 2 * BW)
        if l == 0:
            for n_ in debug_outs:
                if n_ == "Y":
                    dbg["Y"] = Y
                if n_ in RW:
                    dbg[n_] = RW[n_]
        with P.scope():
            m.alloc_linear_bufs()
            import os as _os
            if cfg.split > 1 and cfg.ncores > 1 and not _os.environ.get("NO_YG"):
                for lc in range(4 * BW // 128):
                    P.collective("AllGather", ALU.bypass, pairs, Y, Y.t[lc * 128:(lc + 1) * 128, :],
                                 YG, YG.t[lc * 256:(lc + 1) * 256, :])
            m.merge(GL, YG, wf, ACC16)
            m.linear(ACC16, D, wf, "w_out", D, m.epi_residual(xin, xmid, 2))
        with P.scope():
            m.pspool = Pool([P.ps("ps%d" % i, [128, 512], F32) for i in range(4)])
            m.alloc_norm_bufs()
            m.rmsnorm(xmid, 0, D, hB, 0, lambda c, ci: m.modA.t[:, DC + c, ci:ci + 1], lambda c, ci: m.modS.t[:, 3 * DC + c, ci:ci + 1],
                      deps=[m.modA, m.modS], ones=m.ones)
        with P.scope():
            m.alloc_linear_bufs()
            m.linear(hB, D, wf, "mlp_w1", cfg.DFFL, m.epi_store(dst16=HID16, func=ACT.Relu, square=True))
            nmx = max(128, min(512, (16384 // (cfg.DFFL // 128)) // 128 * 128))
            if cfg.split > 1:
                m.linear(HID16, cfg.DFFL, wf, "mlp_w2", D, m.epi_store(dst32=MP), nmax=nmx)
                if cfg.ncores > 1 and not _os.environ.get("NO_AR"):
                    for r0 in range(0, D, 64):
                        P.collective("AllReduce", ALU.add, pairs, MP, MP.t[r0:r0 + 64, :], MPR, MPR.t[r0:r0 + 64, :])
                for mc in range(DC):
                    for n0, nsz in cfg.tiles:
                        ci = cfg.cond_of(n0)
                        xt_ = m.ld(m.opool, xmid, mc * 128, 128, n0, nsz)
                        pt_ = m.ld(m.opool, MPR, mc * 128, 128, n0, nsz)
                        mcol = m.modS.t[:, 5 * DC + mc, ci:ci + 1]
                        P.op("vector", lambda e, xt_=xt_, pt_=pt_, mcol=mcol, nsz=nsz: e.scalar_tensor_tensor(
                            xt_.t[:, 0:nsz], pt_.t[:, 0:nsz], mcol, xt_.t[:, 0:nsz], ALU.mult, ALU.add),
                            reads=[xt_, pt_, m.modS], writes=[xt_])
                        m.st(xout, mc * 128, 128, n0, nsz, xt_)
            else:
                m.linear(HID16, cfg.DFFL, wf, "mlp_w2", D, m.epi_residual(xmid, xout, 5), nmax=nmx)
        cur = (cur + 2) % 3
        if l == 0 and "X1" in debug_outs:
            dbg["X1"] = xres[cur]
        if l == 0 and "XMID" in debug_outs:
            dbg["XMID"] = xmid
    with P.scope():
        m.pspool = Pool([P.ps("ps%d" % i, [128, 512], F32) for i in range(4)])
        m.alloc_norm_bufs(out_dt=F32)
        xt_tiles = [(a, s) for a, s in cfg.tiles if a >= C]
        m.rmsnorm(xres[cur], 0, D, yT, 0, lambda c, ci: fngt.t[:, c:c + 1], deps=[fngt], ones=m.ones, tiles=xt_tiles, dcol=C)
    outs = [yT]
    dbg_out = {}
    for n_, r in dbg.items():
        shp = list(r.t.shape)
        o = P.dram("dbg_" + n_, shp, F32, kind="ExternalOutput")
        P.dma("gpsimd", o, o.t[:, :], r, r.t[:, :])
        outs.append(o)
        dbg_out[n_] = "dbg_" + n_
    P.finish(outs)
    return nc, P, dbg_out


def tile_weight(W):
    K, M = W.shape
    KC, MC = cdiv(K, 128), cdiv(M, 128)
    Wp = np.zeros((KC * 128, MC * 128), np.float32)
    Wp[:K, :M] = W
    return np.ascontiguousarray(Wp.reshape(KC, 128, MC, 128).transpose(2, 1, 0, 3)).reshape(-1)


def pp(v, nchunks=None):
    v = np.asarray(v, np.float32).reshape(-1)
    k = cdiv(v.size, 128) if nchunks is None else nchunks
    o = np.zeros(k * 128, np.float32)
    o[:v.size] = v
    return np.ascontiguousarray(o.reshape(k, 128).T)


def rot64(idx):
    idx = np.asarray(idx).reshape(-1, 64)
    return np.concatenate([idx[:, 32:], idx[:, :32]], axis=1).reshape(-1)


def host_layer_weights(cfg, inp, l, half=0):
    D, BW, H = cfg.D, cfg.BW, cfg.H_MLA
    BWG = cfg.BWG
    ch = np.arange(half * BW, (half + 1) * BW)
    QL, KVL = cfg.QL, cfg.KVL
    w_in = inp["w_in"][l]
    o_cq, o_ckv, o_kr = 0, QL, QL + KVL
    o_rw = QL + KVL + 64
    o_r, o_k, o_v = o_rw, o_rw + BWG, o_rw + 2 * BWG
    o_wd = o_rw + 3 * BWG
    o_ad = o_wd + 128
    o_gd = o_ad + 128
    o_cv = o_gd + 160
    o_df = o_cv + 3 * BWG
    ext = np.zeros((D, cfg.PCOLS), np.float32)

    def put(name, cols):
        a, s = cfg.seg[name]
        ext[:, a:a + len(cols)] = w_in[:, cols]
    put("cq", np.arange(o_cq, o_cq + QL))
    put("ckv", np.arange(o_ckv, o_ckv + KVL))
    kr = np.arange(o_kr, o_kr + 64)
    put("kr", np.concatenate([kr, rot64(kr)]))
    put("r", o_r + ch)
    put("k", o_k + ch)
    put("v", o_v + ch)
    put("wd", np.arange(o_wd, o_wd + 128))
    put("ad", np.arange(o_ad, o_ad + 128))
    put("gd", np.arange(o_gd, o_gd + 160))
    put("cb", o_cv + ch)
    put("cc", o_cv + BWG + ch)
    put("cu", o_cv + 2 * BWG + ch)
    dq = o_df + ch
    dk = o_df + BWG + ch
    put("dq", dq)
    put("dk", dk)
    put("dv", o_df + 2 * BWG + ch)
    put("dqr", rot64(dq))
    put("dkr", rot64(dk))
    wq = inp["mla_w_uq"][l]
    hs = range(half * H, (half + 1) * H)
    qn = np.concatenate([np.arange(h * 192, h * 192 + 128) for h in hs])
    qr = np.concatenate([np.arange(h * 192 + 128, h * 192 + 192) for h in hs])
    wq_ext = wq[:, np.concatenate([qn, qr, rot64(qr)])]
    wkv = inp["mla_w_ukv"][l]
    kn = np.concatenate([np.arange(h * 256, h * 256 + 128) for h in hs])
    vv = np.concatenate([np.arange(h * 256 + 128, h * 256 + 256) for h in hs])
    wkv_ext = wkv[:, np.concatenate([kn, vv])]
    g2 = np.zeros((256, BW), np.float32)
    g2[:160] = inp["rwkv_g2"][l][:, ch]
    ws = {
        "mod_down": inp["mod_down"][l], "mod_up": inp["mod_up"][l], "w_in": ext, "w_uq": wq_ext, "w_ukv": wkv_ext,
        "w2_0": inp["rwkv_w2"][l, 0][:, ch], "w2_1": inp["rwkv_w2"][l, 1][:, ch],
        "a2_0": inp["rwkv_a2"][l, 0][:, ch], "a2_1": inp["rwkv_a2"][l, 1][:, ch],
        "g2": g2, "gate_down": inp["gate_down"][l], "w_out": inp["w_out"][l],
        "mlp_w1": inp["mlp_w1"][l][:, half * cfg.DFFL:(half + 1) * cfg.DFFL],
        "mlp_w2": inp["mlp_w2"][l][half * cfg.DFFL:(half + 1) * cfg.DFFL, :],
    }
    for i in range(4):
        ws["wb_%d" % i] = inp["w_branch"][l, i]
        ws["gu_%d" % i] = inp["gate_up"][l][:, i, :]
    flats = [np.zeros(cfg.wrows[g] * WCOLS, np.float32) for g in range(cfg.NG)]
    for n, K, M in cfg.wshapes:
        off = cfg.woff[n][0]
        w = ws[n]
        wp = np.zeros((K, M), np.float32)
        wp[:w.shape[0], :w.shape[1]] = w
        t = tile_weight(wp)
        flats[cfg.wgroup_of[n]][off:off + t.size] = t
    return [flats[g].reshape(cfg.wrows[g], WCOLS) for g in range(cfg.NG)]


def host_pvec(cfg, inp, l, half=0):
    pv, PVN = pv_layout(cfg)
    BW, BC, BWG = cfg.BW, cfg.BC, cfg.BWG
    ch = np.arange(half * BW, (half + 1) * BW)
    out = np.zeros((128, PVN), np.float32)

    def put(name, arr):
        a, k = pv[name]
        out[:, a:a + k] = pp(arr, k)
    put("n1g", inp["norm1_g"][l])
    put("n2g", inp["norm2_g"][l])
    put("modb", inp["mod_b"][l])
    put("qg", inp["mla_q_norm_g"][l])
    put("kvg", inp["mla_kv_norm_g"][l])
    for d in range(2):
        mu = inp["rwkv_mu"][l, d]
        mup = np.zeros(3 * BW + 512, np.float32)
        for j in range(3):
            mup[j * BW:(j + 1) * BW] = mu[j * BWG + ch]
        mup[3 * BW:3 * BW + 256] = mu[3 * BWG:3 * BWG + 256]
        mup[3 * BW + 256:3 * BW + 256 + 160] = mu[3 * BWG + 256:]
        put("mu%d" % d, mup)
        put("w0_%d" % d, inp["rwkv_w0"][l, d][ch])
        put("a0_%d" % d, inp["rwkv_a0"][l, d][ch])
    put("kk", inp["rwkv_k_k"][l][ch])
    put("ka", inp["rwkv_k_a"][l][ch])
    put("rk", inp["rwkv_r_k"][l].reshape(-1)[ch])
    for j in range(3):
        put("cw%d" % j, inp["conv_w"][l, j][ch])
    for i in range(4):
        put("gb_%d" % i, inp["gate_b"][l, i])
    return out


def rope_tables(cfg):
    T, C, GW = cfg.T, cfg.C, cfg.GRID_W
    rows = T // GW
    row = np.repeat(np.arange(rows, dtype=np.float32), GW)
    colv = np.tile(np.arange(GW, dtype=np.float32), rows)
    nf = 16
    inv = (10000.0 ** (-np.arange(nf, dtype=np.float32) / nf)).astype(np.float32)
    ang = np.concatenate([row[:, None] * inv, colv[:, None] * inv], axis=-1)
    cos, sin = np.cos(ang).T.astype(np.float32), np.sin(ang).T.astype(np.float32)
    cos64 = np.concatenate([cos, cos], 0)
    sin64 = np.concatenate([-sin, sin], 0)
    c = np.ones((128, cfg.NT), np.float32)
    s = np.zeros((128, cfg.NT), np.float32)
    c[:, C:] = np.concatenate([cos64, cos64], 0)
    s[:, C:] = np.concatenate([sin64, sin64], 0)
    return c, s


def host_consts(cfg):
    G = cfg.BC
    mask16 = np.zeros((16, 512), np.float32)
    esel = np.zeros((16, 128), np.float32)
    for h in range(16):
        g, par = h // 2, h % 2
        if g < G:
            mask16[h, g * 64:(g + 1) * 64] = 1.0
        esel[h, par * 64:(par + 1) * 64] = 1.0
    blk1 = np.zeros((128, 128), np.float32)
    blk1[:64, :64] = 1.0
    blk1[64:, 64:] = 1.0
    oh = np.zeros((TC_SCAN, TC_SCAN, 64), np.float32)
    for t in range(TC_SCAN):
        oh[t, t, :] = 1.0
    c, s = rope_tables(cfg)
    return dict(c_ident=np.eye(128, dtype=np.float32), c_blk1=blk1, c_mask16=mask16, c_esel=esel,
                c_onehot=oh.reshape(TC_SCAN, TC_SCAN * 64), c_cos=c, c_sin=s)


def host_inputs(cfg, inp, ncores, batch_of_core, half_of_core=None):
    L, BW = cfg.DEPTH, cfg.BW
    if half_of_core is None:
        half_of_core = [0] * ncores
    consts = host_consts(cfg)
    dl = np.ascontiguousarray(inp["diff_lambda"].reshape(L, 256))
    fng = pp(inp["final_norm_g"])
    pvecs, bcs = {}, {}
    for half in sorted(set(half_of_core)):
        ch = np.arange(half * BW, (half + 1) * BW)
        pvecs[half] = np.concatenate([host_pvec(cfg, inp, l, half) for l in range(L)], 0)
        bc = np.zeros((L * 128, 2 * BW + 128), np.float32)
        for l in range(L):
            bc[l * 128:(l + 1) * 128, 0:BW] = np.tile(inp["rwkv_ln_g"][l][ch][None, :], (128, 1))
            bc[l * 128:(l + 1) * 128, BW:2 * BW] = np.tile(inp["rwkv_ln_b"][l][ch][None, :], (128, 1))
            bc[l * 128:(l + 1) * 128, 2 * BW:] = np.tile(inp["diff_norm_g"][l][None, :], (128, 1))
        bcs[half] = bc
    NG = cfg.NG
    nshard = max(1, ncores // cfg.split)
    sr = [cfg.wrows[g] // nshard for g in range(NG)]
    wsh = [[np.zeros((L * sr[g], WCOLS), np.float32) for g in range(NG)] for _ in range(ncores)]
    for l in range(L):
        for half in sorted(set(half_of_core)):
            flats = host_layer_weights(cfg, inp, l, half)
            members = [c for c in range(ncores) if half_of_core[c] == half]
            for g in range(NG):
                v = flats[g].reshape(sr[g] // GCH, nshard, GCH, WCOLS)
                for j, c in enumerate(members):
                    wsh[c][g][l * sr[g]:(l + 1) * sr[g]] = v[:, j].reshape(sr[g], WCOLS)
            del flats
    maps = []
    for c in range(ncores):
        b = batch_of_core[c]
        xT = np.ascontiguousarray(np.concatenate([inp["ctx"][b], inp["x"][b]], 0).T)
        cond = np.ascontiguousarray(np.stack([inp["c_ctx"], inp["c"][b]], 1))
        d = dict(xT=xT, cond=cond, pvec=pvecs[half_of_core[c]], bcin=bcs[half_of_core[c]], dlin=dl, fng=fng)
        for g in range(NG):
            d["wflat%d" % g] = wsh[c][g]
        d.update(consts)
        maps.append(d)
    return maps


from concourse.bass_utils import run_bass_kernel_spmd


def kernel(**inputs):
    inputs = {k: np.asarray(v) for k, v in inputs.items()}
    B, T, D = inputs["x"].shape
    C = inputs["ctx"].shape[1]
    L = inputs["w_in"].shape[0]
    ncores = 8
    cfg = Cfg(D=D, T=T, C=C, DEPTH=L, ncores=ncores, split=2)
    nc, P, _ = build_program(cfg)
    batch_of_core = [c % B for c in range(ncores)]
    half_of_core = [c // B for c in range(ncores)]
    maps = host_inputs(cfg, inputs, ncores, batch_of_core, half_of_core)
    res = run_bass_kernel_spmd(nc, maps, core_ids=list(range(ncores)))
    first = {}
    for c in range(ncores):
        first.setdefault(batch_of_core[c], c)
    out = np.stack([np.ascontiguousarray(res.results[first[b]]["yT"].T) for b in range(B)], 0)
    return out.astype(np.float32)
```

```python
import numpy as np
from contextlib import ExitStack
import concourse.bass as bass
import concourse.mybir as mybir

F32 = mybir.dt.float32
BF16 = mybir.dt.bfloat16
ALU = mybir.AluOpType
ACT = mybir.ActivationFunctionType
AX = mybir.AxisListType
ENGS = ("tensor", "vector", "scalar", "gpsimd", "sync")


class Res:
    def __init__(self, name, t):
        self.name = name
        self.t = t
        self.w = {}
        self.r = {}
        self.sem = None

    def __getitem__(self, k):
        return self.t[k]


class Prog:
    def __init__(self, nc):
        self.nc = nc
        self.es = ExitStack()
        self.semes = ExitStack()
        self.q = {e: [] for e in ENGS}
        self.cnt = {}
        self.sems = {}
        self.known = {e: {} for e in ENGS}
        for e in ENGS:
            self._mksem("E_" + e)
        self.nres = 0
        self.scopes = []
        self.depth = 0
        self.bg = []
        self.bg_every = 10**9
        self.bg_count = 0
        self.bg_on = True
        self.bg_busy = False
        self.free_keys = []
        self.scope_keys = []
        self.ninstr = 0

    def _mksem(self, key):
        self.sems[key] = self.semes.enter_context(self.nc.semaphore(key))
        self.cnt[key] = 0
        return key

    def sb(self, name, shape, dt=F32):
        self.nres += 1
        name = "%s_%d" % (name, self.nres)
        t = self.es.enter_context(self.nc.sbuf_tensor(name, list(shape), dt))
        r = Res(name, t)
        r.scoped = self.depth > 0
        return r

    def ps(self, name, shape, dt=F32):
        self.nres += 1
        name = "%s_%d" % (name, self.nres)
        t = self.es.enter_context(self.nc.psum_tensor(name, list(shape), dt))
        r = Res(name, t)
        r.scoped = self.depth > 0
        return r

    def dram(self, name, shape, dt=F32, kind="Internal"):
        t = self.nc.dram_tensor(name, list(shape), dt, kind=kind)
        return Res(name, t)

    def _deps(self, eng, reads, writes, own):
        need = {}
        own_raw = 0
        for r in reads:
            for k, v in r.w.items():
                need[k] = max(need.get(k, 0), v)
                if k == own:
                    own_raw = max(own_raw, v)
        for w in writes:
            for k, v in w.w.items():
                need[k] = max(need.get(k, 0), v)
                if k == own:
                    own_raw = max(own_raw, v)
            for k, v in w.r.items():
                need[k] = max(need.get(k, 0), v)
        kn = self.known[eng]
        out = []
        for k, v in need.items():
            if k == own:
                if eng in ("vector", "scalar", "gpsimd") and own is not None and own.startswith("E_") \
                        and own_raw > kn.get(k, 0):
                    kn[k] = own_raw
                    out.append((k, own_raw))
                continue
            if kn.get(k, 0) >= v:
                continue
            kn[k] = v
            out.append((k, v))
        return out

    def _commit(self, reads, writes, key, val):
        for r in reads:
            r.r[key] = max(r.r.get(key, 0), val)
        for w in writes:
            w.w = {key: val}
            w.r = {}

    def op(self, eng, fn, reads=(), writes=()):
        key = "E_" + eng
        waits = self._deps(eng, reads, writes, key)
        self.cnt[key] += 1
        val = self.cnt[key]
        sems = self.sems
        sem = sems[key]

        e = getattr(self.nc, eng)
        for k, v in waits:
            e.wait_ge(sems[k], v)
        fn(e).then_inc(sem, 1)
        self._commit(reads, writes, key, val)
        self.ninstr += 1

    def pump(self, n=1):
        if self.bg_busy:
            return
        self.bg_busy = True
        while n > 0 and self.bg:
            self.bg.pop(0)[1]()
            n -= 1
        self.bg_busy = False

    def flush_bg(self, tag):
        self.bg_busy = True
        while self.bg and self.bg[0][0] <= tag:
            self.bg.pop(0)[1]()
        self.bg_busy = False

    def dma(self, eng, out_res, out_ap, in_res, in_ap, **kw):
        if eng == "gpsimd" and self.bg and self.bg_on and not self.bg_busy:
            self.bg_count += 1
            if self.bg_count % self.bg_every == 0:
                self.pump(1)
        if out_res.sem is None:
            if getattr(out_res, "scoped", False):
                if self.free_keys:
                    out_res.sem = self.free_keys.pop()
                else:
                    out_res.sem = self._mksem("DS_%d" % len(self.sems))
                self.scope_keys[-1].append(out_res.sem)
            else:
                out_res.sem = self._mksem("D_%d_%s" % (len(self.sems), out_res.name))
        key = out_res.sem
        waits = self._deps(eng, [in_res], [out_res], key)
        self.cnt[key] += 16
        val = self.cnt[key]
        sems = self.sems
        sem = sems[key]

        e = getattr(self.nc, eng)
        for k, v in waits:
            e.wait_ge(sems[k], v)
        e.dma_start(out=out_ap, in_=in_ap, **kw).then_inc(sem, 16)
        in_res.r[key] = max(in_res.r.get(key, 0), val)
        w = dict(out_res.w)
        w[key] = val
        out_res.w = {key: val}
        out_res.r = {}
        self.ninstr += 1

    def collective(self, kind, op, groups, in_res, in_ap, out_res, out_ap):
        if out_res.sem is None:
            out_res.sem = self._mksem("C_%d_%s" % (len(self.sems), out_res.name))
        key = out_res.sem
        waits = self._deps("gpsimd", [in_res], [out_res], key)
        e = self.nc.gpsimd
        for k, v in waits:
            e.wait_ge(self.sems[k], v)
        self.cnt[key] += 1
        val = self.cnt[key]
        e.collective_compute(kind, op, replica_groups=groups, ins=[in_ap], outs=[out_ap]).then_inc(self.sems[key], 1)
        in_res.r[key] = max(in_res.r.get(key, 0), val)
        out_res.w = {key: val}
        out_res.r = {}
        self.ninstr += 1

    def wait_all(self, eng, resources):
        waits = self._deps(eng, resources, [], None)
        sems = self.sems

        e = getattr(self.nc, eng)
        for k, v in waits:
            e.wait_ge(sems[k], v)

    def barrier(self):
        snap = dict(self.cnt)
        sems = self.sems
        for eng in ENGS:
            kn = self.known[eng]
            waits = [(k, v) for k, v in snap.items() if v > 0 and k != "E_" + eng and kn.get(k, 0) < v]
            for k, v in waits:
                kn[k] = v

            e = getattr(self.nc, eng)
            for k, v in waits:
                e.wait_ge(sems[k], v)

    def scope(self):
        prog = self

        class _S:
            def __enter__(s):
                prog.barrier()
                s.old = prog.es
                prog.es = ExitStack()
                prog.depth += 1
                prog.scope_keys.append([])
                return s

            def __exit__(s, *a):
                prog.barrier()
                prog.es.close()
                prog.es = s.old
                prog.depth -= 1
                prog.free_keys.extend(prog.scope_keys.pop())
                return False
        return _S()

    def finish(self, outs):
        for eng in ("sync", "gpsimd", "vector", "scalar", "tensor"):
            self.wait_all(eng, outs)
        self.barrier()
        self.es.close()


class Pool:
    def __init__(self, items):
        self.items = items
        self.i = 0

    def get(self):
        r = self.items[self.i % len(self.items)]
        self.i += 1
        return r

import math
import numpy as np
import concourse.bass as bass


NORM_EPS = 1e-6
GN_EPS = 64e-5
WCOLS = 1024
GCH = 512


def cdiv(a, b):
    return (a + b - 1) // b


def pad128(n):
    return cdiv(n, 128) * 128


class Cfg:
    def __init__(self, D=4096, T=2048, C=256, DEPTH=4, GRID_W=64, ncores=8, split=1):
        self.D, self.T, self.C, self.DEPTH, self.GRID_W, self.ncores = D, T, C, DEPTH, GRID_W, ncores
        self.NT = C + T
        self.split = split
        self.BWG = D // 4
        self.BW = self.BWG // split
        self.H_MLA = self.BW // 128
        self.H_RW = self.BW // 64
        self.H_DF = self.BW // 128
        self.QL = 3 * D // 16
        self.KVL = D // 16
        self.DFF = 4 * D
        self.DFFL = self.DFF // split
        self.GR = 256
        self.MR = 256
        self.DC = D // 128
        self.BC = self.BW // 128
        BW = self.BW
        segs = [("cq", pad128(self.QL)), ("ckv", pad128(self.KVL)), ("kr", 128),
                ("r", BW), ("k", BW), ("v", BW), ("wd", 128), ("ad", 128), ("gd", 256),
                ("cb", BW), ("cc", BW), ("cu", BW), ("dq", BW), ("dk", BW), ("dv", BW),
                ("dqr", BW), ("dkr", BW)]
        self.seg = {}
        o = 0
        for n, s in segs:
            self.seg[n] = (o, s)
            o += s
        self.PCOLS = o
        self.tiles = []
        for a, b in ((0, C), (C, C + T)):
            n = a
            while n < b:
                s = min(512, b - n)
                self.tiles.append((n, s))
                n += s
        self.segs_tok = ((0, C), (C, C + T))
        QLp, KVLp = pad128(self.QL), pad128(self.KVL)
        self.wshapes = [
            ("mod_down", D, self.MR), ("mod_up", self.MR, 6 * D), ("w_in", D, self.PCOLS),
            ("w_uq", QLp, self.H_MLA * 256), ("w_ukv", KVLp, self.H_MLA * 256),
            ("w2_0", 64, BW), ("w2_1", 64, BW), ("a2_0", 64, BW), ("a2_1", 64, BW), ("g2", 256, BW),
            ("wb_0", self.BWG, D), ("wb_1", self.BWG, D), ("wb_2", self.BWG, D), ("wb_3", self.BWG, D),
            ("gate_down", D, self.GR), ("gu_0", self.GR, D), ("gu_1", self.GR, D), ("gu_2", self.GR, D),
            ("gu_3", self.GR, D), ("w_out", D, D), ("mlp_w1", D, self.DFFL), ("mlp_w2", self.DFFL, D),
        ]
        self.wgroup_of = {}
        for n, K, M in self.wshapes:
            self.wgroup_of[n] = 1 if n == "mlp_w1" else (2 if n == "mlp_w2" else 0)
        self.NG = 3
        self.woff = {}
        o = [0] * self.NG
        for n, K, M in self.wshapes:
            g = self.wgroup_of[n]
            self.woff[n] = (o[g], K, M)
            o[g] += pad128(K) * pad128(M)
        self.wrows = [cdiv(cdiv(o[g], WCOLS), 8 * GCH) * 8 * GCH for g in range(self.NG)]

    def cond_of(self, n0):
        return 0 if n0 < self.C else 1


class Model:
    def __init__(self, P, cfg):
        self.P = P
        self.cfg = cfg
        self.nc = P.nc
        self.rr = 0

    def wview(self, wfull, name, mc):
        off, K, M = self.cfg.woff[name]
        kw = pad128(K)
        o = off + mc * 128 * kw
        return bass.AP(wfull[self.cfg.wgroup_of[name]].t, o, [[kw, 128], [1, kw]])

    def evac(self, eng, out_ap, in_ap, reads, writes, func=None, bias=None, scale=1.0):
        P = self.P
        if eng == "scalar":
            f = func if func is not None else ACT.Copy
            if bias is None:
                P.op("scalar", lambda e: e.activation(out_ap, in_ap, f, scale=scale), reads=reads, writes=writes)
            else:
                P.op("scalar", lambda e: e.activation(out_ap, in_ap, f, bias=bias, scale=scale),
                     reads=reads, writes=writes)
        else:
            P.op("vector", lambda e: e.tensor_copy(out_ap, in_ap), reads=reads, writes=writes)

    def alt(self):
        self.rr += 1
        return "scalar" if self.rr % 2 else "vector"

    def alloc_linear_bufs(self):
        P = self.P
        self.xpool = Pool([P.sb("xb%d" % i, [128, 32768], BF16) for i in range(1)])
        wmax = max(pad128(K) for _, K, _ in self.cfg.wshapes)
        nwb = max(2, 32768 // wmax)
        self.wpool = Pool([P.sb("wb%d" % i, [128, wmax], BF16) for i in range(nwb)])
        self.pspool = Pool([P.ps("ps%d" % i, [128, 512], F32) for i in range(6)])
        self.opool = Pool([P.sb("ob%d" % i, [128, 512], F32) for i in range(4)])
        self.opoolb = Pool([P.sb("obb%d" % i, [128, 512], BF16) for i in range(4)])
        self.accpool = Pool([P.sb("acc%d" % i, [128, 512], F32) for i in range(2)])

    def linear(self, xT, K, wfull, wname, M, epi, tiles=None, nmax=512, row0=0):
        P = self.P
        KC = cdiv(K, 128)
        MC = cdiv(M, 128)
        if tiles is None:
            tiles = self.cfg.tiles
        XCAP = 32768
        cap = max(128, min(XCAP // KC, 4 * 512) // 128 * 128)
        sub = []
        for n0, ns in tiles:
            o = 0
            while o < ns:
                s = min(nmax, cap, ns - o)
                sub.append((n0 + o, s))
                o += s
        groups, curg, tot = [], [], 0
        for st_ in sub:
            if curg and (tot + st_[1] > cap or len(curg) >= 3):
                groups.append(curg)
                curg, tot = [], 0
            curg.append(st_)
            tot += st_[1]
        if curg:
            groups.append(curg)
        kfull = K // 128
        krem = K - kfull * 128
        for grp in groups:
            xt = self.xpool.get()
            flat = xt.t
            bases = []
            b0 = 0
            for n0, nsz in grp:
                bases.append(b0)
                if kfull:
                    P.dma("sync", xt, flat[:, b0:b0 + kfull * nsz].rearrange("p (kc n) -> p kc n", n=nsz),
                          xT, xT.t[row0:row0 + kfull * 128, n0:n0 + nsz].rearrange("(kc p) n -> p kc n", p=128))
                if krem:
                    P.dma("sync", xt, flat[0:krem, b0 + kfull * nsz:b0 + (kfull + 1) * nsz],
                          xT, xT.t[row0 + kfull * 128:row0 + K, n0:n0 + nsz])
                b0 += KC * nsz
            for mc in range(MC):
                m0 = mc * 128
                msz = min(128, M - m0)
                wt = self.wpool.get()
                P.dma("sync", wt, wt.t[:, 0:KC * 128], wfull[self.cfg.wgroup_of[wname]], self.wview(wfull, wname, mc)[:, 0:KC * 128])
                for (n0, nsz), b0 in zip(grp, bases):
                    ps = self.pspool.get()
                    for kc in range(KC):
                        ksz = min(128, K - kc * 128)
                        P.op("tensor", lambda e, kc=kc, ksz=ksz, ps=ps, b0=b0, nsz=nsz: e.matmul(
                            ps.t[0:msz, 0:nsz], wt.t[0:ksz, kc * 128:kc * 128 + msz],
                            flat[0:ksz, b0 + kc * nsz:b0 + (kc + 1) * nsz], start=(kc == 0), stop=(kc == KC - 1)),
                            reads=[wt, xt], writes=[ps])
                    epi(ps, m0, msz, n0, nsz)

    def epi_store(self, dst32=None, dst16=None, func=None, bias_fn=None, row0=0, square=False, mul=None):
        P = self.P

        def epi(ps, m0, msz, n0, nsz):
            eng = "scalar" if (func is not None or bias_fn is not None) else self.alt()
            bias = bias_fn(m0, msz, n0) if bias_fn is not None else None
            if dst32 is not None:
                o = self.opool.get()
                self.evac(eng, o.t[0:msz, 0:nsz], ps.t[0:msz, 0:nsz], [ps], [o], func=func, bias=bias)
                if square:
                    P.op("vector", lambda e: e.tensor_tensor(o.t[0:msz, 0:nsz], o.t[0:msz, 0:nsz],
                                                             o.t[0:msz, 0:nsz], ALU.mult), reads=[o], writes=[o])
                if mul is not None:
                    P.op("vector", lambda e: e.tensor_scalar(o.t[0:msz, 0:nsz], o.t[0:msz, 0:nsz], mul, None, ALU.mult),
                         reads=[o], writes=[o])
                P.dma("gpsimd", dst32, dst32.t[row0 + m0:row0 + m0 + msz, n0:n0 + nsz], o, o.t[0:msz, 0:nsz])
                if dst16 is not None:
                    ob = self.opoolb.get()
                    P.op("vector", lambda e: e.tensor_copy(ob.t[0:msz, 0:nsz], o.t[0:msz, 0:nsz]),
                         reads=[o], writes=[ob])
                    P.dma("gpsimd", dst16, dst16.t[row0 + m0:row0 + m0 + msz, n0:n0 + nsz], ob, ob.t[0:msz, 0:nsz])
            else:
                ob = self.opoolb.get()
                if square:
                    o = self.opool.get()
                    self.evac(eng, o.t[0:msz, 0:nsz], ps.t[0:msz, 0:nsz], [ps], [o], func=func, bias=bias)
                    P.op("vector", lambda e: e.tensor_tensor(ob.t[0:msz, 0:nsz], o.t[0:msz, 0:nsz],
                                                             o.t[0:msz, 0:nsz], ALU.mult), reads=[o], writes=[ob])
                else:
                    self.evac(eng, ob.t[0:msz, 0:nsz], ps.t[0:msz, 0:nsz], [ps], [ob], func=func, bias=bias)
                P.dma("gpsimd", dst16, dst16.t[row0 + m0:row0 + m0 + msz, n0:n0 + nsz], ob, ob.t[0:msz, 0:nsz])
        return epi

    def rmsnorm(self, src, srow0, F, dst16, drow0, A_fn, B_fn=None, deps=(), ones=None, tiles=None, dcol=0):
        P = self.P
        cfg = self.cfg
        FC = cdiv(F, 128)
        nmax = max(128, min(512, (8192 // FC) // 128 * 128))
        sub = []
        for a, ns in (tiles if tiles is not None else cfg.tiles):
            o = 0
            while o < ns:
                s_ = min(nmax, ns - o)
                sub.append((a + o, s_))
                o += s_
        for n0, nsz in sub:
            ci = cfg.cond_of(n0)
            xt = self.nx.get()
            v = xt.t[:, 0:FC * nsz].rearrange("p (c n) -> p c n", n=nsz)
            P.dma("sync", xt, v, src, src.t[srow0:srow0 + FC * 128, n0:n0 + nsz].rearrange("(c p) n -> p c n", p=128))
            sq = self.nsq.get()
            sv = sq.t[:, 0:FC * nsz].rearrange("p (c n) -> p c n", n=nsz)
            P.op("scalar", lambda e: e.activation(sv, v, ACT.Square), reads=[xt], writes=[sq])
            ps = self.pspool.get()
            for c in range(FC):
                P.op("tensor", lambda e, c=c: e.matmul(ps.t[:, 0:nsz], ones.t[:, :], sq.t[:, c * nsz:(c + 1) * nsz],
                                                       start=(c == 0), stop=(c == FC - 1)), reads=[ones, sq], writes=[ps])
            rs = self.nrs.get()
            P.op("scalar", lambda e: e.activation(rs.t[:, 0:nsz], ps.t[:, 0:nsz], ACT.Sqrt, bias=self.eps_t.t[:, 0:1],
                                                  scale=1.0 / F), reads=[ps, self.eps_t], writes=[rs])
            P.op("vector", lambda e: e.reciprocal(rs.t[:, 0:nsz], rs.t[:, 0:nsz]), reads=[rs], writes=[rs])
            ot = self.no.get()
            ov = ot.t[:, 0:FC * nsz].rearrange("p (c n) -> p c n", n=nsz)
            P.op("vector", lambda e: e.tensor_tensor(v, v, rs.t[:, 0:nsz].unsqueeze(1).broadcast_to([128, FC, nsz]),
                                                     ALU.mult), reads=[xt, rs], writes=[xt])
            for c in range(FC):
                a_ap = A_fn(c, ci)
                if B_fn is not None:
                    b_ap = B_fn(c, ci)
                    P.op("scalar", lambda e, c=c, a_ap=a_ap, b_ap=b_ap: e.activation(
                        ov[:, c, :], v[:, c, :], ACT.Identity, bias=b_ap, scale=a_ap),
                        reads=[xt] + list(deps), writes=[ot])
                else:
                    P.op("vector", lambda e, c=c, a_ap=a_ap: e.tensor_scalar(
                        ov[:, c, :], v[:, c, :], a_ap, None, ALU.mult), reads=[xt] + list(deps), writes=[ot])
            P.dma("gpsimd", dst16, dst16.t[drow0:drow0 + FC * 128, n0 - dcol:n0 - dcol + nsz].rearrange("(c p) n -> p c n", p=128),
                  ot, ov)

    def alloc_norm_bufs(self, out_dt=BF16):
        P = self.P
        self.nx = Pool([P.sb("nx%d" % i, [128, 8192], F32) for i in range(2 if out_dt == BF16 else 1)])
        self.nsq = Pool([P.sb("nsq%d" % i, [128, 8192], F32) for i in range(1)])
        self.nrs = Pool([P.sb("nrs%d" % i, [128, 512], F32) for i in range(2)])
        self.no = Pool([P.sb("no%d" % i, [128, 8192], out_dt) for i in range(2 if out_dt == BF16 else 1)])

    def ld(self, pool, src, r0, nrows, n0, nsz, q="sync"):
        t = pool.get()
        self.P.dma(q, t, t.t[0:nrows, 0:nsz], src, src.t[r0:r0 + nrows, n0:n0 + nsz])
        return t

    def st(self, dst, r0, nrows, n0, nsz, t, q="gpsimd"):
        self.P.dma(q, dst, dst.t[r0:r0 + nrows, n0:n0 + nsz], t, t.t[0:nrows, 0:nsz])

    def rope(self, src, srow, rrow, nrows_total, dst16, drow):
        P = self.P
        cfg = self.cfg
        done = 0
        while done < nrows_total:
            nr = min(128, nrows_total - done)
            for n0, nsz in cfg.tiles:
                a = self.ld(self.epool, src, srow + done, nr, n0, nsz)
                b = self.ld(self.epool, src, rrow + done, nr, n0, nsz)
                P.op("vector", lambda e: e.tensor_tensor(a.t[0:nr, 0:nsz], a.t[0:nr, 0:nsz],
                                                         self.cos2.t[0:nr, n0:n0 + nsz], ALU.mult),
                     reads=[a, self.cos2], writes=[a])
                P.op("gpsimd", lambda e: e.tensor_tensor(b.t[0:nr, 0:nsz], b.t[0:nr, 0:nsz],
                                                         self.sin2.t[0:nr, n0:n0 + nsz], ALU.mult),
                     reads=[b, self.sin2], writes=[b])
                o = self.epoolb.get()
                P.op("vector", lambda e: e.tensor_tensor(o.t[0:nr, 0:nsz], a.t[0:nr, 0:nsz], b.t[0:nr, 0:nsz], ALU.add),
                     reads=[a, b], writes=[o])
                self.st(dst16, drow + done, nr, n0, nsz, o)
            done += nr

    def cast_rows(self, src, srow, nrows_total, dst16, drow):
        P = self.P
        done = 0
        while done < nrows_total:
            nr = min(128, nrows_total - done)
            for n0, nsz in self.cfg.tiles:
                a = self.ld(self.epool, src, srow + done, nr, n0, nsz)
                o = self.epoolb.get()
                P.op("vector", lambda e: e.tensor_copy(o.t[0:nr, 0:nsz], a.t[0:nr, 0:nsz]), reads=[a], writes=[o])
                self.st(dst16, drow + done, nr, n0, nsz, o)
            done += nr

    def alloc_ew_bufs(self):
        P = self.P
        self.epool = Pool([P.sb("ew%d" % i, [128, 512], F32) for i in range(6)])
        self.epoolb = Pool([P.sb("ewb%d" % i, [128, 512], BF16) for i in range(3)])

    def alloc_attn_bufs(self):
        P = self.P
        NT = self.cfg.NT
        NK = NT // 128
        self.a_q = [P.sb("a_q%d" % i, [128, NT], BF16) for i in range(2)]
        self.a_k = [P.sb("a_k%d" % i, [128, NT], BF16) for i in range(2)]
        self.a_vf = P.sb("a_vf", [128, NT], BF16)
        self.a_vt = P.sb("a_vt", [128, NK, 128], BF16)
        self.a_pb = P.sb("a_pb", [128, NT], BF16)
        self.a_pt = P.sb("a_pt", [128, NK, 128], BF16)
        self.a_y = P.sb("a_y", [128, NT], BF16)
        self.a_sc = P.ps("a_sc", [128, 512 * cdiv(NT, 512)], F32)
        self.a_pT = P.ps("a_pT", [128, 512], F32)
        self.a_pO = [P.ps("a_pO%d" % i, [128, 512], F32) for i in range(2)]
        self.a_small = Pool([P.sb("a_sm%d" % i, [128, 8], F32) for i in range(4)])
        self.a_o = Pool([P.sb("a_o%d" % i, [128, 128], F32) for i in range(3)])
        self.a_ob = Pool([P.sb("a_ob%d" % i, [128, 128], BF16) for i in range(2)])

    def v_to_tokmajor(self):
        P = self.P
        NK = self.cfg.NT // 128
        for c0 in range(0, NK, 4):
            nb = min(4, NK - c0)
            for j in range(nb):
                c = c0 + j
                P.op("tensor", lambda e, c=c, j=j: e.matmul(self.a_pT.t[:, j * 128:(j + 1) * 128],
                                                            self.a_vf.t[:, c * 128:(c + 1) * 128], self.identb.t[:, :],
                                                            start=True, stop=True),
                     reads=[self.a_vf, self.identb], writes=[self.a_pT])
            self.evac(self.alt(), self.a_vt.t[:, c0:c0 + nb, :].rearrange("p c d -> p (c d)"),
                      self.a_pT.t[:, 0:nb * 128], [self.a_pT], [self.a_vt])

    def softmax_pv(self, qparts, kparts, q0, nk, scale, pO):
        P = self.P
        sc = self.a_sc
        npart = len(qparts)
        for k0 in range(0, nk, 512):
            ks = min(512, nk - k0)
            for i, ((qr, qa, qb), (kr, ka, kb)) in enumerate(zip(qparts, kparts)):
                P.op("tensor", lambda e, qr=qr, qa=qa, qb=qb, kr=kr, ka=ka, kb=kb, i=i: e.matmul(
                    sc.t[:, k0:k0 + ks], qr.t[qa:qb, q0:q0 + 128], kr.t[ka:kb, k0:k0 + ks],
                    start=(i == 0), stop=(i == npart - 1)), reads=[qr, kr], writes=[sc])
        sm = self.a_small.get()
        P.op("vector", lambda e: e.reduce_max(sm.t[:, 0:1], sc.t[:, 0:nk], AX.X), reads=[sc], writes=[sm])
        P.op("vector", lambda e: e.tensor_scalar(sm.t[:, 1:2], sm.t[:, 0:1], -scale, None, ALU.mult), reads=[sm], writes=[sm])
        P.op("scalar", lambda e: e.activation(self.a_pb.t[:, 0:nk], sc.t[:, 0:nk], ACT.Exp, bias=sm.t[:, 1:2],
                                              scale=scale, accum_out=sm.t[:, 2:3]), reads=[sc, sm], writes=[self.a_pb, sm])
        P.op("vector", lambda e: e.reciprocal(sm.t[:, 3:4], sm.t[:, 2:3]), reads=[sm], writes=[sm])
        nc_ = nk // 128
        for c0 in range(0, nc_, 4):
            nb = min(4, nc_ - c0)
            for j in range(nb):
                c = c0 + j
                P.op("tensor", lambda e, c=c, j=j: e.matmul(self.a_pT.t[:, j * 128:(j + 1) * 128],
                                                            self.a_pb.t[:, c * 128:(c + 1) * 128], self.identb.t[:, :],
                                                            start=True, stop=True),
                     reads=[self.a_pb, self.identb], writes=[self.a_pT])
            self.evac(self.alt(), self.a_pt.t[:, c0:c0 + nb, :].rearrange("p c d -> p (c d)"),
                      self.a_pT.t[:, 0:nb * 128], [self.a_pT], [self.a_pt])
        for c in range(nc_):
            P.op("tensor", lambda e, c=c: e.matmul(pO.t[:, 0:128], self.a_pt.t[:, c, :], self.a_vt.t[:, c, :],
                                                   start=(c == 0), stop=(c == nc_ - 1)),
                 reads=[self.a_pt, self.a_vt], writes=[pO])
        return sm

    def out_transpose(self, ob, q0):
        P = self.P
        P.op("tensor", lambda e: e.matmul(self.a_pT.t[:, 0:128], ob.t[:, :], self.identb.t[:, :], start=True, stop=True),
             reads=[ob, self.identb], writes=[self.a_pT])
        self.evac(self.alt(), self.a_y.t[:, q0:q0 + 128], self.a_pT.t[:, 0:128], [self.a_pT], [self.a_y])

    def load_rows(self, t, src, r0, nr, p0=0):
        NT = self.cfg.NT
        self.P.dma("sync", t, t.t[p0:p0 + nr, 0:NT], src, src.t[r0:r0 + nr, 0:NT])

    def mla_attention(self, QN, QR, KN, KR, VM, Y, yrow0):
        P = self.P
        cfg = self.cfg
        scale = (128 + 64) ** -0.5
        self.load_rows(self.a_k[1], KR, 0, 64)
        for h in range(cfg.H_MLA):
            self.load_rows(self.a_q[0], QN, h * 128, 128)
            self.load_rows(self.a_q[1], QR, h * 64, 64)
            self.load_rows(self.a_k[0], KN, h * 128, 128)
            self.load_rows(self.a_vf, VM, h * 128, 128)
            self.v_to_tokmajor()
            for q0 in range(0, cfg.NT, 128):
                nk = cfg.C if q0 < cfg.C else cfg.NT
                sm = self.softmax_pv([(self.a_q[0], 0, 128), (self.a_q[1], 0, 64)],
                                     [(self.a_k[0], 0, 128), (self.a_k[1], 0, 64)], q0, nk, scale, self.a_pO[0])
                ob = self.a_ob.get()
                P.op("scalar", lambda e: e.activation(ob.t[:, :], self.a_pO[0].t[:, 0:128], ACT.Copy, scale=sm.t[:, 3:4]),
                     reads=[self.a_pO[0], sm], writes=[ob])
                self.out_transpose(ob, q0)
            P.dma("gpsimd", Y, Y.t[yrow0 + h * 128:yrow0 + (h + 1) * 128, 0:cfg.NT], self.a_y, self.a_y.t[:, 0:cfg.NT])

    def diff_attention(self, DQ, DK, DV, Y, yrow0, lam_col, gtile, lam_init):
        P = self.P
        cfg = self.cfg
        scale = 64 ** -0.5
        for h in range(cfg.H_DF):
            self.load_rows(self.a_q[0], DQ, h * 128, 128)
            self.load_rows(self.a_k[0], DK, h * 128, 128)
            self.load_rows(self.a_vf, DV, h * 128, 128)
            self.v_to_tokmajor()
            for q0 in range(0, cfg.NT, 128):
                nk = cfg.C if q0 < cfg.C else cfg.NT
                sm1 = self.softmax_pv([(self.a_q[0], 0, 64)], [(self.a_k[0], 0, 64)], q0, nk, scale, self.a_pO[0])
                sm2 = self.softmax_pv([(self.a_q[0], 64, 128)], [(self.a_k[0], 64, 128)], q0, nk, scale, self.a_pO[1])
                o1 = self.a_o.get()
                P.op("scalar", lambda e: e.activation(o1.t[:, :], self.a_pO[0].t[:, 0:128], ACT.Copy, scale=sm1.t[:, 3:4]),
                     reads=[self.a_pO[0], sm1], writes=[o1])
                P.op("vector", lambda e: e.tensor_tensor(sm2.t[:, 4:5], sm2.t[:, 3:4], lam_col.t[:, 0:1], ALU.mult),
                     reads=[sm2, lam_col], writes=[sm2])
                o = self.a_o.get()
                P.op("vector", lambda e: e.scalar_tensor_tensor(o.t[:, :], self.a_pO[1].t[:, 0:128], sm2.t[:, 4:5],
                                                                o1.t[:, :], ALU.mult, ALU.add),
                     reads=[self.a_pO[1], sm2, o1], writes=[o])
                sq = self.a_o.get()
                P.op("vector", lambda e: e.tensor_tensor(sq.t[:, :], o.t[:, :], o.t[:, :], ALU.mult), reads=[o], writes=[sq])
                P.op("vector", lambda e: e.reduce_sum(sm2.t[:, 5:6], sq.t[:, :], AX.X), reads=[sq], writes=[sm2])
                P.op("scalar", lambda e: e.activation(sm2.t[:, 6:7], sm2.t[:, 5:6], ACT.Sqrt, bias=self.eps_t.t[:, 0:1],
                                                      scale=1.0 / 128), reads=[sm2, self.eps_t], writes=[sm2])
                P.op("vector", lambda e: e.reciprocal(sm2.t[:, 6:7], sm2.t[:, 6:7]), reads=[sm2], writes=[sm2])
                ob = self.a_ob.get()
                if 0 == 1:
                    P.op("vector", lambda e: e.tensor_copy(ob.t[:, :], o.t[:, :]), reads=[o], writes=[ob])
                elif 0 == 3:
                    P.op("vector", lambda e: e.tensor_scalar(ob.t[:, :], o.t[:, :], sm2.t[:, 6:7], None, ALU.mult), reads=[o, sm2], writes=[ob])
                elif 0 == 4:
                    P.op("vector", lambda e: e.tensor_tensor(ob.t[:, :], o.t[:, :], gtile.t[:, :], ALU.mult), reads=[o, gtile], writes=[ob])
                elif 0 == 5:
                    P.op("scalar", lambda e: e.activation(ob.t[:, :], self.a_pO[1].t[:, 0:128], ACT.Copy, scale=sm2.t[:, 3:4]),
                         reads=[self.a_pO[1], sm2], writes=[ob])
                elif 0 == 6:
                    P.op("scalar", lambda e: e.activation(ob.t[:, :], self.a_pO[1].t[:, 0:128], ACT.Copy, scale=sm2.t[:, 4:5]),
                         reads=[self.a_pO[1], sm2], writes=[ob])
                elif 0 == 7:
                    P.op("vector", lambda e: e.tensor_copy(ob.t[:, 0:2], lam_col.t[:, 0:2]), reads=[lam_col], writes=[ob])
                    P.op("vector", lambda e: e.tensor_copy(ob.t[:, 2:8], sm2.t[:, 2:8]), reads=[sm2], writes=[ob])
                elif 0 == 2:
                    P.op("vector", lambda e: e.tensor_copy(ob.t[:, :], o1.t[:, :]), reads=[o1], writes=[ob])
                else:
                    P.op("vector", lambda e: e.scalar_tensor_tensor(ob.t[:, :], o.t[:, :], sm2.t[:, 6:7], gtile.t[:, :],
                                                                    ALU.mult, ALU.mult), reads=[o, sm2, gtile], writes=[ob])
                self.out_transpose(ob, q0)
            P.dma("gpsimd", Y, Y.t[yrow0 + h * 128:yrow0 + (h + 1) * 128, 0:cfg.NT], self.a_y, self.a_y.t[:, 0:cfg.NT])

    def alloc_pad_bufs(self, nload, nwork):
        P = self.P
        NP = self.cfg.NT + 4
        self.NP = NP
        self.lpool = Pool([P.sb("pl%d" % i, [128, NP], F32) for i in range(nload)])
        for t in self.lpool.items:
            P.op("gpsimd", lambda e, t=t: e.memset(t.t[:, :], 0.0), writes=[t])
        self.wk = Pool([P.sb("pw%d" % i, [128, NP], F32) for i in range(nwork)])
        self.wkb = Pool([P.sb("pwb%d" % i, [128, NP], BF16) for i in range(2)])

    def ldpad(self, src, row0, nr=128):
        cfg = self.cfg
        t = self.lpool.get()
        C, T = cfg.C, cfg.T
        self.P.dma("sync", t, t.t[0:nr, 1:1 + C], src, src.t[row0:row0 + nr, 0:C])
        self.P.dma("sync", t, t.t[0:nr, C + 3:C + 3 + T], src, src.t[row0:row0 + nr, C:C + T])
        return t

    def stpad(self, dst, row0, t, nr=128):
        cfg = self.cfg
        C, T = cfg.C, cfg.T
        self.P.dma("gpsimd", dst, dst.t[row0:row0 + nr, 0:C], t, t.t[0:nr, 1:1 + C])
        self.P.dma("gpsimd", dst, dst.t[row0:row0 + nr, C:C + T], t, t.t[0:nr, C + 3:C + 3 + T])

    def tshift(self, x, mu0, mu1, o):
        P = self.P
        NP = self.NP
        cur, prev, nxt = x.t[:, 1:NP - 1], x.t[:, 0:NP - 2], x.t[:, 2:NP]
        d1 = self.wk.get()
        P.op("vector", lambda e: e.tensor_tensor(d1.t[:, 1:NP - 1], prev, cur, ALU.subtract), reads=[x], writes=[d1])
        d2 = self.wk.get()
        P.op("gpsimd", lambda e: e.tensor_tensor(d2.t[:, 1:NP - 1], nxt, cur, ALU.subtract), reads=[x], writes=[d2])
        P.op("vector", lambda e: e.scalar_tensor_tensor(d1.t[:, 1:NP - 1], d1.t[:, 1:NP - 1], mu0, cur, ALU.mult, ALU.add),
             reads=[d1, x, self.PVt], writes=[d1])
        P.op("vector", lambda e: e.scalar_tensor_tensor(o.t[:, 1:NP - 1], d2.t[:, 1:NP - 1], mu1, d1.t[:, 1:NP - 1],
                                                        ALU.mult, ALU.add), reads=[d1, d2, self.PVt], writes=[o])
        return o

    def blocksum(self, src, dst_fn):
        P = self.P
        NP = self.NP
        for c0 in range(1, NP - 1, 512):
            cs = min(512, NP - 1 - c0)
            ps = self.pspool.get()
            P.op("tensor", lambda e: e.matmul(ps.t[:, 0:cs], self.blk1.t[:, :], src.t[:, c0:c0 + cs], start=True, stop=True),
                 reads=[self.blk1, src], writes=[ps])
            dst_fn(ps, c0, cs)

    def rwkv_prep_a(self, P32, D_):
        P = self.P
        cfg = self.cfg
        NP = self.NP
        BC = cfg.BC
        pv = self.pv
        col = lambda name, c: self.PVt.t[:, pv[name][0] + c:pv[name][0] + c + 1]
        segs = [("wd", 0, ACT.Tanh), ("ad", 0, None), ("gd", 0, ACT.Sigmoid), ("gd", 128, ACT.Sigmoid)]
        for i, (sn, off, fn) in enumerate(segs):
            x = self.ldpad(P32, cfg.seg[sn][0] + off)
            s = self.tshift(x, col("mu0", 3 * BC + i), col("mu1", 3 * BC + i), self.wk.get())
            ob = self.wkb.get()
            if fn is None:
                P.op("vector", lambda e: e.tensor_copy(ob.t[:, 1:NP - 1], s.t[:, 1:NP - 1]), reads=[s], writes=[ob])
            else:
                P.op("scalar", lambda e, fn=fn: e.activation(ob.t[:, 1:NP - 1], s.t[:, 1:NP - 1], fn), reads=[s], writes=[ob])
            self.stpad(D_["LORA16"], i * 128, ob)

    def rwkv_prep_b(self, wfull, D_):
        cfg = self.cfg
        pv = self.pv
        col = lambda name, c: self.PVt.t[:, pv[name][0] + c:pv[name][0] + c + 1]
        for d in range(2):
            self.linear(D_["LORA16"], 64, wfull, "w2_%d" % d, cfg.BW,
                        self.epi_store(dst32=D_["LW%d" % d], func=ACT.Sigmoid,
                                       bias_fn=lambda m0, msz, n0, d=d: col("w0_%d" % d, m0 // 128)[0:msz], mul=-0.6065306597126334),
                        row0=d * 64)
            self.linear(D_["LORA16"], 64, wfull, "a2_%d" % d, cfg.BW,
                        self.epi_store(dst32=D_["A%d" % d], func=ACT.Sigmoid,
                                       bias_fn=lambda m0, msz, n0, d=d: col("a0_%d" % d, m0 // 128)[0:msz]),
                        row0=128 + d * 64)
        self.linear(D_["LORA16"], 256, wfull, "g2", cfg.BW, self.epi_store(dst32=D_["G"]), row0=256)

    def rwkv_prep_c(self, P32, D_):
        P = self.P
        cfg = self.cfg
        NP = self.NP
        BC = cfg.BC
        pv = self.pv
        PVt = self.PVt
        col = lambda name, c: PVt.t[:, pv[name][0] + c:pv[name][0] + c + 1]
        L = {n: P.sb("rp_" + n, [128, NP], F32) for n in ("r", "k", "v", "kk", "kd0", "kd1")}
        for c in range(BC):
            xr = self.ldpad(P32, cfg.seg["r"][0] + c * 128)
            r_s = self.tshift(xr, col("mu0", c), col("mu1", c), L["r"])
            self.stpad(D_["R"], c * 128, r_s)
            xk = self.ldpad(P32, cfg.seg["k"][0] + c * 128)
            k_s = self.tshift(xk, col("mu0", BC + c), col("mu1", BC + c), L["k"])
            xv = self.ldpad(P32, cfg.seg["v"][0] + c * 128)
            v_s = self.tshift(xv, col("mu0", 2 * BC + c), col("mu1", 2 * BC + c), L["v"])
            self.stpad(D_["V"], c * 128, v_s)
            kkr = self.wk.get()
            P.op("vector", lambda e: e.tensor_scalar(kkr.t[:, 1:NP - 1], k_s.t[:, 1:NP - 1], col("kk", c), None, ALU.mult),
                 reads=[k_s, PVt], writes=[kkr])
            sq = self.wk.get()
            P.op("gpsimd", lambda e: e.tensor_tensor(sq.t[:, 1:NP - 1], kkr.t[:, 1:NP - 1], kkr.t[:, 1:NP - 1], ALU.mult),
                 reads=[kkr], writes=[sq])
            rn = self.wk.get()

            def f1(ps, c0, cs):
                P.op("scalar", lambda e: e.activation(rn.t[:, c0:c0 + cs], ps.t[:, 0:cs], ACT.Sqrt, bias=self.eps12.t[:, 0:1]),
                     reads=[ps, self.eps12], writes=[rn])
            self.blocksum(sq, f1)
            P.op("vector", lambda e: e.reciprocal(rn.t[:, 1:NP - 1], rn.t[:, 1:NP - 1]), reads=[rn], writes=[rn])
            kk = L["kk"]
            P.op("vector", lambda e: e.tensor_tensor(kk.t[:, 1:NP - 1], kkr.t[:, 1:NP - 1], rn.t[:, 1:NP - 1], ALU.mult),
                 reads=[kkr, rn], writes=[kk])
            self.stpad(D_["KK"], c * 128, kk)
            kds = []
            for d in range(2):
                a = self.ldpad(D_["A%d" % d], c * 128)
                t = self.wk.get()
                P.op("vector", lambda e: e.tensor_scalar(t.t[:, 1:NP - 1], a.t[:, 1:NP - 1], col("ka", c), col("c1", c), ALU.mult, ALU.add),
                     reads=[a, PVt], writes=[t])
                kd = L["kd%d" % d]
                P.op("vector", lambda e: e.tensor_tensor(kd.t[:, 1:NP - 1], k_s.t[:, 1:NP - 1], t.t[:, 1:NP - 1], ALU.mult),
                     reads=[k_s, t], writes=[kd])
                self.stpad(D_["KD%d" % d], c * 128, kd)
                kds.append(kd)
                nb = self.wk.get()
                P.op("vector", lambda e: e.scalar_tensor_tensor(nb.t[:, 1:NP - 1], kk.t[:, 1:NP - 1], -1.0, a.t[:, 1:NP - 1],
                                                                ALU.mult, ALU.mult), reads=[kk, a], writes=[nb])
                self.stpad(D_["NB%d" % d], c * 128, nb)
                lw = self.ldpad(D_["LW%d" % d], c * 128)
                w = self.wk.get()
                P.op("scalar", lambda e: e.activation(w.t[:, 1:NP - 1], lw.t[:, 1:NP - 1], ACT.Exp), reads=[lw], writes=[w])
                self.stpad(D_["W%d" % d], c * 128, w)
            s_ = self.wk.get()
            P.op("gpsimd", lambda e: e.tensor_tensor(s_.t[:, 1:NP - 1], kds[0].t[:, 1:NP - 1], kds[1].t[:, 1:NP - 1], ALU.add),
                 reads=kds, writes=[s_])
            pr = self.wk.get()
            P.op("vector", lambda e: e.scalar_tensor_tensor(pr.t[:, 1:NP - 1], s_.t[:, 1:NP - 1], col("rk", c), r_s.t[:, 1:NP - 1],
                                                            ALU.mult, ALU.mult), reads=[s_, r_s, PVt], writes=[pr])
            bon = self.wk.get()

            def f2(ps, c0, cs):
                P.op("vector", lambda e: e.tensor_tensor(bon.t[:, c0:c0 + cs], ps.t[:, 0:cs], v_s.t[:, c0:c0 + cs], ALU.mult),
                     reads=[ps, v_s], writes=[bon])
            self.blocksum(pr, f2)
            self.stpad(D_["BON"], c * 128, bon)

    def rwkv_scan(self, D_, TC=32):
        P = self.P
        cfg = self.cfg
        G = cfg.BC
        GV = G * 64
        H = cfg.H_RW
        C, NT = cfg.C, cfg.NT
        S = []
        for d in range(2):
            s = {}
            s["M"] = P.sb("sM%d" % d, [128, GV], F32)
            P.op("vector", lambda e, s=s: e.memset(s["M"].t[:, :], 0.0), writes=[s["M"]])
            for nm in ("KK", "R", "KD", "NB", "W", "V"):
                s[nm] = P.sb("s%s%d" % (nm, d), [128, G, TC], F32)
            s["LK"] = P.sb("sLK%d" % d, [128, TC, 16 * ((H + 15) // 16)], F32)
            s["LR"] = P.sb("sLR%d" % d, [128, TC, 16 * ((H + 15) // 16)], F32)
            for nm in ("LK", "LR"):
                P.op("gpsimd", lambda e, t=s[nm]: e.memset(t.t[:, :, :], 0.0), writes=[s[nm]])
            s["VT"] = P.sb("sVT%d" % d, [TC, G * 128], F32)
            s["SA"] = P.sb("sSA%d" % d, [16, GV], F32)
            s["T1"] = P.sb("sT1%d" % d, [128, GV], F32)
            s["T2"] = P.sb("sT2%d" % d, [128, GV], F32)
            s["OM"] = P.sb("sOM%d" % d, [16, 8, GV], F32)
            s["OR"] = P.sb("sOR%d" % d, [16, TC, 64], F32)
            s["pA"] = P.ps("pA%d" % d, [128, 512], F32)
            s["pB"] = P.ps("pB%d" % d, [128, 512], F32)
            s["pS"] = P.ps("pS%d" % d, [128, 512], F32)
            s["pV"] = P.ps("pV%d" % d, [128, 512], F32)
            S.append(s)
        HP = H
        assert HP <= 16 and GV <= 512

        def chunks(d):
            out = []
            for a, b in ((0, C), (C, NT)):
                cs = [(t0, min(TC, b - t0)) for t0 in range(a, b, TC)]
                if d == 1:
                    cs = cs[::-1]
                out += cs
            return out
        ch = [chunks(0), chunks(1)]
        src = [dict(KK="KK", R="R", KD="KD0", NB="NB0", W="W0", V="V"), dict(KK="KK", R="R", KD="KD1", NB="NB1", W="W1", V="V")]

        def bcast(t, ti):
            return t.t[:, :, ti:ti + 1].broadcast_to([128, G, 64])

        def load_chunk(d, t0, tn):
            s = S[d]
            for nm in ("KK", "R", "KD", "NB", "W", "V"):
                dr = D_[src[d][nm]]
                P.dma("sync", s[nm], s[nm].t[:, :, 0:tn], dr, dr.t[:, t0:t0 + tn].rearrange("(g p) t -> p g t", p=128))
            for nm, L in (("KK", "LK"), ("R", "LR")):
                for par in range(2):
                    lo = par * 64
                    outv = s[L].t[lo:lo + 64, 0:tn, 0:2 * G].rearrange("p t (g two) -> p t g two", two=2)[:, :, :, par]
                    inv = s[nm].t[lo:lo + 64, :, 0:tn].rearrange("p g t -> p t g")
                    P.op("gpsimd", lambda e, outv=outv, inv=inv: e.tensor_copy(outv, inv), reads=[s[nm]], writes=[s[L]])
            for g0 in range(0, G, 4):
                nb = min(4, G - g0)
                for j in range(nb):
                    g = g0 + j
                    P.op("tensor", lambda e, g=g, j=j: e.matmul(s["pB"].t[0:tn, j * 128:(j + 1) * 128], s["V"].t[:, g, 0:tn],
                                                                self.ident32.t[:, :], start=True, stop=True),
                         reads=[s["V"], self.ident32], writes=[s["pB"]])
                self.evac("scalar", s["VT"].t[0:tn, g0 * 128:(g0 + nb) * 128], s["pB"].t[0:tn, 0:nb * 128], [s["pB"]], [s["VT"]])

        def mm_sa(d, ti):
            s = S[d]
            P.op("tensor", lambda e: e.matmul(s["pA"].t[0:HP, 0:GV], s["LK"].t[:, ti, 0:HP], s["M"].t[:, :], start=True, stop=True),
                 reads=[s["LK"], s["M"]], writes=[s["pA"]])

        def mm_o(d, ti):
            s = S[d]
            P.op("tensor", lambda e: e.matmul(s["pB"].t[0:HP, 0:GV], s["LR"].t[:, ti, 0:HP], s["M"].t[:, :], start=True, stop=True),
                 reads=[s["LR"], s["M"]], writes=[s["pB"]])
            P.op("vector", lambda e: e.tensor_tensor(s["OM"].t[0:HP, ti % 8, :], s["pB"].t[0:HP, 0:GV], self.mask16.t[0:HP, 0:GV], ALU.mult),
                 reads=[s["pB"], self.mask16], writes=[s["OM"]])
            if (d == 0 and ti % 8 == 7) or (d == 1 and ti % 8 == 0):
                t8 = (ti // 8) * 8
                P.op("vector", lambda e: e.tensor_reduce(
                    s["OR"].t[0:HP, t8:t8 + 8, :], s["OM"].t[0:HP, 0:8, :].rearrange("p t (g v) -> p t v g", v=64), AX.X, ALU.add),
                    reads=[s["OM"]], writes=[s["OR"]])

        def step(d, ti, tn):
            s = S[d]
            M3 = s["M"].t[:, :].rearrange("p (g v) -> p g v", v=64)
            for par in range(2):
                rhs = s["VT"].t[0:tn, :].rearrange("t (g q v) -> t g q v", q=2, v=64)[:, :, par, :]
                P.op("tensor", lambda e, par=par, rhs=rhs: e.matmul(
                    s["pV"].t[par * 64:(par + 1) * 64, 0:GV].rearrange("p (g v) -> p g v", v=64),
                    self.onehot.t[0:tn, ti, :], rhs, start=True, stop=True),
                    reads=[self.onehot, s["VT"]], writes=[s["pV"]])
            P.op("vector", lambda e: e.tensor_tensor(s["T2"].t[:, :].rearrange("p (g v) -> p g v", v=64),
                                                     s["pV"].t[:, 0:GV].rearrange("p (g v) -> p g v", v=64),
                                                     bcast(s["KD"], ti), ALU.mult), reads=[s["pV"], s["KD"]], writes=[s["T2"]])
            mm_sa(d, ti)
            P.op("vector", lambda e: e.tensor_tensor(s["SA"].t[0:HP, :], s["pA"].t[0:HP, 0:GV], self.mask16.t[0:HP, 0:GV], ALU.mult),
                 reads=[s["pA"], self.mask16], writes=[s["SA"]])
            P.op("tensor", lambda e: e.matmul(s["pS"].t[:, 0:GV], self.esel.t[0:HP, :], s["SA"].t[0:HP, :], start=True, stop=True),
                 reads=[self.esel, s["SA"]], writes=[s["pS"]])
            P.op("vector", lambda e: e.tensor_tensor(s["T1"].t[:, :].rearrange("p (g v) -> p g v", v=64),
                                                     s["pS"].t[:, 0:GV].rearrange("p (g v) -> p g v", v=64),
                                                     bcast(s["NB"], ti), ALU.mult), reads=[s["pS"], s["NB"]], writes=[s["T1"]])
            P.op("gpsimd", lambda e: e.tensor_tensor(M3, M3, bcast(s["W"], ti), ALU.mult), reads=[s["M"], s["W"]], writes=[s["M"]])
            P.op("gpsimd", lambda e: e.tensor_tensor(s["M"].t[:, :], s["M"].t[:, :], s["T2"].t[:, :], ALU.add),
                 reads=[s["M"], s["T2"]], writes=[s["M"]])
            P.op("vector", lambda e: e.tensor_tensor(s["M"].t[:, :], s["M"].t[:, :], s["T1"].t[:, :], ALU.add),
                 reads=[s["M"], s["T1"]], writes=[s["M"]])
            mm_o(d, ti)

        def flush(d, t0, tn):
            s = S[d]
            assert tn % 8 == 0
            dst = D_["O%d" % d]
            P.dma("gpsimd", dst, dst.t[t0:t0 + tn, :].rearrange("t (h v) -> h t v", v=64), s["OR"], s["OR"].t[0:HP, 0:tn, :])

        nch = len(ch[0])
        for ci in range(nch):
            for d in range(2):
                t0, tn = ch[d][ci]
                load_chunk(d, t0, tn)
            order = [list(range(ch[0][ci][1])), list(range(ch[1][ci][1]))[::-1]]
            for si in range(max(len(order[0]), len(order[1]))):
                for d in range(2):
                    if si < len(order[d]):
                        step(d, order[d][si], ch[d][ci][1])
            for d in range(2):
                flush(d, *ch[d][ci])

    def rwkv_readout(self, D_, Y, yrow0, lng, lnb):
        P = self.P
        cfg = self.cfg
        BW, H = cfg.BW, cfg.H_RW
        tk = Pool([P.sb("ro%d" % i, [128, BW], F32) for i in range(5)])
        sm = Pool([P.sb("rs%d" % i, [128, H], F32) for i in range(4)])
        fm = Pool([P.sb("rf%d" % i, [128, 128], F32) for i in range(4)])
        fmb = Pool([P.sb("rfb%d" % i, [128, 128], BF16) for i in range(2)])
        for t0 in range(0, cfg.NT, 128):
            a = tk.get()
            b = tk.get()
            P.dma("sync", a, a.t[:, :], D_["O0"], D_["O0"].t[t0:t0 + 128, :])
            P.dma("sync", b, b.t[:, :], D_["O1"], D_["O1"].t[t0:t0 + 128, :])
            P.op("vector", lambda e: e.tensor_tensor(a.t[:, :], a.t[:, :], b.t[:, :], ALU.add), reads=[a, b], writes=[a])
            a3 = a.t[:, :].rearrange("p (h v) -> p h v", v=64)
            mean = sm.get()
            P.op("vector", lambda e: e.tensor_reduce(mean.t[:, :], a3, AX.X, ALU.add), reads=[a], writes=[mean])
            P.op("vector", lambda e: e.tensor_scalar(mean.t[:, :], mean.t[:, :], -1.0 / 64, None, ALU.mult), reads=[mean], writes=[mean])
            P.op("vector", lambda e: e.tensor_tensor(a3, a3, mean.t[:, :].unsqueeze(2).broadcast_to([128, H, 64]), ALU.add),
                 reads=[a, mean], writes=[a])
            sq = tk.get()
            P.op("gpsimd", lambda e: e.tensor_tensor(sq.t[:, :], a.t[:, :], a.t[:, :], ALU.mult), reads=[a], writes=[sq])
            var = sm.get()
            P.op("vector", lambda e: e.tensor_reduce(var.t[:, :], sq.t[:, :].rearrange("p (h v) -> p h v", v=64), AX.X, ALU.add),
                 reads=[sq], writes=[var])
            P.op("scalar", lambda e: e.activation(var.t[:, :], var.t[:, :], ACT.Sqrt, bias=self.epsgn.t[:, 0:1], scale=1.0 / 64),
                 reads=[var, self.epsgn], writes=[var])
            P.op("vector", lambda e: e.reciprocal(var.t[:, :], var.t[:, :]), reads=[var], writes=[var])
            P.op("vector", lambda e: e.tensor_tensor(a3, a3, var.t[:, :].unsqueeze(2).broadcast_to([128, H, 64]), ALU.mult),
                 reads=[a, var], writes=[a])
            P.op("vector", lambda e: e.tensor_tensor(a.t[:, :], a.t[:, :], lng.t[:, :], ALU.mult), reads=[a, lng], writes=[a])
            P.op("gpsimd", lambda e: e.tensor_tensor(a.t[:, :], a.t[:, :], lnb.t[:, :], ALU.add), reads=[a, lnb], writes=[a])
            for c in range(cfg.BC):
                ps = self.pspool.get()
                P.op("tensor", lambda e, c=c: e.matmul(ps.t[:, 0:128], a.t[:, c * 128:(c + 1) * 128], self.ident32.t[:, :],
                                                       start=True, stop=True), reads=[a, self.ident32], writes=[ps])
                bon = fm.get()
                g = fm.get()
                P.dma("sync", bon, bon.t[:, :], D_["BON"], D_["BON"].t[c * 128:(c + 1) * 128, t0:t0 + 128])
                P.dma("sync", g, g.t[:, :], D_["G"], D_["G"].t[c * 128:(c + 1) * 128, t0:t0 + 128])
                P.op("vector", lambda e: e.tensor_tensor(bon.t[:, :], ps.t[:, 0:128], bon.t[:, :], ALU.add), reads=[ps, bon], writes=[bon])
                ob = fmb.get()
                P.op("vector", lambda e: e.tensor_tensor(ob.t[:, :], bon.t[:, :], g.t[:, :], ALU.mult), reads=[bon, g], writes=[ob])
                P.dma("gpsimd", Y, Y.t[yrow0 + c * 128:yrow0 + (c + 1) * 128, t0:t0 + 128], ob, ob.t[:, :])

    def conv(self, P32, Y, yrow0):
        P = self.P
        cfg = self.cfg
        NP = self.NP
        pv = self.pv
        col = lambda name, c: self.PVt.t[:, pv[name][0] + c:pv[name][0] + c + 1]
        for c in range(cfg.BC):
            xb = self.ldpad(P32, cfg.seg["cb"][0] + c * 128)
            xc = self.ldpad(P32, cfg.seg["cc"][0] + c * 128)
            xu = self.ldpad(P32, cfg.seg["cu"][0] + c * 128)
            z = self.wk.get()
            P.op("vector", lambda e: e.tensor_tensor(z.t[:, :], xc.t[:, :], xu.t[:, :], ALU.mult), reads=[xc, xu], writes=[z])
            y = self.wk.get()
            P.op("vector", lambda e: e.tensor_scalar(y.t[:, 1:NP - 1], z.t[:, 0:NP - 2], col("cw0", c), None, ALU.mult),
                 reads=[z, self.PVt], writes=[y])
            P.op("vector", lambda e: e.scalar_tensor_tensor(y.t[:, 1:NP - 1], z.t[:, 1:NP - 1], col("cw1", c), y.t[:, 1:NP - 1],
                                                            ALU.mult, ALU.add), reads=[z, y, self.PVt], writes=[y])
            P.op("vector", lambda e: e.scalar_tensor_tensor(y.t[:, 1:NP - 1], z.t[:, 2:NP], col("cw2", c), y.t[:, 1:NP - 1],
                                                            ALU.mult, ALU.add), reads=[z, y, self.PVt], writes=[y])
            ob = self.wkb.get()
            P.op("vector", lambda e: e.tensor_tensor(ob.t[:, 1:NP - 1], y.t[:, 1:NP - 1], xb.t[:, 1:NP - 1], ALU.mult),
                 reads=[y, xb], writes=[ob])
            self.stpad(Y, yrow0 + c * 128, ob)

    def merge(self, GL, Y, wfull, ACC16):
        P = self.P
        cfg = self.cfg
        pv = self.pv
        BWC = cfg.BWG // 128
        BWL = cfg.BW // 128
        nmax = max(128, min(512, (32768 // (2 + 4 * BWC)) // 128 * 128))
        sub = []
        for a_, ns_ in cfg.tiles:
            o_ = 0
            while o_ < ns_:
                s_ = min(nmax, ns_ - o_)
                sub.append((a_ + o_, s_))
                o_ += s_
        for n0, nsz in sub:
            xt = self.xpool.get()
            flat = xt.t
            P.dma("sync", xt, flat[:, 0:2 * nsz].rearrange("p (kc n) -> p kc n", n=nsz),
                  GL, GL.t[0:256, n0:n0 + nsz].rearrange("(kc p) n -> p kc n", p=128))
            for rk in range(cfg.split):
                for i in range(4):
                    for kl in range(BWL):
                        s0 = 2 + i * BWC + rk * BWL + kl
                        r0 = ((i * BWL + kl) * cfg.split + rk) * 128
                        P.dma("sync", xt, flat[:, s0 * nsz:(s0 + 1) * nsz], Y, Y.t[r0:r0 + 128, n0:n0 + nsz])
            for mc in range(cfg.DC):
                acc = self.accpool.get()
                for i in range(4):
                    wg = self.wpool.get()
                    P.dma("sync", wg, wg.t[:, 0:256], wfull[0], self.wview(wfull, "gu_%d" % i, mc)[:, 0:256])
                    P.dma("sync", wg, wg.t[:, 256:256 + BWC * 128], wfull[0], self.wview(wfull, "wb_%d" % i, mc)[:, 0:BWC * 128])
                    pg = self.pspool.get()
                    for kc in range(2):
                        P.op("tensor", lambda e, kc=kc: e.matmul(pg.t[:, 0:nsz], wg.t[:, kc * 128:(kc + 1) * 128],
                                                                 flat[:, kc * nsz:(kc + 1) * nsz], start=(kc == 0), stop=(kc == 1)),
                             reads=[wg, xt], writes=[pg])
                    gt = self.opool.get()
                    gb = self.PVt.t[:, pv["gb_%d" % i][0] + mc:pv["gb_%d" % i][0] + mc + 1]
                    P.op("scalar", lambda e: e.activation(gt.t[:, 0:nsz], pg.t[:, 0:nsz], ACT.Sigmoid, bias=gb),
                         reads=[pg, self.PVt], writes=[gt])
                    pb = self.pspool.get()
                    for kc in range(BWC):
                        xc = 2 + i * BWC + kc
                        P.op("tensor", lambda e, kc=kc, xc=xc: e.matmul(pb.t[:, 0:nsz], wg.t[:, 256 + kc * 128:256 + (kc + 1) * 128],
                                                                        flat[:, xc * nsz:(xc + 1) * nsz], start=(kc == 0), stop=(kc == BWC - 1)),
                             reads=[wg, xt], writes=[pb])
                    if i == 0:
                        P.op("vector", lambda e: e.tensor_tensor(acc.t[:, 0:nsz], pb.t[:, 0:nsz], gt.t[:, 0:nsz], ALU.mult),
                             reads=[pb, gt], writes=[acc])
                    else:
                        P.op("vector", lambda e: e.tensor_tensor(gt.t[:, 0:nsz], pb.t[:, 0:nsz], gt.t[:, 0:nsz], ALU.mult),
                             reads=[pb, gt], writes=[gt])
                        P.op("gpsimd", lambda e: e.tensor_tensor(acc.t[:, 0:nsz], acc.t[:, 0:nsz], gt.t[:, 0:nsz], ALU.add),
                             reads=[acc, gt], writes=[acc])
                ob = self.opoolb.get()
                P.op("vector", lambda e: e.tensor_copy(ob.t[:, 0:nsz], acc.t[:, 0:nsz]), reads=[acc], writes=[ob])
                P.dma("gpsimd", ACC16, ACC16.t[mc * 128:(mc + 1) * 128, n0:n0 + nsz], ob, ob.t[:, 0:nsz])

    def epi_residual(self, xold, xnew, j):
        P = self.P
        cfg = self.cfg

        def epi(ps, m0, msz, n0, nsz):
            ci = cfg.cond_of(n0)
            xt = self.ld(self.opool, xold, m0, msz, n0, nsz)
            mcol = self.modS.t[0:msz, j * cfg.DC + m0 // 128, ci:ci + 1]
            P.op("vector", lambda e: e.scalar_tensor_tensor(xt.t[0:msz, 0:nsz], ps.t[0:msz, 0:nsz], mcol, xt.t[0:msz, 0:nsz],
                                                            ALU.mult, ALU.add), reads=[ps, xt, self.modS], writes=[xt])
            self.st(xnew, m0, msz, n0, nsz, xt)
        return epi

    def modulation(self, COND16, wfull, T16):
        P = self.P
        cfg = self.cfg
        pv = self.pv
        self.linear(COND16, cfg.D, wfull, "mod_down", cfg.MR, self.epi_store(dst16=T16), tiles=[(0, 2)])

        def epi(ps, m0, msz, n0, nsz):
            mc = m0 // 128
            b = self.PVt.t[:, pv["modb"][0] + mc:pv["modb"][0] + mc + 1]
            P.op("scalar", lambda e: e.activation(self.modS.t[:, mc, 0:2], ps.t[:, 0:2], ACT.Identity, bias=b),
                 reads=[ps, self.PVt], writes=[self.modS])
        self.linear(T16, cfg.MR, wfull, "mod_up", 6 * cfg.D, epi, tiles=[(0, 2)])
        DC = cfg.DC
        for k, (gname, j) in enumerate((("n1g", 1), ("n2g", 4))):
            g = self.PVt.t[:, pv[gname][0]:pv[gname][0] + DC]
            out = self.modA.t[:, k * DC:(k + 1) * DC, :]
            P.op("vector", lambda e, out=out, j=j: e.tensor_scalar(out, self.modS.t[:, j * DC:(j + 1) * DC, :], 1.0, None, ALU.add),
                 reads=[self.modS], writes=[self.modA])
            P.op("vector", lambda e, out=out, g=g: e.tensor_tensor(out, out, g.unsqueeze(2).broadcast_to([128, DC, 2]), ALU.mult),
                 reads=[self.modA, self.PVt], writes=[self.modA])

import math
import numpy as np
import concourse.bass as bass
import concourse.mybir as mybir


TC_SCAN = 32


def pv_layout(cfg):
    DC, BC = cfg.DC, cfg.BC
    QC, KVC = pad128(cfg.QL) // 128, pad128(cfg.KVL) // 128
    RC = 3 * BC + 4
    items = [("n1g", DC), ("n2g", DC), ("modb", 6 * DC), ("qg", QC), ("kvg", KVC), ("mu0", RC), ("mu1", RC),
             ("w0_0", BC), ("w0_1", BC), ("a0_0", BC), ("a0_1", BC), ("kk", BC), ("ka", BC), ("c1", BC), ("rk", BC),
             ("cw0", BC), ("cw1", BC), ("cw2", BC), ("gb_0", DC), ("gb_1", DC), ("gb_2", DC), ("gb_3", DC)]
    pv = {}
    o = 0
    for n, k in items:
        pv[n] = (o, k)
        o += k
    return pv, o


def build_program(cfg, debug_outs=()):
    nc = bass.Bass("TRN2", target_bir_lowering=False)
    P = Prog(nc)
    m = Model(P, cfg)
    D, NT, C, T, BW, L = cfg.D, cfg.NT, cfg.C, cfg.T, cfg.BW, cfg.DEPTH
    DC, BC = cfg.DC, cfg.BC
    H = cfg.H_MLA
    QLp, KVLp = pad128(cfg.QL), pad128(cfg.KVL)
    pv, PVN = pv_layout(cfg)
    m.pv = pv

    def inp(name, shape):
        return P.dram(name, shape, F32, kind="ExternalInput")
    xT = inp("xT", [D, NT])
    cond = inp("cond", [D, 2])
    pvec = inp("pvec", [L * 128, PVN])
    bcin = inp("bcin", [L * 128, 2 * BW + 128])
    dlin = inp("dlin", [L, 256])
    fng = inp("fng", [128, DC])
    c_ident = inp("c_ident", [128, 128])
    c_blk1 = inp("c_blk1", [128, 128])
    c_mask16 = inp("c_mask16", [16, 512])
    c_esel = inp("c_esel", [16, 128])
    c_onehot = inp("c_onehot", [TC_SCAN, TC_SCAN * 64])
    c_cos = inp("c_cos", [128, NT])
    c_sin = inp("c_sin", [128, NT])
    NG = cfg.NG
    nshard = max(1, cfg.ncores // cfg.split)
    shard_rows = [cfg.wrows[g] // nshard for g in range(NG)]
    wsh = [inp("wflat%d" % g, [L * shard_rows[g], WCOLS]) for g in range(NG)]
    yT = P.dram("yT", [D, T], F32, kind="ExternalOutput")
    dbg = {}

    wfull = []
    for l in range(L):
        grp = []
        for g in range(NG):
            wf = P.dram("wfull%d_%d" % (l, g), [cfg.wrows[g], WCOLS], BF16)
            sr = shard_rows[g]
            tgt = wf if cfg.ncores == 1 else P.dram("wsh16_%d_%d" % (l, g), [sr, WCOLS], BF16)
            items = []
            for r0 in range(0, sr, 4096):
                rn = min(4096, sr - r0)
                items.append(lambda tgt=tgt, g=g, l=l, sr=sr, r0=r0, rn=rn: P.dma(
                    "gpsimd", tgt, tgt.t[r0:r0 + rn, :], wsh[g], wsh[g].t[l * sr + r0:l * sr + r0 + rn, :]))
            if cfg.ncores > 1:
                wgroups = [list(range(h * nshard, (h + 1) * nshard)) for h in range(cfg.split)]
                for j in range(sr // GCH):
                    items.append(lambda tgt=tgt, wf=wf, j=j: P.collective(
                        "AllGather", ALU.bypass, wgroups, tgt, tgt.t[j * GCH:(j + 1) * GCH, :],
                        wf, wf.t[j * nshard * GCH:(j + 1) * nshard * GCH, :]))
            for it in items:
                it()
            grp.append(wf)
        wfull.append(grp)

    def const_tile(name, src, shape, dt=F32):
        t = P.sb(name, shape, dt)
        P.dma("sync", t, t.t[:], src, src.t[:])
        return t
    m.ident32 = const_tile("ident32", c_ident, [128, 128])
    m.blk1 = const_tile("blk1", c_blk1, [128, 128])
    m.mask16 = const_tile("mask16", c_mask16, [16, 512])
    m.esel = const_tile("esel", c_esel, [16, 128])
    oh = P.sb("onehot", [TC_SCAN, TC_SCAN, 64], F32)
    P.dma("sync", oh, oh.t[:].rearrange("p a b -> p (a b)"), c_onehot, c_onehot.t[:])
    m.onehot = oh
    m.cos2 = const_tile("cos2", c_cos, [128, NT])
    m.sin2 = const_tile("sin2", c_sin, [128, NT])
    m.identb = P.sb("identb", [128, 128], BF16)
    P.op("vector", lambda e: e.tensor_copy(m.identb.t[:], m.ident32.t[:]), reads=[m.ident32], writes=[m.identb])
    m.ones = P.sb("ones", [128, 128], F32)
    P.op("vector", lambda e: e.memset(m.ones.t[:], 1.0), writes=[m.ones])
    for nm, val in (("eps_t", NORM_EPS), ("eps12", 1e-12), ("epsgn", GN_EPS)):
        t = P.sb(nm, [128, 1], F32)
        P.op("vector", lambda e, t=t, val=val: e.memset(t.t[:], val), writes=[t])
        setattr(m, nm, t)
    fngt = const_tile("fngt", fng, [128, DC])
    m.PVt = P.sb("PVt", [128, PVN], F32)
    m.modS = P.sb("modS", [128, 6 * DC, 2], F32)
    m.modA = P.sb("modA", [128, 2 * DC, 2], F32)
    bct = P.sb("bct", [128, 2 * BW + 128], F32)
    lam_row = P.sb("lam_row", [1, 260], F32)
    lam_col = P.sb("lam_col", [128, 2], F32)
    gtile = P.sb("gtile", [128, 128], F32)

    def dr(name, shape, dt=F32):
        return P.dram(name, shape, dt)
    xres = [xT, dr("xresA", [D, NT]), dr("xresB", [D, NT])]
    hB = dr("hB", [D, NT], BF16)
    P32 = dr("P32", [cfg.PCOLS, NT])
    CQN = dr("CQN", [QLp, NT], BF16)
    CKVN = dr("CKVN", [KVLp, NT], BF16)
    Q32 = dr("Q32", [H * 128, NT])
    QN = dr("QN", [H * 128, NT], BF16)
    QR = dr("QR", [H * 64, NT], BF16)
    KN = dr("KN", [H * 128, NT], BF16)
    VM = dr("VM", [H * 128, NT], BF16)
    KR = dr("KR", [64, NT], BF16)
    DQ = dr("DQ", [BW, NT], BF16)
    DK = dr("DK", [BW, NT], BF16)
    DV = dr("DV", [BW, NT], BF16)
    Y = dr("Y", [4 * BW, NT], BF16)
    YG = dr("YG", [cfg.split * 4 * BW, NT], BF16) if cfg.split > 1 else Y
    pairs = [[c, c + cfg.ncores // 2] for c in range(cfg.ncores // 2)] if cfg.split > 1 else None
    MP = dr("MP", [D, NT]) if cfg.split > 1 else None
    MPR = dr("MPR", [D, NT]) if cfg.split > 1 else None
    RW = {n: dr("RW_" + n, [BW, NT]) for n in ("R", "V", "KK", "KD0", "KD1", "NB0", "NB1", "W0", "W1", "BON", "G",
                                               "LW0", "LW1", "A0", "A1")}
    RW["LORA16"] = dr("LORA16", [512, NT], BF16)
    RW["O0"] = dr("RW_O0", [NT, BW])
    RW["O1"] = dr("RW_O1", [NT, BW])
    GL = dr("GL", [256, NT], BF16)
    ACC16 = dr("ACC16", [D, NT], BF16)
    HID16 = dr("HID16", [cfg.DFFL, NT], BF16)
    COND16 = dr("COND16", [D, 2], BF16)
    T16 = dr("T16", [cfg.MR, 2], BF16)

    with P.scope():
        ct = P.sb("condt", [128, DC, 2], F32)
        cb = P.sb("condb", [128, DC, 2], BF16)
        P.dma("sync", ct, ct.t[:], cond, cond.t[:, :].rearrange("(c p) n -> p c n", p=128))
        P.op("scalar", lambda e: e.activation(cb.t[:], ct.t[:], ACT.Silu), reads=[ct], writes=[cb])
        P.dma("gpsimd", COND16, COND16.t[:, :].rearrange("(c p) n -> p c n", p=128), cb, cb.t[:])

    col = lambda name, c: m.PVt.t[:, pv[name][0] + c:pv[name][0] + c + 1]
    cur = 0
    for l in range(L):
        P.flush_bg(l)
        wf = wfull[l]
        lam_init = 0.8 - 0.6 * math.exp(-0.3 * l)
        P.dma("sync", m.PVt, m.PVt.t[:, :], pvec, pvec.t[l * 128:(l + 1) * 128, :])
        P.dma("sync", bct, bct.t[:, :], bcin, bcin.t[l * 128:(l + 1) * 128, :])
        P.dma("sync", lam_row, lam_row.t[0:1, 0:256], dlin, dlin.t[l:l + 1, :])
        ka = m.PVt.t[:, pv["ka"][0]:pv["ka"][0] + BC]
        P.op("vector", lambda e: e.tensor_scalar(m.PVt.t[:, pv["c1"][0]:pv["c1"][0] + BC], ka, -1.0, 1.0, ALU.mult, ALU.add),
             reads=[m.PVt], writes=[m.PVt])
        P.op("vector", lambda e: e.tensor_tensor(lam_row.t[0:1, 0:64], lam_row.t[0:1, 0:64], lam_row.t[0:1, 64:128], ALU.mult),
             reads=[lam_row], writes=[lam_row])
        P.op("vector", lambda e: e.tensor_tensor(lam_row.t[0:1, 128:192], lam_row.t[0:1, 128:192], lam_row.t[0:1, 192:256], ALU.mult),
             reads=[lam_row], writes=[lam_row])
        P.op("vector", lambda e: e.reduce_sum(lam_row.t[0:1, 256:257], lam_row.t[0:1, 0:64], AX.X), reads=[lam_row], writes=[lam_row])
        P.op("vector", lambda e: e.reduce_sum(lam_row.t[0:1, 257:258], lam_row.t[0:1, 128:192], AX.X), reads=[lam_row], writes=[lam_row])
        P.op("scalar", lambda e: e.activation(lam_row.t[0:1, 256:258], lam_row.t[0:1, 256:258], ACT.Exp), reads=[lam_row], writes=[lam_row])
        P.op("vector", lambda e: e.tensor_tensor(lam_row.t[0:1, 258:259], lam_row.t[0:1, 257:258], lam_row.t[0:1, 256:257], ALU.subtract),
             reads=[lam_row], writes=[lam_row])
        P.op("vector", lambda e: e.tensor_scalar(lam_row.t[0:1, 258:259], lam_row.t[0:1, 258:259], -lam_init, None, ALU.add),
             reads=[lam_row], writes=[lam_row])
        P.op("vector", lambda e: e.tensor_copy(lam_row.t[0:1, 259:260], lam_row.t[0:1, 258:259]), reads=[lam_row], writes=[lam_row])
        with P.scope():
            pl = P.ps("pl", [128, 512], F32)
            P.op("tensor", lambda e: e.matmul(pl.t[:, 0:2], m.ones.t[0:1, :], lam_row.t[0:1, 258:260], start=True, stop=True),
                 reads=[m.ones, lam_row], writes=[pl])
            P.op("vector", lambda e: e.tensor_copy(lam_col.t[:, 0:2], pl.t[:, 0:2]), reads=[pl], writes=[lam_col])
        P.op("vector", lambda e: e.tensor_scalar(gtile.t[:, :], bct.t[:, 2 * BW:2 * BW + 128], 1.0 - lam_init, None, ALU.mult),
             reads=[bct], writes=[gtile])
        with P.scope():
            m.alloc_linear_bufs()
            m.modulation(COND16, wf, T16)
        xin = xres[cur]
        xmid = xres[(cur + 1) % 3]
        xout = xres[(cur + 2) % 3]
        with P.scope():
            m.pspool = Pool([P.ps("ps%d" % i, [128, 512], F32) for i in range(4)])
            m.alloc_norm_bufs()
            m.rmsnorm(xin, 0, D, hB, 0, lambda c, ci: m.modA.t[:, c, ci:ci + 1], lambda c, ci: m.modS.t[:, c, ci:ci + 1],
                      deps=[m.modA, m.modS], ones=m.ones)
        with P.scope():
            m.alloc_linear_bufs()
            m.linear(hB, D, wf, "w_in", cfg.PCOLS, m.epi_store(dst32=P32))
            m.linear(hB, D, wf, "gate_down", cfg.GR, m.epi_store(dst16=GL))
        if "P32" in debug_outs and l == 0:
            dbg["P32"] = P32
        with P.scope():
            m.pspool = Pool([P.ps("ps%d" % i, [128, 512], F32) for i in range(4)])
            m.alloc_norm_bufs()
            m.rmsnorm(P32, cfg.seg["cq"][0], cfg.QL, CQN, 0, lambda c, ci: col("qg", c), deps=[m.PVt], ones=m.ones)
            m.rmsnorm(P32, cfg.seg["ckv"][0], cfg.KVL, CKVN, 0, lambda c, ci: col("kvg", c), deps=[m.PVt], ones=m.ones)
        with P.scope():
            m.alloc_linear_bufs()
            m.alloc_ew_bufs()
            e_qn = m.epi_store(dst16=QN)
            e_q32 = m.epi_store(dst32=Q32, row0=-H * 128)

            def epi_q(ps, m0, msz, n0, nsz):
                (e_qn if m0 < H * 128 else e_q32)(ps, m0, msz, n0, nsz)
            m.linear(CQN, QLp, wf, "w_uq", H * 256, epi_q)
            e_kn = m.epi_store(dst16=KN)
            e_vm = m.epi_store(dst16=VM, row0=-H * 128)

            def epi_kv(ps, m0, msz, n0, nsz):
                (e_kn if m0 < H * 128 else e_vm)(ps, m0, msz, n0, nsz)
            m.linear(CKVN, KVLp, wf, "w_ukv", H * 256, epi_kv)
            m.rope(Q32, 0, H * 64, H * 64, QR, 0)
            m.rope(P32, cfg.seg["kr"][0], cfg.seg["kr"][0] + 64, 64, KR, 0)
            m.rope(P32, cfg.seg["dq"][0], cfg.seg["dqr"][0], BW, DQ, 0)
            m.rope(P32, cfg.seg["dk"][0], cfg.seg["dkr"][0], BW, DK, 0)
            m.cast_rows(P32, cfg.seg["dv"][0], BW, DV, 0)
        with P.scope():
            m.alloc_attn_bufs()
            m.mla_attention(QN, QR, KN, KR, VM, Y, 0)
            m.diff_attention(DQ, DK, DV, Y, 3 * BW, lam_col, gtile, lam_init)
        with P.scope():
            m.alloc_pad_bufs(3, 4)
            m.rwkv_prep_a(P32, RW)
        with P.scope():
            m.alloc_linear_bufs()
            m.rwkv_prep_b(wf, RW)
        with P.scope():
            m.pspool = Pool([P.ps("ps%d" % i, [128, 512], F32) for i in range(4)])
            m.alloc_pad_bufs(6, 4)
            m.rwkv_prep_c(P32, RW)
        with P.scope():
            P.bg_on = False
            m.rwkv_scan(RW, TC=TC_SCAN)
            P.bg_on = True
        with P.scope():
            m.pspool = Pool([P.ps("ps%d" % i, [128, 512], F32) for i in range(4)])
            lng = Res("lng", bct.t[:, 0:BW])
            lnb = Res("lnb", bct.t[:, BW:2 * BW])
            lng.w = lnb.w = bct.w
            lng.r = lnb.r = bct.r
            m.rwkv_readout(RW, Y, BW, lng, lnb)
        with P.scope():
            m.alloc_pad_bufs(4, 4)
            m.conv(P32, Y, 2 * BW)
        if l == 0:
            for n_ in debug_outs:
                if n_ == "Y":
                    dbg["Y"] = Y
                if n_ in RW:
                    dbg[n_] = RW[n_]
        with P.scope():
            m.alloc_linear_bufs()
            import os as _os
            if cfg.split > 1 and cfg.ncores > 1 and not _os.environ.get("NO_YG"):
                for lc in range(4 * BW // 128):
                    P.collective("AllGather", ALU.bypass, pairs, Y, Y.t[lc * 128:(lc + 1) * 128, :],
                                 YG, YG.t[lc * 256:(lc + 1) * 256, :])
            m.merge(GL, YG, wf, ACC16)
            m.linear(ACC16, D, wf, "w_out", D, m.epi_residual(xin, xmid, 2))
        with P.scope():
            m.pspool = Pool([P.ps("ps%d" % i, [128, 512], F32) for i in range(4)])
            m.alloc_norm_bufs()
            m.rmsnorm(xmid, 0, D, hB, 0, lambda c, ci: m.modA.t[:, DC + c, ci:ci + 1], lambda c, ci: m.modS.t[:, 3 * DC + c, ci:ci + 1],
                      deps=[m.modA, m.modS], ones=m.ones)
        with P.scope():
            m.alloc_linear_bufs()
            m.linear(hB, D, wf, "mlp_w1", cfg.DFFL, m.epi_store(dst16=HID16, func=ACT.Relu, square=True))
            nmx = 512
            if cfg.split > 1:
                m.linear(HID16, cfg.DFFL, wf, "mlp_w2", D, m.epi_store(dst32=MP), nmax=nmx)
                if cfg.ncores > 1 and not _os.environ.get("NO_AR"):
                    for r0 in range(0, D, 64):
                        P.collective("AllReduce", ALU.add, pairs, MP, MP.t[r0:r0 + 64, :], MPR, MPR.t[r0:r0 + 64, :])
                for mc in range(DC):
                    for n0, nsz in cfg.tiles:
                        ci = cfg.cond_of(n0)
                        xt_ = m.ld(m.opool, xmid, mc * 128, 128, n0, nsz)
                        pt_ = m.ld(m.opool, MPR, mc * 128, 128, n0, nsz)
                        mcol = m.modS.t[:, 5 * DC + mc, ci:ci + 1]
                        P.op("vector", lambda e, xt_=xt_, pt_=pt_, mcol=mcol, nsz=nsz: e.scalar_tensor_tensor(
                            xt_.t[:, 0:nsz], pt_.t[:, 0:nsz], mcol, xt_.t[:, 0:nsz], ALU.mult, ALU.add),
                            reads=[xt_, pt_, m.modS], writes=[xt_])
                        m.st(xout, mc * 128, 128, n0, nsz, xt_)
            else:
                m.linear(HID16, cfg.DFFL, wf, "mlp_w2", D, m.epi_residual(xmid, xout, 5), nmax=nmx)
        cur = (cur + 2) % 3
        if l == 0 and "X1" in debug_outs:
            dbg["X1"] = xres[cur]
        if l == 0 and "XMID" in debug_outs:
            dbg["XMID"] = xmid
    with P.scope():
        m.pspool = Pool([P.ps("ps%d" % i, [128, 512], F32) for i in range(4)])
        m.alloc_norm_bufs(out_dt=F32)
        xt_tiles = [(a, s) for a, s in cfg.tiles if a >= C]
        m.rmsnorm(xres[cur], 0, D, yT, 0, lambda c, ci: fngt.t[:, c:c + 1], deps=[fngt], ones=m.ones, tiles=xt_tiles, dcol=C)
    outs = [yT]
    dbg_out = {}
    for n_, r in dbg.items():
        shp = list(r.t.shape)
        o = P.dram("dbg_" + n_, shp, F32, kind="ExternalOutput")
        P.dma("gpsimd", o, o.t[:, :], r, r.t[:, :])
        outs.append(o)
        dbg_out[n_] = "dbg_" + n_
    P.finish(outs)
    return nc, P, dbg_out


def tile_weight(W):
    K, M = W.shape
    KC, MC = cdiv(K, 128), cdiv(M, 128)
    Wp = np.zeros((KC * 128, MC * 128), np.float32)
    Wp[:K, :M] = W
    return np.ascontiguousarray(Wp.reshape(KC, 128, MC, 128).transpose(2, 1, 0, 3)).reshape(-1)


def pp(v, nchunks=None):
    v = np.asarray(v, np.float32).reshape(-1)
    k = cdiv(v.size, 128) if nchunks is None else nchunks
    o = np.zeros(k * 128, np.float32)
    o[:v.size] = v
    return np.ascontiguousarray(o.reshape(k, 128).T)


def rot64(idx):
    idx = np.asarray(idx).reshape(-1, 64)
    return np.concatenate([idx[:, 32:], idx[:, :32]], axis=1).reshape(-1)


def host_layer_weights(cfg, inp, l, half=0):
    D, BW, H = cfg.D, cfg.BW, cfg.H_MLA
    BWG = cfg.BWG
    ch = np.arange(half * BW, (half + 1) * BW)
    QL, KVL = cfg.QL, cfg.KVL
    w_in = inp["w_in"][l]
    o_cq, o_ckv, o_kr = 0, QL, QL + KVL
    o_rw = QL + KVL + 64
    o_r, o_k, o_v = o_rw, o_rw + BWG, o_rw + 2 * BWG
    o_wd = o_rw + 3 * BWG
    o_ad = o_wd + 128
    o_gd = o_ad + 128
    o_cv = o_gd + 160
    o_df = o_cv + 3 * BWG
    ext = np.zeros((D, cfg.PCOLS), np.float32)

    def put(name, cols):
        a, s = cfg.seg[name]
        ext[:, a:a + len(cols)] = w_in[:, cols]
    put("cq", np.arange(o_cq, o_cq + QL))
    put("ckv", np.arange(o_ckv, o_ckv + KVL))
    kr = np.arange(o_kr, o_kr + 64)
    put("kr", np.concatenate([kr, rot64(kr)]))
    put("r", o_r + ch)
    put("k", o_k + ch)
    put("v", o_v + ch)
    put("wd", np.arange(o_wd, o_wd + 128))
    put("ad", np.arange(o_ad, o_ad + 128))
    put("gd", np.arange(o_gd, o_gd + 160))
    put("cb", o_cv + ch)
    put("cc", o_cv + BWG + ch)
    put("cu", o_cv + 2 * BWG + ch)
    dq = o_df + ch
    dk = o_df + BWG + ch
    put("dq", dq)
    put("dk", dk)
    put("dv", o_df + 2 * BWG + ch)
    put("dqr", rot64(dq))
    put("dkr", rot64(dk))
    wq = inp["mla_w_uq"][l]
    hs = range(half * H, (half + 1) * H)
    qn = np.concatenate([np.arange(h * 192, h * 192 + 128) for h in hs])
    qr = np.concatenate([np.arange(h * 192 + 128, h * 192 + 192) for h in hs])
    wq_ext = wq[:, np.concatenate([qn, qr, rot64(qr)])]
    wkv = inp["mla_w_ukv"][l]
    kn = np.concatenate([np.arange(h * 256, h * 256 + 128) for h in hs])
    vv = np.concatenate([np.arange(h * 256 + 128, h * 256 + 256) for h in hs])
    wkv_ext = wkv[:, np.concatenate([kn, vv])]
    g2 = np.zeros((256, BW), np.float32)
    g2[:160] = inp["rwkv_g2"][l][:, ch]
    ws = {
        "mod_down": inp["mod_down"][l], "mod_up": inp["mod_up"][l], "w_in": ext, "w_uq": wq_ext, "w_ukv": wkv_ext,
        "w2_0": inp["rwkv_w2"][l, 0][:, ch], "w2_1": inp["rwkv_w2"][l, 1][:, ch],
        "a2_0": inp["rwkv_a2"][l, 0][:, ch], "a2_1": inp["rwkv_a2"][l, 1][:, ch],
        "g2": g2, "gate_down": inp["gate_down"][l], "w_out": inp["w_out"][l],
        "mlp_w1": inp["mlp_w1"][l][:, half * cfg.DFFL:(half + 1) * cfg.DFFL],
        "mlp_w2": inp["mlp_w2"][l][half * cfg.DFFL:(half + 1) * cfg.DFFL, :],
    }
    for i in range(4):
        ws["wb_%d" % i] = inp["w_branch"][l, i]
        ws["gu_%d" % i] = inp["gate_up"][l][:, i, :]
    flats = [np.zeros(cfg.wrows[g] * WCOLS, np.float32) for g in range(cfg.NG)]
    for n, K, M in cfg.wshapes:
        off = cfg.woff[n][0]
        w = ws[n]
        wp = np.zeros((K, M), np.float32)
        wp[:w.shape[0], :w.shape[1]] = w
        t = tile_weight(wp)
        flats[cfg.wgroup_of[n]][off:off + t.size] = t
    return [flats[g].reshape(cfg.wrows[g], WCOLS) for g in range(cfg.NG)]


def host_pvec(cfg, inp, l, half=0):
    pv, PVN = pv_layout(cfg)
    BW, BC, BWG = cfg.BW, cfg.BC, cfg.BWG
    ch = np.arange(half * BW, (half + 1) * BW)
    out = np.zeros((128, PVN), np.float32)

    def put(name, arr):
        a, k = pv[name]
        out[:, a:a + k] = pp(arr, k)
    put("n1g", inp["norm1_g"][l])
    put("n2g", inp["norm2_g"][l])
    put("modb", inp["mod_b"][l])
    put("qg", inp["mla_q_norm_g"][l])
    put("kvg", inp["mla_kv_norm_g"][l])
    for d in range(2):
        mu = inp["rwkv_mu"][l, d]
        mup = np.zeros(3 * BW + 512, np.float32)
        for j in range(3):
            mup[j * BW:(j + 1) * BW] = mu[j * BWG + ch]
        mup[3 * BW:3 * BW + 256] = mu[3 * BWG:3 * BWG + 256]
        mup[3 * BW + 256:3 * BW + 256 + 160] = mu[3 * BWG + 256:]
        put("mu%d" % d, mup)
        put("w0_%d" % d, inp["rwkv_w0"][l, d][ch])
        put("a0_%d" % d, inp["rwkv_a0"][l, d][ch])
    put("kk", inp["rwkv_k_k"][l][ch])
    put("ka", inp["rwkv_k_a"][l][ch])
    put("rk", inp["rwkv_r_k"][l].reshape(-1)[ch])
    for j in range(3):
        put("cw%d" % j, inp["conv_w"][l, j][ch])
    for i in range(4):
        put("gb_%d" % i, inp["gate_b"][l, i])
    return out


def rope_tables(cfg):
    T, C, GW = cfg.T, cfg.C, cfg.GRID_W
    rows = T // GW
    row = np.repeat(np.arange(rows, dtype=np.float32), GW)
    colv = np.tile(np.arange(GW, dtype=np.float32), rows)
    nf = 16
    inv = (10000.0 ** (-np.arange(nf, dtype=np.float32) / nf)).astype(np.float32)
    ang = np.concatenate([row[:, None] * inv, colv[:, None] * inv], axis=-1)
    cos, sin = np.cos(ang).T.astype(np.float32), np.sin(ang).T.astype(np.float32)
    cos64 = np.concatenate([cos, cos], 0)
    sin64 = np.concatenate([-sin, sin], 0)
    c = np.ones((128, cfg.NT), np.float32)
    s = np.zeros((128, cfg.NT), np.float32)
    c[:, C:] = np.concatenate([cos64, cos64], 0)
    s[:, C:] = np.concatenate([sin64, sin64], 0)
    return c, s


def host_consts(cfg):
    G = cfg.BC
    mask16 = np.zeros((16, 512), np.float32)
    esel = np.zeros((16, 128), np.float32)
    for h in range(16):
        g, par = h // 2, h % 2
        if g < G:
            mask16[h, g * 64:(g + 1) * 64] = 1.0
        esel[h, par * 64:(par + 1) * 64] = 1.0
    blk1 = np.zeros((128, 128), np.float32)
    blk1[:64, :64] = 1.0
    blk1[64:, 64:] = 1.0
    oh = np.zeros((TC_SCAN, TC_SCAN, 64), np.float32)
    for t in range(TC_SCAN):
        oh[t, t, :] = 1.0
    c, s = rope_tables(cfg)
    return dict(c_ident=np.eye(128, dtype=np.float32), c_blk1=blk1, c_mask16=mask16, c_esel=esel,
                c_onehot=oh.reshape(TC_SCAN, TC_SCAN * 64), c_cos=c, c_sin=s)


def host_inputs(cfg, inp, ncores, batch_of_core, half_of_core=None):
    L, BW = cfg.DEPTH, cfg.BW
    if half_of_core is None:
        half_of_core = [0] * ncores
    consts = host_consts(cfg)
    dl = np.ascontiguousarray(inp["diff_lambda"].reshape(L, 256))
    fng = pp(inp["final_norm_g"])
    pvecs, bcs = {}, {}
    for half in sorted(set(half_of_core)):
        ch = np.arange(half * BW, (half + 1) * BW)
        pvecs[half] = np.concatenate([host_pvec(cfg, inp, l, half) for l in range(L)], 0)
        bc = np.zeros((L * 128, 2 * BW + 128), np.float32)
        for l in range(L):
            bc[l * 128:(l + 1) * 128, 0:BW] = np.tile(inp["rwkv_ln_g"][l][ch][None, :], (128, 1))
            bc[l * 128:(l + 1) * 128, BW:2 * BW] = np.tile(inp["rwkv_ln_b"][l][ch][None, :], (128, 1))
            bc[l * 128:(l + 1) * 128, 2 * BW:] = np.tile(inp["diff_norm_g"][l][None, :], (128, 1))
        bcs[half] = bc
    NG = cfg.NG
    nshard = max(1, ncores // cfg.split)
    sr = [cfg.wrows[g] // nshard for g in range(NG)]
    wsh = [[np.zeros((L * sr[g], WCOLS), np.float32) for g in range(NG)] for _ in range(ncores)]
    for l in range(L):
        for half in sorted(set(half_of_core)):
            flats = host_layer_weights(cfg, inp, l, half)
            members = [c for c in range(ncores) if half_of_core[c] == half]
            for g in range(NG):
                v = flats[g].reshape(sr[g] // GCH, nshard, GCH, WCOLS)
                for j, c in enumerate(members):
                    wsh[c][g][l * sr[g]:(l + 1) * sr[g]] = v[:, j].reshape(sr[g], WCOLS)
            del flats
    maps = []
    for c in range(ncores):
        b = batch_of_core[c]
        xT = np.ascontiguousarray(np.concatenate([inp["ctx"][b], inp["x"][b]], 0).T)
        cond = np.ascontiguousarray(np.stack([inp["c_ctx"], inp["c"][b]], 1))
        d = dict(xT=xT, cond=cond, pvec=pvecs[half_of_core[c]], bcin=bcs[half_of_core[c]], dlin=dl, fng=fng)
        for g in range(NG):
            d["wflat%d" % g] = wsh[c][g]
        d.update(consts)
        maps.append(d)
    return maps


from concourse.bass_utils import run_bass_kernel_spmd


def kernel(**inputs):
    inputs = {k: np.asarray(v) for k, v in inputs.items()}
    B, T, D = inputs["x"].shape
    C = inputs["ctx"].shape[1]
    L = inputs["w_in"].shape[0]
    ncores = 8
    cfg = Cfg(D=D, T=T, C=C, DEPTH=L, ncores=ncores, split=2)
    nc, P, _ = build_program(cfg)
    batch_of_core = [c % B for c in range(ncores)]
    half_of_core = [c // B for c in range(ncores)]
    maps = host_inputs(cfg, inputs, ncores, batch_of_core, half_of_core)
    res = run_bass_kernel_spmd(nc, maps, core_ids=list(range(ncores)))
    first = {}
    for c in range(ncores):
        first.setdefault(batch_of_core[c], c)
    out = np.stack([np.ascontiguousarray(res.results[first[b]]["yT"].T) for b in range(B)], 0)
    return out.astype(np.float32)
```

```python
import numpy as np
from contextlib import ExitStack
import concourse.bass as bass
import concourse.mybir as mybir

F32 = mybir.dt.float32
BF16 = mybir.dt.bfloat16
ALU = mybir.AluOpType
ACT = mybir.ActivationFunctionType
AX = mybir.AxisListType
ENGS = ("tensor", "vector", "scalar", "gpsimd", "sync")


class Res:
    def __init__(self, name, t):
        self.name = name
        self.t = t
        self.w = {}
        self.r = {}
        self.sem = None

    def __getitem__(self, k):
        return self.t[k]


class Prog:
    def __init__(self, nc):
        self.nc = nc
        self.es = ExitStack()
        self.semes = ExitStack()
        self.q = {e: [] for e in ENGS}
        self.cnt = {}
        self.sems = {}
        self.known = {e: {} for e in ENGS}
        for e in ENGS:
            self._mksem("E_" + e)
        self.nres = 0
        self.scopes = []
        self.depth = 0
        self.bg = []
        self.bg_every = 10**9
        self.bg_count = 0
        self.bg_on = True
        self.bg_busy = False
        self.free_keys = []
        self.scope_keys = []
        self.ninstr = 0

    def _mksem(self, key):
        self.sems[key] = self.semes.enter_context(self.nc.semaphore(key))
        self.cnt[key] = 0
        return key

    def sb(self, name, shape, dt=F32):
        self.nres += 1
        name = "%s_%d" % (name, self.nres)
        t = self.es.enter_context(self.nc.sbuf_tensor(name, list(shape), dt))
        r = Res(name, t)
        r.scoped = self.depth > 0
        return r

    def ps(self, name, shape, dt=F32):
        self.nres += 1
        name = "%s_%d" % (name, self.nres)
        t = self.es.enter_context(self.nc.psum_tensor(name, list(shape), dt))
        r = Res(name, t)
        r.scoped = self.depth > 0
        return r

    def dram(self, name, shape, dt=F32, kind="Internal"):
        t = self.nc.dram_tensor(name, list(shape), dt, kind=kind)
        return Res(name, t)

    def _deps(self, eng, reads, writes, own):
        need = {}
        own_raw = 0
        for r in reads:
            for k, v in r.w.items():
                need[k] = max(need.get(k, 0), v)
                if k == own:
                    own_raw = max(own_raw, v)
        for w in writes:
            for k, v in w.w.items():
                need[k] = max(need.get(k, 0), v)
                if k == own:
                    own_raw = max(own_raw, v)
            for k, v in w.r.items():
                need[k] = max(need.get(k, 0), v)
        kn = self.known[eng]
        out = []
        for k, v in need.items():
            if k == own:
                if eng in ("vector", "scalar", "gpsimd") and own is not None and own.startswith("E_") \
                        and own_raw > kn.get(k, 0):
                    kn[k] = own_raw
                    out.append((k, own_raw))
                continue
            if kn.get(k, 0) >= v:
                continue
            kn[k] = v
            out.append((k, v))
        return out

    def _commit(self, reads, writes, key, val):
        for r in reads:
            r.r[key] = max(r.r.get(key, 0), val)
        for w in writes:
            w.w = {key: val}
            w.r = {}

    def op(self, eng, fn, reads=(), writes=()):
        key = "E_" + eng
        waits = self._deps(eng, reads, writes, key)
        self.cnt[key] += 1
        val = self.cnt[key]
        sems = self.sems
        sem = sems[key]

        e = getattr(self.nc, eng)
        for k, v in waits:
            e.wait_ge(sems[k], v)
        fn(e).then_inc(sem, 1)
        self._commit(reads, writes, key, val)
        self.ninstr += 1

    def pump(self, n=1):
        if self.bg_busy:
            return
        self.bg_busy = True
        while n > 0 and self.bg:
            self.bg.pop(0)[1]()
            n -= 1
        self.bg_busy = False

    def flush_bg(self, tag):
        self.bg_busy = True
        while self.bg and self.bg[0][0] <= tag:
            self.bg.pop(0)[1]()
        self.bg_busy = False

    def dma(self, eng, out_res, out_ap, in_res, in_ap, **kw):
        if eng == "gpsimd" and self.bg and self.bg_on and not self.bg_busy:
            self.bg_count += 1
            if self.bg_count % self.bg_every == 0:
                self.pump(1)
        if out_res.sem is None:
            if getattr(out_res, "scoped", False):
                if self.free_keys:
                    out_res.sem = self.free_keys.pop()
                else:
                    out_res.sem = self._mksem("DS_%d" % len(self.sems))
                self.scope_keys[-1].append(out_res.sem)
            else:
                out_res.sem = self._mksem("D_%d_%s" % (len(self.sems), out_res.name))
        key = out_res.sem
        waits = self._deps(eng, [in_res], [out_res], key)
        self.cnt[key] += 16
        val = self.cnt[key]
        sems = self.sems
        sem = sems[key]

        e = getattr(self.nc, eng)
        for k, v in waits:
            e.wait_ge(sems[k], v)
        e.dma_start(out=out_ap, in_=in_ap, **kw).then_inc(sem, 16)
        in_res.r[key] = max(in_res.r.get(key, 0), val)
        w = dict(out_res.w)
        w[key] = val
        out_res.w = {key: val}
        out_res.r = {}
        self.ninstr += 1

    def collective(self, kind, op, groups, in_res, in_ap, out_res, out_ap):
        if out_res.sem is None:
            out_res.sem = self._mksem("C_%d_%s" % (len(self.sems), out_res.name))
        key = out_res.sem
        waits = self._deps("gpsimd", [in_res], [out_res], key)
        e = self.nc.gpsimd
        for k, v in waits:
            e.wait_ge(self.sems[k], v)
        self.cnt[key] += 1
        val = self.cnt[key]
        e.collective_compute(kind, op, replica_groups=groups, ins=[in_ap], outs=[out_ap]).then_inc(self.sems[key], 1)
        in_res.r[key] = max(in_res.r.get(key, 0), val)
        out_res.w = {key: val}
        out_res.r = {}
        self.ninstr += 1

    def wait_all(self, eng, resources):
        waits = self._deps(eng, resources, [], None)
        sems = self.sems

        e = getattr(self.nc, eng)
        for k, v in waits:
            e.wait_ge(sems[k], v)

    def barrier(self):
        snap = dict(self.cnt)
        sems = self.sems
        for eng in ENGS:
            kn = self.known[eng]
            waits = [(k, v) for k, v in snap.items() if v > 0 and k != "E_" + eng and kn.get(k, 0) < v]
            for k, v in waits:
                kn[k] = v

            e = getattr(self.nc, eng)
            for k, v in waits:
                e.wait_ge(sems[k], v)

    def scope(self):
        prog = self

        class _S:
            def __enter__(s):
                prog.barrier()
                s.old = prog.es
                prog.es = ExitStack()
                prog.depth += 1
                prog.scope_keys.append([])
                return s

            def __exit__(s, *a):
                prog.barrier()
                prog.es.close()
                prog.es = s.old
                prog.depth -= 1
                prog.free_keys.extend(prog.scope_keys.pop())
                return False
        return _S()

    def finish(self, outs):
        for eng in ("sync", "gpsimd", "vector", "scalar", "tensor"):
            self.wait_all(eng, outs)
        self.barrier()
        self.es.close()


class Pool:
    def __init__(self, items):
        self.items = items
        self.i = 0

    def get(self):
        r = self.items[self.i % len(self.items)]
        self.i += 1
        return r

import math
import numpy as np
import concourse.bass as bass


NORM_EPS = 1e-6
GN_EPS = 64e-5
WCOLS = 1024
GCH = 512


def cdiv(a, b):
    return (a + b - 1) // b


def pad128(n):
    return cdiv(n, 128) * 128


class Cfg:
    def __init__(self, D=4096, T=2048, C=256, DEPTH=4, GRID_W=64, ncores=8, split=1):
        self.D, self.T, self.C, self.DEPTH, self.GRID_W, self.ncores = D, T, C, DEPTH, GRID_W, ncores
        self.NT = C + T
        self.split = split
        self.BWG = D // 4
        self.BW = self.BWG // split
        self.H_MLA = self.BW // 128
        self.H_RW = self.BW // 64
        self.H_DF = self.BW // 128
        self.QL = 3 * D // 16
        self.KVL = D // 16
        self.DFF = 4 * D
        self.DFFL = self.DFF // split
        self.GR = 256
        self.MR = 256
        self.DC = D // 128
        self.BC = self.BW // 128
        BW = self.BW
        segs = [("cq", pad128(self.QL)), ("ckv", pad128(self.KVL)), ("kr", 128),
                ("r", BW), ("k", BW), ("v", BW), ("wd", 128), ("ad", 128), ("gd", 256),
                ("cb", BW), ("cc", BW), ("cu", BW), ("dq", BW), ("dk", BW), ("dv", BW),
                ("dqr", BW), ("dkr", BW)]
        self.seg = {}
        o = 0
        for n, s in segs:
            self.seg[n] = (o, s)
            o += s
        self.PCOLS = o
        self.tiles = []
        for a, b in ((0, C), (C, C + T)):
            n = a
            while n < b:
                s = min(512, b - n)
                self.tiles.append((n, s))
                n += s
        self.segs_tok = ((0, C), (C, C + T))
        QLp, KVLp = pad128(self.QL), pad128(self.KVL)
        self.wshapes = [
            ("mod_down", D, self.MR), ("mod_up", self.MR, 6 * D), ("w_in", D, self.PCOLS),
            ("w_uq", QLp, self.H_MLA * 256), ("w_ukv", KVLp, self.H_MLA * 256),
            ("w2_0", 64, BW), ("w2_1", 64, BW), ("a2_0", 64, BW), ("a2_1", 64, BW), ("g2", 256, BW),
            ("wb_0", self.BWG, D), ("wb_1", self.BWG, D), ("wb_2", self.BWG, D), ("wb_3", self.BWG, D),
            ("gate_down", D, self.GR), ("gu_0", self.GR, D), ("gu_1", self.GR, D), ("gu_2", self.GR, D),
            ("gu_3", self.GR, D), ("w_out", D, D), ("mlp_w1", D, self.DFFL), ("mlp_w2", self.DFFL, D),
        ]
        self.wgroup_of = {}
        for n, K, M in self.wshapes:
            self.wgroup_of[n] = 1 if n == "mlp_w1" else (2 if n == "mlp_w2" else 0)
        self.NG = 3
        self.woff = {}
        o = [0] * self.NG
        for n, K, M in self.wshapes:
            g = self.wgroup_of[n]
            self.woff[n] = (o[g], K, M)
            o[g] += pad128(K) * pad128(M)
        self.wrows = [cdiv(cdiv(o[g], WCOLS), 8 * GCH) * 8 * GCH for g in range(self.NG)]

    def cond_of(self, n0):
        return 0 if n0 < self.C else 1


class Model:
    def __init__(self, P, cfg):
        self.P = P
        self.cfg = cfg
        self.nc = P.nc
        self.rr = 0

    def wview(self, wfull, name, mc):
        off, K, M = self.cfg.woff[name]
        kw = pad128(K)
        o = off + mc * 128 * kw
        return bass.AP(wfull[self.cfg.wgroup_of[name]].t, o, [[kw, 128], [1, kw]])

    def evac(self, eng, out_ap, in_ap, reads, writes, func=None, bias=None, scale=1.0):
        P = self.P
        if eng == "scalar":
            f = func if func is not None else ACT.Copy
            if bias is None:
                P.op("scalar", lambda e: e.activation(out_ap, in_ap, f, scale=scale), reads=reads, writes=writes)
            else:
                P.op("scalar", lambda e: e.activation(out_ap, in_ap, f, bias=bias, scale=scale),
                     reads=reads, writes=writes)
        else:
            P.op("vector", lambda e: e.tensor_copy(out_ap, in_ap), reads=reads, writes=writes)

    def alt(self):
        self.rr += 1
        return "scalar" if self.rr % 2 else "vector"

    def alloc_linear_bufs(self):
        P = self.P
        self.xpool = Pool([P.sb("xb%d" % i, [128, 32768], BF16) for i in range(1)])
        wmax = max(pad128(K) for _, K, _ in self.cfg.wshapes)
        nwb = max(2, 32768 // wmax)
        self.wpool = Pool([P.sb("wb%d" % i, [128, wmax], BF16) for i in range(nwb)])
        self.pspool = Pool([P.ps("ps%d" % i, [128, 512], F32) for i in range(6)])
        self.opool = Pool([P.sb("ob%d" % i, [128, 512], F32) for i in range(4)])
        self.opoolb = Pool([P.sb("obb%d" % i, [128, 512], BF16) for i in range(4)])
        self.accpool = Pool([P.sb("acc%d" % i, [128, 512], F32) for i in range(2)])

    def linear(self, xT, K, wfull, wname, M, epi, tiles=None, nmax=512, row0=0):
        P = self.P
        KC = cdiv(K, 128)
        MC = cdiv(M, 128)
        if tiles is None:
            tiles = self.cfg.tiles
        XCAP = 32768
        cap = max(128, min(XCAP // KC, 4 * 512) // 128 * 128)
        sub = []
        for n0, ns in tiles:
            o = 0
            while o < ns:
                s = min(nmax, cap, ns - o)
                sub.append((n0 + o, s))
                o += s
        groups, curg, tot = [], [], 0
        for st_ in sub:
            if curg and (tot + st_[1] > cap or len(curg) >= 3):
                groups.append(curg)
                curg, tot = [], 0
            curg.append(st_)
            tot += st_[1]
        if curg:
            groups.append(curg)
        kfull = K // 128
        krem = K - kfull * 128
        for grp in groups:
            xt = self.xpool.get()
            flat = xt.t
            bases = []
            b0 = 0
            for n0, nsz in grp:
                bases.append(b0)
                if kfull:
                    P.dma("sync", xt, flat[:, b0:b0 + kfull * nsz].rearrange("p (kc n) -> p kc n", n=nsz),
                          xT, xT.t[row0:row0 + kfull * 128, n0:n0 + nsz].rearrange("(kc p) n -> p kc n", p=128))
                if krem:
                    P.dma("sync", xt, flat[0:krem, b0 + kfull * nsz:b0 + (kfull + 1) * nsz],
                          xT, xT.t[row0 + kfull * 128:row0 + K, n0:n0 + nsz])
                b0 += KC * nsz
            for mc in range(MC):
                m0 = mc * 128
                msz = min(128, M - m0)
                wt = self.wpool.get()
                P.dma("sync", wt, wt.t[:, 0:KC * 128], wfull[self.cfg.wgroup_of[wname]], self.wview(wfull, wname, mc)[:, 0:KC * 128])
                for (n0, nsz), b0 in zip(grp, bases):
                    ps = self.pspool.get()
                    for kc in range(KC):
                        ksz = min(128, K - kc * 128)
                        P.op("tensor", lambda e, kc=kc, ksz=ksz, ps=ps, b0=b0, nsz=nsz: e.matmul(
                            ps.t[0:msz, 0:nsz], wt.t[0:ksz, kc * 128:kc * 128 + msz],
                            flat[0:ksz, b0 + kc * nsz:b0 + (kc + 1) * nsz], start=(kc == 0), stop=(kc == KC - 1)),
                            reads=[wt, xt], writes=[ps])
                    epi(ps, m0, msz, n0, nsz)

    def epi_store(self, dst32=None, dst16=None, func=None, bias_fn=None, row0=0, square=False, mul=None):
        P = self.P

        def epi(ps, m0, msz, n0, nsz):
            eng = "scalar" if (func is not None or bias_fn is not None) else self.alt()
            bias = bias_fn(m0, msz, n0) if bias_fn is not None else None
            if dst32 is not None:
                o = self.opool.get()
                self.evac(eng, o.t[0:msz, 0:nsz], ps.t[0:msz, 0:nsz], [ps], [o], func=func, bias=bias)
                if square:
                    P.op("vector", lambda e: e.tensor_tensor(o.t[0:msz, 0:nsz], o.t[0:msz, 0:nsz],
                                                             o.t[0:msz, 0:nsz], ALU.mult), reads=[o], writes=[o])
                if mul is not None:
                    P.op("vector", lambda e: e.tensor_scalar(o.t[0:msz, 0:nsz], o.t[0:msz, 0:nsz], mul, None, ALU.mult),
                         reads=[o], writes=[o])
                P.dma("gpsimd", dst32, dst32.t[row0 + m0:row0 + m0 + msz, n0:n0 + nsz], o, o.t[0:msz, 0:nsz])
                if dst16 is not None:
                    ob = self.opoolb.get()
                    P.op("vector", lambda e: e.tensor_copy(ob.t[0:msz, 0:nsz], o.t[0:msz, 0:nsz]),
                         reads=[o], writes=[ob])
                    P.dma("gpsimd", dst16, dst16.t[row0 + m0:row0 + m0 + msz, n0:n0 + nsz], ob, ob.t[0:msz, 0:nsz])
            else:
                ob = self.opoolb.get()
                if square:
                    o = self.opool.get()
                    self.evac(eng, o.t[0:msz, 0:nsz], ps.t[0:msz, 0:nsz], [ps], [o], func=func, bias=bias)
                    P.op("vector", lambda e: e.tensor_tensor(ob.t[0:msz, 0:nsz], o.t[0:msz, 0:nsz],
                                                             o.t[0:msz, 0:nsz], ALU.mult), reads=[o], writes=[ob])
                else:
                    self.evac(eng, ob.t[0:msz, 0:nsz], ps.t[0:msz, 0:nsz], [ps], [ob], func=func, bias=bias)
                P.dma("gpsimd", dst16, dst16.t[row0 + m0:row0 + m0 + msz, n0:n0 + nsz], ob, ob.t[0:msz, 0:nsz])
        return epi

    def rmsnorm(self, src, srow0, F, dst16, drow0, A_fn, B_fn=None, deps=(), ones=None, tiles=None, dcol=0):
        P = self.P
        cfg = self.cfg
        FC = cdiv(F, 128)
        nmax = max(128, min(512, (8192 // FC) // 128 * 128))
        sub = []
        for a, ns in (tiles if tiles is not None else cfg.tiles):
            o = 0
            while o < ns:
                s_ = min(nmax, ns - o)
                sub.append((a + o, s_))
                o += s_
        for n0, nsz in sub:
            ci = cfg.cond_of(n0)
            xt = self.nx.get()
            v = xt.t[:, 0:FC * nsz].rearrange("p (c n) -> p c n", n=nsz)
            P.dma("sync", xt, v, src, src.t[srow0:srow0 + FC * 128, n0:n0 + nsz].rearrange("(c p) n -> p c n", p=128))
            sq = self.nsq.get()
            sv = sq.t[:, 0:FC * nsz].rearrange("p (c n) -> p c n", n=nsz)
            P.op("scalar", lambda e: e.activation(sv, v, ACT.Square), reads=[xt], writes=[sq])
            ps = self.pspool.get()
            for c in range(FC):
                P.op("tensor", lambda e, c=c: e.matmul(ps.t[:, 0:nsz], ones.t[:, :], sq.t[:, c * nsz:(c + 1) * nsz],
                                                       start=(c == 0), stop=(c == FC - 1)), reads=[ones, sq], writes=[ps])
            rs = self.nrs.get()
            P.op("scalar", lambda e: e.activation(rs.t[:, 0:nsz], ps.t[:, 0:nsz], ACT.Sqrt, bias=self.eps_t.t[:, 0:1],
                                                  scale=1.0 / F), reads=[ps, self.eps_t], writes=[rs])
            P.op("vector", lambda e: e.reciprocal(rs.t[:, 0:nsz], rs.t[:, 0:nsz]), reads=[rs], writes=[rs])
            ot = self.no.get()
            ov = ot.t[:, 0:FC * nsz].rearrange("p (c n) -> p c n", n=nsz)
            P.op("vector", lambda e: e.tensor_tensor(v, v, rs.t[:, 0:nsz].unsqueeze(1).broadcast_to([128, FC, nsz]),
                                                     ALU.mult), reads=[xt, rs], writes=[xt])
            for c in range(FC):
                a_ap = A_fn(c, ci)
                if B_fn is not None:
                    b_ap = B_fn(c, ci)
                    P.op("scalar", lambda e, c=c, a_ap=a_ap, b_ap=b_ap: e.activation(
                        ov[:, c, :], v[:, c, :], ACT.Identity, bias=b_ap, scale=a_ap),
                        reads=[xt] + list(deps), writes=[ot])
                else:
                    P.op("vector", lambda e, c=c, a_ap=a_ap: e.tensor_scalar(
                        ov[:, c, :], v[:, c, :], a_ap, None, ALU.mult), reads=[xt] + list(deps), writes=[ot])
            P.dma("gpsimd", dst16, dst16.t[drow0:drow0 + FC * 128, n0 - dcol:n0 - dcol + nsz].rearrange("(c p) n -> p c n", p=128),
                  ot, ov)

    def alloc_norm_bufs(self, out_dt=BF16):
        P = self.P
        self.nx = Pool([P.sb("nx%d" % i, [128, 8192], F32) for i in range(2 if out_dt == BF16 else 1)])
        self.nsq = Pool([P.sb("nsq%d" % i, [128, 8192], F32) for i in range(1)])
        self.nrs = Pool([P.sb("nrs%d" % i, [128, 512], F32) for i in range(2)])
        self.no = Pool([P.sb("no%d" % i, [128, 8192], out_dt) for i in range(2 if out_dt == BF16 else 1)])

    def ld(self, pool, src, r0, nrows, n0, nsz, q="sync"):
        t = pool.get()
        self.P.dma(q, t, t.t[0:nrows, 0:nsz], src, src.t[r0:r0 + nrows, n0:n0 + nsz])
        return t

    def st(self, dst, r0, nrows, n0, nsz, t, q="gpsimd"):
        self.P.dma(q, dst, dst.t[r0:r0 + nrows, n0:n0 + nsz], t, t.t[0:nrows, 0:nsz])

    def rope(self, src, srow, rrow, nrows_total, dst16, drow):
        P = self.P
        cfg = self.cfg
        done = 0
        while done < nrows_total:
            nr = min(128, nrows_total - done)
            for n0, nsz in cfg.tiles:
                a = self.ld(self.epool, src, srow + done, nr, n0, nsz)
                b = self.ld(self.epool, src, rrow + done, nr, n0, nsz)
                P.op("vector", lambda e: e.tensor_tensor(a.t[0:nr, 0:nsz], a.t[0:nr, 0:nsz],
                                                         self.cos2.t[0:nr, n0:n0 + nsz], ALU.mult),
                     reads=[a, self.cos2], writes=[a])
                P.op("gpsimd", lambda e: e.tensor_tensor(b.t[0:nr, 0:nsz], b.t[0:nr, 0:nsz],
                                                         self.sin2.t[0:nr, n0:n0 + nsz], ALU.mult),
                     reads=[b, self.sin2], writes=[b])
                o = self.epoolb.get()
                P.op("vector", lambda e: e.tensor_tensor(o.t[0:nr, 0:nsz], a.t[0:nr, 0:nsz], b.t[0:nr, 0:nsz], ALU.add),
                     reads=[a, b], writes=[o])
                self.st(dst16, drow + done, nr, n0, nsz, o)
            done += nr

    def cast_rows(self, src, srow, nrows_total, dst16, drow):
        P = self.P
        done = 0
        while done < nrows_total:
            nr = min(128, nrows_total - done)
            for n0, nsz in self.cfg.tiles:
                a = self.ld(self.epool, src, srow + done, nr, n0, nsz)
                o = self.epoolb.get()
                P.op("vector", lambda e: e.tensor_copy(o.t[0:nr, 0:nsz], a.t[0:nr, 0:nsz]), reads=[a], writes=[o])
                self.st(dst16, drow + done, nr, n0, nsz, o)
            done += nr

    def alloc_ew_bufs(self):
        P = self.P
        self.epool = Pool([P.sb("ew%d" % i, [128, 512], F32) for i in range(6)])
        self.epoolb = Pool([P.sb("ewb%d" % i, [128, 512], BF16) for i in range(3)])

    def alloc_attn_bufs(self):
        P = self.P
        NT = self.cfg.NT
        NK = NT // 128
        self.a_q = [P.sb("a_q%d" % i, [128, NT], BF16) for i in range(2)]
        self.a_k = [P.sb("a_k%d" % i, [128, NT], BF16) for i in range(2)]
        self.a_vf = P.sb("a_vf", [128, NT], BF16)
        self.a_vt = P.sb("a_vt", [128, NK, 128], BF16)
        self.a_pb = P.sb("a_pb", [128, NT], BF16)
        self.a_pt = P.sb("a_pt", [128, NK, 128], BF16)
        self.a_y = P.sb("a_y", [128, NT], BF16)
        self.a_sc = P.ps("a_sc", [128, 512 * cdiv(NT, 512)], F32)
        self.a_pT = P.ps("a_pT", [128, 512], F32)
        self.a_pO = [P.ps("a_pO%d" % i, [128, 512], F32) for i in range(2)]
        self.a_small = Pool([P.sb("a_sm%d" % i, [128, 8], F32) for i in range(4)])
        self.a_o = Pool([P.sb("a_o%d" % i, [128, 128], F32) for i in range(3)])
        self.a_ob = Pool([P.sb("a_ob%d" % i, [128, 128], BF16) for i in range(2)])

    def v_to_tokmajor(self):
        P = self.P
        NK = self.cfg.NT // 128
        for c0 in range(0, NK, 4):
            nb = min(4, NK - c0)
            for j in range(nb):
                c = c0 + j
                P.op("tensor", lambda e, c=c, j=j: e.matmul(self.a_pT.t[:, j * 128:(j + 1) * 128],
                                                            self.a_vf.t[:, c * 128:(c + 1) * 128], self.identb.t[:, :],
                                                            start=True, stop=True),
                     reads=[self.a_vf, self.identb], writes=[self.a_pT])
            self.evac(self.alt(), self.a_vt.t[:, c0:c0 + nb, :].rearrange("p c d -> p (c d)"),
                      self.a_pT.t[:, 0:nb * 128], [self.a_pT], [self.a_vt])

    def softmax_pv(self, qparts, kparts, q0, nk, scale, pO):
        P = self.P
        sc = self.a_sc
        npart = len(qparts)
        for k0 in range(0, nk, 512):
            ks = min(512, nk - k0)
            for i, ((qr, qa, qb), (kr, ka, kb)) in enumerate(zip(qparts, kparts)):
                P.op("tensor", lambda e, qr=qr, qa=qa, qb=qb, kr=kr, ka=ka, kb=kb, i=i: e.matmul(
                    sc.t[:, k0:k0 + ks], qr.t[qa:qb, q0:q0 + 128], kr.t[ka:kb, k0:k0 + ks],
                    start=(i == 0), stop=(i == npart - 1)), reads=[qr, kr], writes=[sc])
        sm = self.a_small.get()
        P.op("vector", lambda e: e.reduce_max(sm.t[:, 0:1], sc.t[:, 0:nk], AX.X), reads=[sc], writes=[sm])
        P.op("vector", lambda e: e.tensor_scalar(sm.t[:, 1:2], sm.t[:, 0:1], -scale, None, ALU.mult), reads=[sm], writes=[sm])
        P.op("scalar", lambda e: e.activation(self.a_pb.t[:, 0:nk], sc.t[:, 0:nk], ACT.Exp, bias=sm.t[:, 1:2],
                                              scale=scale, accum_out=sm.t[:, 2:3]), reads=[sc, sm], writes=[self.a_pb, sm])
        P.op("vector", lambda e: e.reciprocal(sm.t[:, 3:4], sm.t[:, 2:3]), reads=[sm], writes=[sm])
        nc_ = nk // 128
        for c0 in range(0, nc_, 4):
            nb = min(4, nc_ - c0)
            for j in range(nb):
                c = c0 + j
                P.op("tensor", lambda e, c=c, j=j: e.matmul(self.a_pT.t[:, j * 128:(j + 1) * 128],
                                                            self.a_pb.t[:, c * 128:(c + 1) * 128], self.identb.t[:, :],
                                                            start=True, stop=True),
                     reads=[self.a_pb, self.identb], writes=[self.a_pT])
            self.evac(self.alt(), self.a_pt.t[:, c0:c0 + nb, :].rearrange("p c d -> p (c d)"),
                      self.a_pT.t[:, 0:nb * 128], [self.a_pT], [self.a_pt])
        for c in range(nc_):
            P.op("tensor", lambda e, c=c: e.matmul(pO.t[:, 0:128], self.a_pt.t[:, c, :], self.a_vt.t[:, c, :],
                                                   start=(c == 0), stop=(c == nc_ - 1)),
                 reads=[self.a_pt, self.a_vt], writes=[pO])
        return sm

    def out_transpose(self, ob, q0):
        P = self.P
        P.op("tensor", lambda e: e.matmul(self.a_pT.t[:, 0:128], ob.t[:, :], self.identb.t[:, :], start=True, stop=True),
             reads=[ob, self.identb], writes=[self.a_pT])
        self.evac(self.alt(), self.a_y.t[:, q0:q0 + 128], self.a_pT.t[:, 0:128], [self.a_pT], [self.a_y])

    def load_rows(self, t, src, r0, nr, p0=0):
        NT = self.cfg.NT
        self.P.dma("sync", t, t.t[p0:p0 + nr, 0:NT], src, src.t[r0:r0 + nr, 0:NT])

    def mla_attention(self, QN, QR, KN, KR, VM, Y, yrow0):
        P = self.P
        cfg = self.cfg
        scale = (128 + 64) ** -0.5
        self.load_rows(self.a_k[1], KR, 0, 64)
        for h in range(cfg.H_MLA):
            self.load_rows(self.a_q[0], QN, h * 128, 128)
            self.load_rows(self.a_q[1], QR, h * 64, 64)
            self.load_rows(self.a_k[0], KN, h * 128, 128)
            self.load_rows(self.a_vf, VM, h * 128, 128)
            self.v_to_tokmajor()
            for q0 in range(0, cfg.NT, 128):
                nk = cfg.C if q0 < cfg.C else cfg.NT
                sm = self.softmax_pv([(self.a_q[0], 0, 128), (self.a_q[1], 0, 64)],
                                     [(self.a_k[0], 0, 128), (self.a_k[1], 0, 64)], q0, nk, scale, self.a_pO[0])
                ob = self.a_ob.get()
                P.op("scalar", lambda e: e.activation(ob.t[:, :], self.a_pO[0].t[:, 0:128], ACT.Copy, scale=sm.t[:, 3:4]),
                     reads=[self.a_pO[0], sm], writes=[ob])
                self.out_transpose(ob, q0)
            P.dma("gpsimd", Y, Y.t[yrow0 + h * 128:yrow0 + (h + 1) * 128, 0:cfg.NT], self.a_y, self.a_y.t[:, 0:cfg.NT])

    def diff_attention(self, DQ, DK, DV, Y, yrow0, lam_col, gtile, lam_init):
        P = self.P
        cfg = self.cfg
        scale = 64 ** -0.5
        for h in range(cfg.H_DF):
            self.load_rows(self.a_q[0], DQ, h * 128, 128)
            self.load_rows(self.a_k[0], DK, h * 128, 128)
            self.load_rows(self.a_vf, DV, h * 128, 128)
            self.v_to_tokmajor()
            for q0 in range(0, cfg.NT, 128):
                nk = cfg.C if q0 < cfg.C else cfg.NT
                sm1 = self.softmax_pv([(self.a_q[0], 0, 64)], [(self.a_k[0], 0, 64)], q0, nk, scale, self.a_pO[0])
                sm2 = self.softmax_pv([(self.a_q[0], 64, 128)], [(self.a_k[0], 64, 128)], q0, nk, scale, self.a_pO[1])
                o1 = self.a_o.get()
                P.op("scalar", lambda e: e.activation(o1.t[:, :], self.a_pO[0].t[:, 0:128], ACT.Copy, scale=sm1.t[:, 3:4]),
                     reads=[self.a_pO[0], sm1], writes=[o1])
                P.op("vector", lambda e: e.tensor_tensor(sm2.t[:, 4:5], sm2.t[:, 3:4], lam_col.t[:, 0:1], ALU.mult),
                     reads=[sm2, lam_col], writes=[sm2])
                o = self.a_o.get()
                P.op("vector", lambda e: e.scalar_tensor_tensor(o.t[:, :], self.a_pO[1].t[:, 0:128], sm2.t[:, 4:5],
                                                                o1.t[:, :], ALU.mult, ALU.add),
                     reads=[self.a_pO[1], sm2, o1], writes=[o])
                sq = self.a_o.get()
                P.op("vector", lambda e: e.tensor_tensor(sq.t[:, :], o.t[:, :], o.t[:, :], ALU.mult), reads=[o], writes=[sq])
                P.op("vector", lambda e: e.reduce_sum(sm2.t[:, 5:6], sq.t[:, :], AX.X), reads=[sq], writes=[sm2])
                P.op("scalar", lambda e: e.activation(sm2.t[:, 6:7], sm2.t[:, 5:6], ACT.Sqrt, bias=self.eps_t.t[:, 0:1],
                                                      scale=1.0 / 128), reads=[sm2, self.eps_t], writes=[sm2])
                P.op("vector", lambda e: e.reciprocal(sm2.t[:, 6:7], sm2.t[:, 6:7]), reads=[sm2], writes=[sm2])
                ob = self.a_ob.get()
                if 0 == 1:
                    P.op("vector", lambda e: e.tensor_copy(ob.t[:, :], o.t[:, :]), reads=[o], writes=[ob])
                elif 0 == 3:
                    P.op("vector", lambda e: e.tensor_scalar(ob.t[:, :], o.t[:, :], sm2.t[:, 6:7], None, ALU.mult), reads=[o, sm2], writes=[ob])
                elif 0 == 4:
                    P.op("vector", lambda e: e.tensor_tensor(ob.t[:, :], o.t[:, :], gtile.t[:, :], ALU.mult), reads=[o, gtile], writes=[ob])
                elif 0 == 5:
                    P.op("scalar", lambda e: e.activation(ob.t[:, :], self.a_pO[1].t[:, 0:128], ACT.Copy, scale=sm2.t[:, 3:4]),
                         reads=[self.a_pO[1], sm2], writes=[ob])
                elif 0 == 6:
                    P.op("scalar", lambda e: e.activation(ob.t[:, :], self.a_pO[1].t[:, 0:128], ACT.Copy, scale=sm2.t[:, 4:5]),
                         reads=[self.a_pO[1], sm2], writes=[ob])
                elif 0 == 7:
                    P.op("vector", lambda e: e.tensor_copy(ob.t[:, 0:2], lam_col.t[:, 0:2]), reads=[lam_col], writes=[ob])
                    P.op("vector", lambda e: e.tensor_copy(ob.t[:, 2:8], sm2.t[:, 2:8]), reads=[sm2], writes=[ob])
                elif 0 == 2:
                    P.op("vector", lambda e: e.tensor_copy(ob.t[:, :], o1.t[:, :]), reads=[o1], writes=[ob])
                else:
                    P.op("vector", lambda e: e.scalar_tensor_tensor(ob.t[:, :], o.t[:, :], sm2.t[:, 6:7], gtile.t[:, :],
                                                                    ALU.mult, ALU.mult), reads=[o, sm2, gtile], writes=[ob])
                self.out_transpose(ob, q0)
            P.dma("gpsimd", Y, Y.t[yrow0 + h * 128:yrow0 + (h + 1) * 128, 0:cfg.NT], self.a_y, self.a_y.t[:, 0:cfg.NT])

    def alloc_pad_bufs(self, nload, nwork):
        P = self.P
        NP = self.cfg.NT + 4
        self.NP = NP
        self.lpool = Pool([P.sb("pl%d" % i, [128, NP], F32) for i in range(nload)])
        for t in self.lpool.items:
            P.op("gpsimd", lambda e, t=t: e.memset(t.t[:, :], 0.0), writes=[t])
        self.wk = Pool([P.sb("pw%d" % i, [128, NP], F32) for i in range(nwork)])
        self.wkb = Pool([P.sb("pwb%d" % i, [128, NP], BF16) for i in range(2)])

    def ldpad(self, src, row0, nr=128):
        cfg = self.cfg
        t = self.lpool.get()
        C, T = cfg.C, cfg.T
        self.P.dma("sync", t, t.t[0:nr, 1:1 + C], src, src.t[row0:row0 + nr, 0:C])
        self.P.dma("sync", t, t.t[0:nr, C + 3:C + 3 + T], src, src.t[row0:row0 + nr, C:C + T])
        return t

    def stpad(self, dst, row0, t, nr=128):
        cfg = self.cfg
        C, T = cfg.C, cfg.T
        self.P.dma("gpsimd", dst, dst.t[row0:row0 + nr, 0:C], t, t.t[0:nr, 1:1 + C])
        self.P.dma("gpsimd", dst, dst.t[row0:row0 + nr, C:C + T], t, t.t[0:nr, C + 3:C + 3 + T])

    def tshift(self, x, mu0, mu1, o):
        P = self.P
        NP = self.NP
        cur, prev, nxt = x.t[:, 1:NP - 1], x.t[:, 0:NP - 2], x.t[:, 2:NP]
        d1 = self.wk.get()
        P.op("vector", lambda e: e.tensor_tensor(d1.t[:, 1:NP - 1], prev, cur, ALU.subtract), reads=[x], writes=[d1])
        d2 = self.wk.get()
        P.op("gpsimd", lambda e: e.tensor_tensor(d2.t[:, 1:NP - 1], nxt, cur, ALU.subtract), reads=[x], writes=[d2])
        P.op("vector", lambda e: e.scalar_tensor_tensor(d1.t[:, 1:NP - 1], d1.t[:, 1:NP - 1], mu0, cur, ALU.mult, ALU.add),
             reads=[d1, x, self.PVt], writes=[d1])
        P.op("vector", lambda e: e.scalar_tensor_tensor(o.t[:, 1:NP - 1], d2.t[:, 1:NP - 1], mu1, d1.t[:, 1:NP - 1],
                                                        ALU.mult, ALU.add), reads=[d1, d2, self.PVt], writes=[o])
        return o

    def blocksum(self, src, dst_fn):
        P = self.P
        NP = self.NP
        for c0 in range(1, NP - 1, 512):
            cs = min(512, NP - 1 - c0)
            ps = self.pspool.get()
            P.op("tensor", lambda e: e.matmul(ps.t[:, 0:cs], self.blk1.t[:, :], src.t[:, c0:c0 + cs], start=True, stop=True),
                 reads=[self.blk1, src], writes=[ps])
            dst_fn(ps, c0, cs)

    def rwkv_prep_a(self, P32, D_):
        P = self.P
        cfg = self.cfg
        NP = self.NP
        BC = cfg.BC
        pv = self.pv
        col = lambda name, c: self.PVt.t[:, pv[name][0] + c:pv[name][0] + c + 1]
        segs = [("wd", 0, ACT.Tanh), ("ad", 0, None), ("gd", 0, ACT.Sigmoid), ("gd", 128, ACT.Sigmoid)]
        for i, (sn, off, fn) in enumerate(segs):
            x = self.ldpad(P32, cfg.seg[sn][0] + off)
            s = self.tshift(x, col("mu0", 3 * BC + i), col("mu1", 3 * BC + i), self.wk.get())
            ob = self.wkb.get()
            if fn is None:
                P.op("vector", lambda e: e.tensor_copy(ob.t[:, 1:NP - 1], s.t[:, 1:NP - 1]), reads=[s], writes=[ob])
            else:
                P.op("scalar", lambda e, fn=fn: e.activation(ob.t[:, 1:NP - 1], s.t[:, 1:NP - 1], fn), reads=[s], writes=[ob])
            self.stpad(D_["LORA16"], i * 128, ob)

    def rwkv_prep_b(self, wfull, D_):
        cfg = self.cfg
        pv = self.pv
        col = lambda name, c: self.PVt.t[:, pv[name][0] + c:pv[name][0] + c + 1]
        for d in range(2):
            self.linear(D_["LORA16"], 64, wfull, "w2_%d" % d, cfg.BW,
                        self.epi_store(dst32=D_["LW%d" % d], func=ACT.Sigmoid,
                                       bias_fn=lambda m0, msz, n0, d=d: col("w0_%d" % d, m0 // 128)[0:msz], mul=-0.6065306597126334),
                        row0=d * 64)
            self.linear(D_["LORA16"], 64, wfull, "a2_%d" % d, cfg.BW,
                        self.epi_store(dst32=D_["A%d" % d], func=ACT.Sigmoid,
                                       bias_fn=lambda m0, msz, n0, d=d: col("a0_%d" % d, m0 // 128)[0:msz]),
                        row0=128 + d * 64)
        self.linear(D_["LORA16"], 256, wfull, "g2", cfg.BW, self.epi_store(dst32=D_["G"]), row0=256)

    def rwkv_prep_c(self, P32, D_):
        P = self.P
        cfg = self.cfg
        NP = self.NP
        BC = cfg.BC
        pv = self.pv
        PVt = self.PVt
        col = lambda name, c: PVt.t[:, pv[name][0] + c:pv[name][0] + c + 1]
        L = {n: P.sb("rp_" + n, [128, NP], F32) for n in ("r", "k", "v", "kk", "kd0", "kd1")}
        for c in range(BC):
            xr = self.ldpad(P32, cfg.seg["r"][0] + c * 128)
            r_s = self.tshift(xr, col("mu0", c), col("mu1", c), L["r"])
            self.stpad(D_["R"], c * 128, r_s)
            xk = self.ldpad(P32, cfg.seg["k"][0] + c * 128)
            k_s = self.tshift(xk, col("mu0", BC + c), col("mu1", BC + c), L["k"])
            xv = self.ldpad(P32, cfg.seg["v"][0] + c * 128)
            v_s = self.tshift(xv, col("mu0", 2 * BC + c), col("mu1", 2 * BC + c), L["v"])
            self.stpad(D_["V"], c * 128, v_s)
            kkr = self.wk.get()
            P.op("vector", lambda e: e.tensor_scalar(kkr.t[:, 1:NP - 1], k_s.t[:, 1:NP - 1], col("kk", c), None, ALU.mult),
                 reads=[k_s, PVt], writes=[kkr])
            sq = self.wk.get()
            P.op("gpsimd", lambda e: e.tensor_tensor(sq.t[:, 1:NP - 1], kkr.t[:, 1:NP - 1], kkr.t[:, 1:NP - 1], ALU.mult),
                 reads=[kkr], writes=[sq])
            rn = self.wk.get()

            def f1(ps, c0, cs):
                P.op("scalar", lambda e: e.activation(rn.t[:, c0:c0 + cs], ps.t[:, 0:cs], ACT.Sqrt, bias=self.eps12.t[:, 0:1]),
                     reads=[ps, self.eps12], writes=[rn])
            self.blocksum(sq, f1)
            P.op("vector", lambda e: e.reciprocal(rn.t[:, 1:NP - 1], rn.t[:, 1:NP - 1]), reads=[rn], writes=[rn])
            kk = L["kk"]
            P.op("vector", lambda e: e.tensor_tensor(kk.t[:, 1:NP - 1], kkr.t[:, 1:NP - 1], rn.t[:, 1:NP - 1], ALU.mult),
                 reads=[kkr, rn], writes=[kk])
            self.stpad(D_["KK"], c * 128, kk)
            kds = []
            for d in range(2):
                a = self.ldpad(D_["A%d" % d], c * 128)
                t = self.wk.get()
                P.op("vector", lambda e: e.tensor_scalar(t.t[:, 1:NP - 1], a.t[:, 1:NP - 1], col("ka", c), col("c1", c), ALU.mult, ALU.add),
                     reads=[a, PVt], writes=[t])
                kd = L["kd%d" % d]
                P.op("vector", lambda e: e.tensor_tensor(kd.t[:, 1:NP - 1], k_s.t[:, 1:NP - 1], t.t[:, 1:NP - 1], ALU.mult),
                     reads=[k_s, t], writes=[kd])
                self.stpad(D_["KD%d" % d], c * 128, kd)
                kds.append(kd)
                nb = self.wk.get()
                P.op("vector", lambda e: e.scalar_tensor_tensor(nb.t[:, 1:NP - 1], kk.t[:, 1:NP - 1], -1.0, a.t[:, 1:NP - 1],
                                                                ALU.mult, ALU.mult), reads=[kk, a], writes=[nb])
                self.stpad(D_["NB%d" % d], c * 128, nb)
                lw = self.ldpad(D_["LW%d" % d], c * 128)
                w = self.wk.get()
                P.op("scalar", lambda e: e.activation(w.t[:, 1:NP - 1], lw.t[:, 1:NP - 1], ACT.Exp), reads=[lw], writes=[w])
                self.stpad(D_["W%d" % d], c * 128, w)
            s_ = self.wk.get()
            P.op("gpsimd", lambda e: e.tensor_tensor(s_.t[:, 1:NP - 1], kds[0].t[:, 1:NP - 1], kds[1].t[:, 1:NP - 1], ALU.add),
                 reads=kds, writes=[s_])
            pr = self.wk.get()
            P.op("vector", lambda e: e.scalar_tensor_tensor(pr.t[:, 1:NP - 1], s_.t[:, 1:NP - 1], col("rk", c), r_s.t[:, 1:NP - 1],
                                                            ALU.mult, ALU.mult), reads=[s_, r_s, PVt], writes=[pr])
            bon = self.wk.get()

            def f2(ps, c0, cs):
                P.op("vector", lambda e: e.tensor_tensor(bon.t[:, c0:c0 + cs], ps.t[:, 0:cs], v_s.t[:, c0:c0 + cs], ALU.mult),
                     reads=[ps, v_s], writes=[bon])
            self.blocksum(pr, f2)
            self.stpad(D_["BON"], c * 128, bon)

    def rwkv_scan(self, D_, TC=32):
        P = self.P
        cfg = self.cfg
        G = cfg.BC
        GV = G * 64
        H = cfg.H_RW
        C, NT = cfg.C, cfg.NT
        S = []
        for d in range(2):
            s = {}
            s["M"] = P.sb("sM%d" % d, [128, GV], F32)
            P.op("vector", lambda e, s=s: e.memset(s["M"].t[:, :], 0.0), writes=[s["M"]])
            for nm in ("KK", "R", "KD", "NB", "W", "V"):
                s[nm] = P.sb("s%s%d" % (nm, d), [128, G, TC], F32)
            s["LK"] = P.sb("sLK%d" % d, [128, TC, 16 * ((H + 15) // 16)], F32)
            s["LR"] = P.sb("sLR%d" % d, [128, TC, 16 * ((H + 15) // 16)], F32)
            for nm in ("LK", "LR"):
                P.op("gpsimd", lambda e, t=s[nm]: e.memset(t.t[:, :, :], 0.0), writes=[s[nm]])
            s["VT"] = P.sb("sVT%d" % d, [TC, G * 128], F32)
            s["SA"] = P.sb("sSA%d" % d, [16, GV], F32)
            s["T1"] = P.sb("sT1%d" % d, [128, GV], F32)
            s["T2"] = P.sb("sT2%d" % d, [128, GV], F32)
            s["OM"] = P.sb("sOM%d" % d, [16, 8, GV], F32)
            s["OR"] = P.sb("sOR%d" % d, [16, TC, 64], F32)
            s["pA"] = P.ps("pA%d" % d, [128, 512], F32)
            s["pB"] = P.ps("pB%d" % d, [128, 512], F32)
            s["pS"] = P.ps("pS%d" % d, [128, 512], F32)
            s["pV"] = P.ps("pV%d" % d, [128, 512], F32)
            S.append(s)
        HP = H
        assert HP <= 16 and GV <= 512

        def chunks(d):
            out = []
            for a, b in ((0, C), (C, NT)):
                cs = [(t0, min(TC, b - t0)) for t0 in range(a, b, TC)]
                if d == 1:
                    cs = cs[::-1]
                out += cs
            return out
        ch = [chunks(0), chunks(1)]
        src = [dict(KK="KK", R="R", KD="KD0", NB="NB0", W="W0", V="V"), dict(KK="KK", R="R", KD="KD1", NB="NB1", W="W1", V="V")]

        def bcast(t, ti):
            return t.t[:, :, ti:ti + 1].broadcast_to([128, G, 64])

        def load_chunk(d, t0, tn):
            s = S[d]
            for nm in ("KK", "R", "KD", "NB", "W", "V"):
                dr = D_[src[d][nm]]
                P.dma("sync", s[nm], s[nm].t[:, :, 0:tn], dr, dr.t[:, t0:t0 + tn].rearrange("(g p) t -> p g t", p=128))
            for nm, L in (("KK", "LK"), ("R", "LR")):
                for par in range(2):
                    lo = par * 64
                    outv = s[L].t[lo:lo + 64, 0:tn, 0:2 * G].rearrange("p t (g two) -> p t g two", two=2)[:, :, :, par]
                    inv = s[nm].t[lo:lo + 64, :, 0:tn].rearrange("p g t -> p t g")
                    P.op("gpsimd", lambda e, outv=outv, inv=inv: e.tensor_copy(outv, inv), reads=[s[nm]], writes=[s[L]])
            for g0 in range(0, G, 4):
                nb = min(4, G - g0)
                for j in range(nb):
                    g = g0 + j
                    P.op("tensor", lambda e, g=g, j=j: e.matmul(s["pB"].t[0:tn, j * 128:(j + 1) * 128], s["V"].t[:, g, 0:tn],
                                                                self.ident32.t[:, :], start=True, stop=True),
                         reads=[s["V"], self.ident32], writes=[s["pB"]])
                self.evac("scalar", s["VT"].t[0:tn, g0 * 128:(g0 + nb) * 128], s["pB"].t[0:tn, 0:nb * 128], [s["pB"]], [s["VT"]])

        def mm_sa(d, ti):
            s = S[d]
            P.op("tensor", lambda e: e.matmul(s["pA"].t[0:HP, 0:GV], s["LK"].t[:, ti, 0:HP], s["M"].t[:, :], start=True, stop=True),
                 reads=[s["LK"], s["M"]], writes=[s["pA"]])

        def mm_o(d, ti):
            s = S[d]
            P.op("tensor", lambda e: e.matmul(s["pB"].t[0:HP, 0:GV], s["LR"].t[:, ti, 0:HP], s["M"].t[:, :], start=True, stop=True),
                 reads=[s["LR"], s["M"]], writes=[s["pB"]])
            P.op("vector", lambda e: e.tensor_tensor(s["OM"].t[0:HP, ti % 8, :], s["pB"].t[0:HP, 0:GV], self.mask16.t[0:HP, 0:GV], ALU.mult),
                 reads=[s["pB"], self.mask16], writes=[s["OM"]])
            if (d == 0 and ti % 8 == 7) or (d == 1 and ti % 8 == 0):
                t8 = (ti // 8) * 8
                P.op("vector", lambda e: e.tensor_reduce(
                    s["OR"].t[0:HP, t8:t8 + 8, :], s["OM"].t[0:HP, 0:8, :].rearrange("p t (g v) -> p t v g", v=64), AX.X, ALU.add),
                    reads=[s["OM"]], writes=[s["OR"]])

        def step(d, ti, tn, prev):
            s = S[d]
            M3 = s["M"].t[:, :].rearrange("p (g v) -> p g v", v=64)
            mm_sa(d, ti)
            P.op("vector", lambda e: e.tensor_tensor(s["SA"].t[0:HP, :], s["pA"].t[0:HP, 0:GV], self.mask16.t[0:HP, 0:GV], ALU.mult),
                 reads=[s["pA"], self.mask16], writes=[s["SA"]])
            if prev is not None:
                mm_o(d, prev)
            for par in range(2):
                rhs = s["VT"].t[0:tn, :].rearrange("t (g q v) -> t g q v", q=2, v=64)[:, :, par, :]
                P.op("tensor", lambda e, par=par, rhs=rhs: e.matmul(
                    s["pV"].t[par * 64:(par + 1) * 64, 0:GV].rearrange("p (g v) -> p g v", v=64),
                    self.onehot.t[0:tn, ti, :], rhs, start=True, stop=True),
                    reads=[self.onehot, s["VT"]], writes=[s["pV"]])
            P.op("vector", lambda e: e.tensor_tensor(s["T2"].t[:, :].rearrange("p (g v) -> p g v", v=64),
                                                     s["pV"].t[:, 0:GV].rearrange("p (g v) -> p g v", v=64),
                                                     bcast(s["KD"], ti), ALU.mult), reads=[s["pV"], s["KD"]], writes=[s["T2"]])
            P.op("tensor", lambda e: e.matmul(s["pS"].t[:, 0:GV], self.esel.t[0:HP, :], s["SA"].t[0:HP, :], start=True, stop=True),
                 reads=[self.esel, s["SA"]], writes=[s["pS"]])
            P.op("vector", lambda e: e.tensor_tensor(s["T1"].t[:, :].rearrange("p (g v) -> p g v", v=64),
                                                     s["pS"].t[:, 0:GV].rearrange("p (g v) -> p g v", v=64),
                                                     bcast(s["NB"], ti), ALU.mult), reads=[s["pS"], s["NB"]], writes=[s["T1"]])
            P.op("gpsimd", lambda e: e.tensor_tensor(M3, M3, bcast(s["W"], ti), ALU.mult), reads=[s["M"], s["W"]], writes=[s["M"]])
            P.op("gpsimd", lambda e: e.tensor_tensor(s["M"].t[:, :], s["M"].t[:, :], s["T2"].t[:, :], ALU.add),
                 reads=[s["M"], s["T2"]], writes=[s["M"]])
            P.op("vector", lambda e: e.tensor_tensor(s["M"].t[:, :], s["M"].t[:, :], s["T1"].t[:, :], ALU.add),
                 reads=[s["M"], s["T1"]], writes=[s["M"]])

        def flush(d, t0, tn):
            s = S[d]
            assert tn % 8 == 0
            dst = D_["O%d" % d]
            P.dma("gpsimd", dst, dst.t[t0:t0 + tn, :].rearrange("t (h v) -> h t v", v=64), s["OR"], s["OR"].t[0:HP, 0:tn, :])

        nch = len(ch[0])
        for ci in range(nch):
            for d in range(2):
                t0, tn = ch[d][ci]
                load_chunk(d, t0, tn)
            order = [list(range(ch[0][ci][1])), list(range(ch[1][ci][1]))[::-1]]
            for si in range(max(len(order[0]), len(order[1]))):
                for d in range(2):
                    if si < len(order[d]):
                        step(d, order[d][si], ch[d][ci][1], order[d][si - 1] if si > 0 else None)
            for d in range(2):
                mm_o(d, order[d][-1])
            for d in range(2):
                flush(d, *ch[d][ci])

    def rwkv_readout(self, D_, Y, yrow0, lng, lnb):
        P = self.P
        cfg = self.cfg
        BW, H = cfg.BW, cfg.H_RW
        tk = Pool([P.sb("ro%d" % i, [128, BW], F32) for i in range(5)])
        sm = Pool([P.sb("rs%d" % i, [128, H], F32) for i in range(4)])
        fm = Pool([P.sb("rf%d" % i, [128, 128], F32) for i in range(4)])
        fmb = Pool([P.sb("rfb%d" % i, [128, 128], BF16) for i in range(2)])
        for t0 in range(0, cfg.NT, 128):
            a = tk.get()
            b = tk.get()
            P.dma("sync", a, a.t[:, :], D_["O0"], D_["O0"].t[t0:t0 + 128, :])
            P.dma("sync", b, b.t[:, :], D_["O1"], D_["O1"].t[t0:t0 + 128, :])
            P.op("vector", lambda e: e.tensor_tensor(a.t[:, :], a.t[:, :], b.t[:, :], ALU.add), reads=[a, b], writes=[a])
            a3 = a.t[:, :].rearrange("p (h v) -> p h v", v=64)
            mean = sm.get()
            P.op("vector", lambda e: e.tensor_reduce(mean.t[:, :], a3, AX.X, ALU.add), reads=[a], writes=[mean])
            P.op("vector", lambda e: e.tensor_scalar(mean.t[:, :], mean.t[:, :], -1.0 / 64, None, ALU.mult), reads=[mean], writes=[mean])
            P.op("vector", lambda e: e.tensor_tensor(a3, a3, mean.t[:, :].unsqueeze(2).broadcast_to([128, H, 64]), ALU.add),
                 reads=[a, mean], writes=[a])
            sq = tk.get()
            P.op("gpsimd", lambda e: e.tensor_tensor(sq.t[:, :], a.t[:, :], a.t[:, :], ALU.mult), reads=[a], writes=[sq])
            var = sm.get()
            P.op("vector", lambda e: e.tensor_reduce(var.t[:, :], sq.t[:, :].rearrange("p (h v) -> p h v", v=64), AX.X, ALU.add),
                 reads=[sq], writes=[var])
            P.op("scalar", lambda e: e.activation(var.t[:, :], var.t[:, :], ACT.Sqrt, bias=self.epsgn.t[:, 0:1], scale=1.0 / 64),
                 reads=[var, self.epsgn], writes=[var])
            P.op("vector", lambda e: e.reciprocal(var.t[:, :], var.t[:, :]), reads=[var], writes=[var])
            P.op("vector", lambda e: e.tensor_tensor(a3, a3, var.t[:, :].unsqueeze(2).broadcast_to([128, H, 64]), ALU.mult),
                 reads=[a, var], writes=[a])
            P.op("vector", lambda e: e.tensor_tensor(a.t[:, :], a.t[:, :], lng.t[:, :], ALU.mult), reads=[a, lng], writes=[a])
            P.op("gpsimd", lambda e: e.tensor_tensor(a.t[:, :], a.t[:, :], lnb.t[:, :], ALU.add), reads=[a, lnb], writes=[a])
            for c in range(cfg.BC):
                ps = self.pspool.get()
                P.op("tensor", lambda e, c=c: e.matmul(ps.t[:, 0:128], a.t[:, c * 128:(c + 1) * 128], self.ident32.t[:, :],
                                                       start=True, stop=True), reads=[a, self.ident32], writes=[ps])
                bon = fm.get()
                g = fm.get()
                P.dma("sync", bon, bon.t[:, :], D_["BON"], D_["BON"].t[c * 128:(c + 1) * 128, t0:t0 + 128])
                P.dma("sync", g, g.t[:, :], D_["G"], D_["G"].t[c * 128:(c + 1) * 128, t0:t0 + 128])
                P.op("vector", lambda e: e.tensor_tensor(bon.t[:, :], ps.t[:, 0:128], bon.t[:, :], ALU.add), reads=[ps, bon], writes=[bon])
                ob = fmb.get()
                P.op("vector", lambda e: e.tensor_tensor(ob.t[:, :], bon.t[:, :], g.t[:, :], ALU.mult), reads=[bon, g], writes=[ob])
                P.dma("gpsimd", Y, Y.t[yrow0 + c * 128:yrow0 + (c + 1) * 128, t0:t0 + 128], ob, ob.t[:, :])

    def conv(self, P32, Y, yrow0):
        P = self.P
        cfg = self.cfg
        NP = self.NP
        pv = self.pv
        col = lambda name, c: self.PVt.t[:, pv[name][0] + c:pv[name][0] + c + 1]
        for c in range(cfg.BC):
            xb = self.ldpad(P32, cfg.seg["cb"][0] + c * 128)
            xc = self.ldpad(P32, cfg.seg["cc"][0] + c * 128)
            xu = self.ldpad(P32, cfg.seg["cu"][0] + c * 128)
            z = self.wk.get()
            P.op("vector", lambda e: e.tensor_tensor(z.t[:, :], xc.t[:, :], xu.t[:, :], ALU.mult), reads=[xc, xu], writes=[z])
            y = self.wk.get()
            P.op("vector", lambda e: e.tensor_scalar(y.t[:, 1:NP - 1], z.t[:, 0:NP - 2], col("cw0", c), None, ALU.mult),
                 reads=[z, self.PVt], writes=[y])
            P.op("vector", lambda e: e.scalar_tensor_tensor(y.t[:, 1:NP - 1], z.t[:, 1:NP - 1], col("cw1", c), y.t[:, 1:NP - 1],
                                                            ALU.mult, ALU.add), reads=[z, y, self.PVt], writes=[y])
            P.op("vector", lambda e: e.scalar_tensor_tensor(y.t[:, 1:NP - 1], z.t[:, 2:NP], col("cw2", c), y.t[:, 1:NP - 1],
                                                            ALU.mult, ALU.add), reads=[z, y, self.PVt], writes=[y])
            ob = self.wkb.get()
            P.op("vector", lambda e: e.tensor_tensor(ob.t[:, 1:NP - 1], y.t[:, 1:NP - 1], xb.t[:, 1:NP - 1], ALU.mult),
                 reads=[y, xb], writes=[ob])
            self.stpad(Y, yrow0 + c * 128, ob)

    def merge(self, GL, Y, wfull, ACC16):
        P = self.P
        cfg = self.cfg
        pv = self.pv
        BWC = cfg.BWG // 128
        BWL = cfg.BW // 128
        nmax = max(128, min(512, (32768 // (2 + 4 * BWC)) // 128 * 128))
        sub = []
        for a_, ns_ in cfg.tiles:
            o_ = 0
            while o_ < ns_:
                s_ = min(nmax, ns_ - o_)
                sub.append((a_ + o_, s_))
                o_ += s_
        for n0, nsz in sub:
            xt = self.xpool.get()
            flat = xt.t
            P.dma("sync", xt, flat[:, 0:2 * nsz].rearrange("p (kc n) -> p kc n", n=nsz),
                  GL, GL.t[0:256, n0:n0 + nsz].rearrange("(kc p) n -> p kc n", p=128))
            for rk in range(cfg.split):
                for i in range(4):
                    for kl in range(BWL):
                        s0 = 2 + i * BWC + rk * BWL + kl
                        r0 = ((i * BWL + kl) * cfg.split + rk) * 128
                        P.dma("sync", xt, flat[:, s0 * nsz:(s0 + 1) * nsz], Y, Y.t[r0:r0 + 128, n0:n0 + nsz])
            for mc in range(cfg.DC):
                acc = self.accpool.get()
                for i in range(4):
                    wg = self.wpool.get()
                    P.dma("sync", wg, wg.t[:, 0:256], wfull[0], self.wview(wfull, "gu_%d" % i, mc)[:, 0:256])
                    P.dma("sync", wg, wg.t[:, 256:256 + BWC * 128], wfull[0], self.wview(wfull, "wb_%d" % i, mc)[:, 0:BWC * 128])
                    pg = self.pspool.get()
                    for kc in range(2):
                        P.op("tensor", lambda e, kc=kc: e.matmul(pg.t[:, 0:nsz], wg.t[:, kc * 128:(kc + 1) * 128],
                                                                 flat[:, kc * nsz:(kc + 1) * nsz], start=(kc == 0), stop=(kc == 1)),
                             reads=[wg, xt], writes=[pg])
                    gt = self.opool.get()
                    gb = self.PVt.t[:, pv["gb_%d" % i][0] + mc:pv["gb_%d" % i][0] + mc + 1]
                    P.op("scalar", lambda e: e.activation(gt.t[:, 0:nsz], pg.t[:, 0:nsz], ACT.Sigmoid, bias=gb),
                         reads=[pg, self.PVt], writes=[gt])
                    pb = self.pspool.get()
                    for kc in range(BWC):
                        xc = 2 + i * BWC + kc
                        P.op("tensor", lambda e, kc=kc, xc=xc: e.matmul(pb.t[:, 0:nsz], wg.t[:, 256 + kc * 128:256 + (kc + 1) * 128],
                                                                        flat[:, xc * nsz:(xc + 1) * nsz], start=(kc == 0), stop=(kc == BWC - 1)),
                             reads=[wg, xt], writes=[pb])
                    if i == 0:
                        P.op("vector", lambda e: e.tensor_tensor(acc.t[:, 0:nsz], pb.t[:, 0:nsz], gt.t[:, 0:nsz], ALU.mult),
                             reads=[pb, gt], writes=[acc])
                    else:
                        P.op("vector", lambda e: e.tensor_tensor(gt.t[:, 0:nsz], pb.t[:, 0:nsz], gt.t[:, 0:nsz], ALU.mult),
                             reads=[pb, gt], writes=[gt])
                        P.op("gpsimd", lambda e: e.tensor_tensor(acc.t[:, 0:nsz], acc.t[:, 0:nsz], gt.t[:, 0:nsz], ALU.add),
                             reads=[acc, gt], writes=[acc])
                ob = self.opoolb.get()
                P.op("vector", lambda e: e.tensor_copy(ob.t[:, 0:nsz], acc.t[:, 0:nsz]), reads=[acc], writes=[ob])
                P.dma("gpsimd", ACC16, ACC16.t[mc * 128:(mc + 1) * 128, n0:n0 + nsz], ob, ob.t[:, 0:nsz])

    def epi_residual(self, xold, xnew, j):
        P = self.P
        cfg = self.cfg

        def epi(ps, m0, msz, n0, nsz):
            ci = cfg.cond_of(n0)
            xt = self.ld(self.opool, xold, m0, msz, n0, nsz)
            mcol = self.modS.t[0:msz, j * cfg.DC + m0 // 128, ci:ci + 1]
            P.op("vector", lambda e: e.scalar_tensor_tensor(xt.t[0:msz, 0:nsz], ps.t[0:msz, 0:nsz], mcol, xt.t[0:msz, 0:nsz],
                                                            ALU.mult, ALU.add), reads=[ps, xt, self.modS], writes=[xt])
            self.st(xnew, m0, msz, n0, nsz, xt)
        return epi

    def modulation(self, COND16, wfull, T16):
        P = self.P
        cfg = self.cfg
        pv = self.pv
        self.linear(COND16, cfg.D, wfull, "mod_down", cfg.MR, self.epi_store(dst16=T16), tiles=[(0, 2)])

        def epi(ps, m0, msz, n0, nsz):
            mc = m0 // 128
            b = self.PVt.t[:, pv["modb"][0] + mc:pv["modb"][0] + mc + 1]
            P.op("scalar", lambda e: e.activation(self.modS.t[:, mc, 0:2], ps.t[:, 0:2], ACT.Identity, bias=b),
                 reads=[ps, self.PVt], writes=[self.modS])
        self.linear(T16, cfg.MR, wfull, "mod_up", 6 * cfg.D, epi, tiles=[(0, 2)])
        DC = cfg.DC
        for k, (gname, j) in enumerate((("n1g", 1), ("n2g", 4))):
            g = self.PVt.t[:, pv[gname][0]:pv[gname][0] + DC]
            out = self.modA.t[:, k * DC:(k + 1) * DC, :]
            P.op("vector", lambda e, out=out, j=j: e.tensor_scalar(out, self.modS.t[:, j * DC:(j + 1) * DC, :], 1.0, None, ALU.add),
                 reads=[self.modS], writes=[self.modA])
            P.op("vector", lambda e, out=out, g=g: e.tensor_tensor(out, out, g.unsqueeze(2).broadcast_to([128, DC, 2]), ALU.mult),
                 reads=[self.modA, self.PVt], writes=[self.modA])

import math
import numpy as np
import concourse.bass as bass
import concourse.mybir as mybir


TC_SCAN = 32


def pv_layout(cfg):
    DC, BC = cfg.DC, cfg.BC
    QC, KVC = pad128(cfg.QL) // 128, pad128(cfg.KVL) // 128
    RC = 3 * BC + 4
    items = [("n1g", DC), ("n2g", DC), ("modb", 6 * DC), ("qg", QC), ("kvg", KVC), ("mu0", RC), ("mu1", RC),
             ("w0_0", BC), ("w0_1", BC), ("a0_0", BC), ("a0_1", BC), ("kk", BC), ("ka", BC), ("c1", BC), ("rk", BC),
             ("cw0", BC), ("cw1", BC), ("cw2", BC), ("gb_0", DC), ("gb_1", DC), ("gb_2", DC), ("gb_3", DC)]
    pv = {}
    o = 0
    for n, k in items:
        pv[n] = (o, k)
        o += k
    return pv, o


def build_program(cfg, debug_outs=()):
    nc = bass.Bass("TRN2", target_bir_lowering=False)
    P = Prog(nc)
    m = Model(P, cfg)
    D, NT, C, T, BW, L = cfg.D, cfg.NT, cfg.C, cfg.T, cfg.BW, cfg.DEPTH
    DC, BC = cfg.DC, cfg.BC
    H = cfg.H_MLA
    QLp, KVLp = pad128(cfg.QL), pad128(cfg.KVL)
    pv, PVN = pv_layout(cfg)
    m.pv = pv

    def inp(name, shape):
        return P.dram(name, shape, F32, kind="ExternalInput")
    xT = inp("xT", [D, NT])
    cond = inp("cond", [D, 2])
    pvec = inp("pvec", [L * 128, PVN])
    bcin = inp("bcin", [L * 128, 2 * BW + 128])
    dlin = inp("dlin", [L, 256])
    fng = inp("fng", [128, DC])
    c_ident = inp("c_ident", [128, 128])
    c_blk1 = inp("c_blk1", [128, 128])
    c_mask16 = inp("c_mask16", [16, 512])
    c_esel = inp("c_esel", [16, 128])
    c_onehot = inp("c_onehot", [TC_SCAN, TC_SCAN * 64])
    c_cos = inp("c_cos", [128, NT])
    c_sin = inp("c_sin", [128, NT])
    NG = cfg.NG
    nshard = max(1, cfg.ncores // cfg.split)
    shard_rows = [cfg.wrows[g] // nshard for g in range(NG)]
    wsh = [inp("wflat%d" % g, [L * shard_rows[g], WCOLS]) for g in range(NG)]
    yT = P.dram("yT", [D, T], F32, kind="ExternalOutput")
    dbg = {}

    wfull = []
    for l in range(L):
        grp = []
        for g in range(NG):
            wf = P.dram("wfull%d_%d" % (l, g), [cfg.wrows[g], WCOLS], BF16)
            sr = shard_rows[g]
            tgt = wf if cfg.ncores == 1 else P.dram("wsh16_%d_%d" % (l, g), [sr, WCOLS], BF16)
            items = []
            for r0 in range(0, sr, 4096):
                rn = min(4096, sr - r0)
                items.append(lambda tgt=tgt, g=g, l=l, sr=sr, r0=r0, rn=rn: P.dma(
                    "gpsimd", tgt, tgt.t[r0:r0 + rn, :], wsh[g], wsh[g].t[l * sr + r0:l * sr + r0 + rn, :]))
            if cfg.ncores > 1:
                wgroups = [list(range(h * nshard, (h + 1) * nshard)) for h in range(cfg.split)]
                for j in range(sr // GCH):
                    items.append(lambda tgt=tgt, wf=wf, j=j: P.collective(
                        "AllGather", ALU.bypass, wgroups, tgt, tgt.t[j * GCH:(j + 1) * GCH, :],
                        wf, wf.t[j * nshard * GCH:(j + 1) * nshard * GCH, :]))
            for it in items:
                it()
            grp.append(wf)
        wfull.append(grp)

    def const_tile(name, src, shape, dt=F32):
        t = P.sb(name, shape, dt)
        P.dma("sync", t, t.t[:], src, src.t[:])
        return t
    m.ident32 = const_tile("ident32", c_ident, [128, 128])
    m.blk1 = const_tile("blk1", c_blk1, [128, 128])
    m.mask16 = const_tile("mask16", c_mask16, [16, 512])
    m.esel = const_tile("esel", c_esel, [16, 128])
    oh = P.sb("onehot", [TC_SCAN, TC_SCAN, 64], F32)
    P.dma("sync", oh, oh.t[:].rearrange("p a b -> p (a b)"), c_onehot, c_onehot.t[:])
    m.onehot = oh
    m.cos2 = const_tile("cos2", c_cos, [128, NT])
    m.sin2 = const_tile("sin2", c_sin, [128, NT])
    m.identb = P.sb("identb", [128, 128], BF16)
    P.op("vector", lambda e: e.tensor_copy(m.identb.t[:], m.ident32.t[:]), reads=[m.ident32], writes=[m.identb])
    m.ones = P.sb("ones", [128, 128], F32)
    P.op("vector", lambda e: e.memset(m.ones.t[:], 1.0), writes=[m.ones])
    for nm, val in (("eps_t", NORM_EPS), ("eps12", 1e-12), ("epsgn", GN_EPS)):
        t = P.sb(nm, [128, 1], F32)
        P.op("vector", lambda e, t=t, val=val: e.memset(t.t[:], val), writes=[t])
        setattr(m, nm, t)
    fngt = const_tile("fngt", fng, [128, DC])
    m.PVt = P.sb("PVt", [128, PVN], F32)
    m.modS = P.sb("modS", [128, 6 * DC, 2], F32)
    m.modA = P.sb("modA", [128, 2 * DC, 2], F32)
    bct = P.sb("bct", [128, 2 * BW + 128], F32)
    lam_row = P.sb("lam_row", [1, 260], F32)
    lam_col = P.sb("lam_col", [128, 2], F32)
    gtile = P.sb("gtile", [128, 128], F32)

    def dr(name, shape, dt=F32):
        return P.dram(name, shape, dt)
    xres = [xT, dr("xresA", [D, NT]), dr("xresB", [D, NT])]
    hB = dr("hB", [D, NT], BF16)
    P32 = dr("P32", [cfg.PCOLS, NT])
    CQN = dr("CQN", [QLp, NT], BF16)
    CKVN = dr("CKVN", [KVLp, NT], BF16)
    Q32 = dr("Q32", [H * 128, NT])
    QN = dr("QN", [H * 128, NT], BF16)
    QR = dr("QR", [H * 64, NT], BF16)
    KN = dr("KN", [H * 128, NT], BF16)
    VM = dr("VM", [H * 128, NT], BF16)
    KR = dr("KR", [64, NT], BF16)
    DQ = dr("DQ", [BW, NT], BF16)
    DK = dr("DK", [BW, NT], BF16)
    DV = dr("DV", [BW, NT], BF16)
    Y = dr("Y", [4 * BW, NT], BF16)
    YG = dr("YG", [cfg.split * 4 * BW, NT], BF16) if cfg.split > 1 else Y
    pairs = [[c, c + cfg.ncores // 2] for c in range(cfg.ncores // 2)] if cfg.split > 1 else None
    MP = dr("MP", [D, NT]) if cfg.split > 1 else None
    MPR = dr("MPR", [D, NT]) if cfg.split > 1 else None
    RW = {n: dr("RW_" + n, [BW, NT]) for n in ("R", "V", "KK", "KD0", "KD1", "NB0", "NB1", "W0", "W1", "BON", "G",
                                               "LW0", "LW1", "A0", "A1")}
    RW["LORA16"] = dr("LORA16", [512, NT], BF16)
    RW["O0"] = dr("RW_O0", [NT, BW])
    RW["O1"] = dr("RW_O1", [NT, BW])
    GL = dr("GL", [256, NT], BF16)
    ACC16 = dr("ACC16", [D, NT], BF16)
    HID16 = dr("HID16", [cfg.DFFL, NT], BF16)
    COND16 = dr("COND16", [D, 2], BF16)
    T16 = dr("T16", [cfg.MR, 2], BF16)

    with P.scope():
        ct = P.sb("condt", [128, DC, 2], F32)
        cb = P.sb("condb", [128, DC, 2], BF16)
        P.dma("sync", ct, ct.t[:], cond, cond.t[:, :].rearrange("(c p) n -> p c n", p=128))
        P.op("scalar", lambda e: e.activation(cb.t[:], ct.t[:], ACT.Silu), reads=[ct], writes=[cb])
        P.dma("gpsimd", COND16, COND16.t[:, :].rearrange("(c p) n -> p c n", p=128), cb, cb.t[:])

    col = lambda name, c: m.PVt.t[:, pv[name][0] + c:pv[name][0] + c + 1]
    cur = 0
    for l in range(L):
        P.flush_bg(l)
        wf = wfull[l]
        lam_init = 0.8 - 0.6 * math.exp(-0.3 * l)
        P.dma("sync", m.PVt, m.PVt.t[:, :], pvec, pvec.t[l * 128:(l + 1) * 128, :])
        P.dma("sync", bct, bct.t[:, :], bcin, bcin.t[l * 128:(l + 1) * 128, :])
        P.dma("sync", lam_row, lam_row.t[0:1, 0:256], dlin, dlin.t[l:l + 1, :])
        ka = m.PVt.t[:, pv["ka"][0]:pv["ka"][0] + BC]
        P.op("vector", lambda e: e.tensor_scalar(m.PVt.t[:, pv["c1"][0]:pv["c1"][0] + BC], ka, -1.0, 1.0, ALU.mult, ALU.add),
             reads=[m.PVt], writes=[m.PVt])
        P.op("vector", lambda e: e.tensor_tensor(lam_row.t[0:1, 0:64], lam_row.t[0:1, 0:64], lam_row.t[0:1, 64:128], ALU.mult),
             reads=[lam_row], writes=[lam_row])
        P.op("vector", lambda e: e.tensor_tensor(lam_row.t[0:1, 128:192], lam_row.t[0:1, 128:192], lam_row.t[0:1, 192:256], ALU.mult),
             reads=[lam_row], writes=[lam_row])
        P.op("vector", lambda e: e.reduce_sum(lam_row.t[0:1, 256:257], lam_row.t[0:1, 0:64], AX.X), reads=[lam_row], writes=[lam_row])
        P.op("vector", lambda e: e.reduce_sum(lam_row.t[0:1, 257:258], lam_row.t[0:1, 128:192], AX.X), reads=[lam_row], writes=[lam_row])
        P.op("scalar", lambda e: e.activation(lam_row.t[0:1, 256:258], lam_row.t[0:1, 256:258], ACT.Exp), reads=[lam_row], writes=[lam_row])
        P.op("vector", lambda e: e.tensor_tensor(lam_row.t[0:1, 258:259], lam_row.t[0:1, 257:258], lam_row.t[0:1, 256:257], ALU.subtract),
             reads=[lam_row], writes=[lam_row])
        P.op("vector", lambda e: e.tensor_scalar(lam_row.t[0:1, 258:259], lam_row.t[0:1, 258:259], -lam_init, None, ALU.add),
             reads=[lam_row], writes=[lam_row])
        P.op("vector", lambda e: e.tensor_copy(lam_row.t[0:1, 259:260], lam_row.t[0:1, 258:259]), reads=[lam_row], writes=[lam_row])
        with P.scope():
            pl = P.ps("pl", [128, 512], F32)
            P.op("tensor", lambda e: e.matmul(pl.t[:, 0:2], m.ones.t[0:1, :], lam_row.t[0:1, 258:260], start=True, stop=True),
                 reads=[m.ones, lam_row], writes=[pl])
            P.op("vector", lambda e: e.tensor_copy(lam_col.t[:, 0:2], pl.t[:, 0:2]), reads=[pl], writes=[lam_col])
        P.op("vector", lambda e: e.tensor_scalar(gtile.t[:, :], bct.t[:, 2 * BW:2 * BW + 128], 1.0 - lam_init, None, ALU.mult),
             reads=[bct], writes=[gtile])
        with P.scope():
            m.alloc_linear_bufs()
            m.modulation(COND16, wf, T16)
        xin = xres[cur]
        xmid = xres[(cur + 1) % 3]
        xout = xres[(cur + 2) % 3]
        with P.scope():
            m.pspool = Pool([P.ps("ps%d" % i, [128, 512], F32) for i in range(4)])
            m.alloc_norm_bufs()
            m.rmsnorm(xin, 0, D, hB, 0, lambda c, ci: m.modA.t[:, c, ci:ci + 1], lambda c, ci: m.modS.t[:, c, ci:ci + 1],
                      deps=[m.modA, m.modS], ones=m.ones)
        with P.scope():
            m.alloc_linear_bufs()
            m.linear(hB, D, wf, "w_in", cfg.PCOLS, m.epi_store(dst32=P32))
            m.linear(hB, D, wf, "gate_down", cfg.GR, m.epi_store(dst16=GL))
        if "P32" in debug_outs and l == 0:
            dbg["P32"] = P32
        with P.scope():
            m.pspool = Pool([P.ps("ps%d" % i, [128, 512], F32) for i in range(4)])
            m.alloc_norm_bufs()
            m.rmsnorm(P32, cfg.seg["cq"][0], cfg.QL, CQN, 0, lambda c, ci: col("qg", c), deps=[m.PVt], ones=m.ones)
            m.rmsnorm(P32, cfg.seg["ckv"][0], cfg.KVL, CKVN, 0, lambda c, ci: col("kvg", c), deps=[m.PVt], ones=m.ones)
        with P.scope():
            m.alloc_linear_bufs()
            m.alloc_ew_bufs()
            e_qn = m.epi_store(dst16=QN)
            e_q32 = m.epi_store(dst32=Q32, row0=-H * 128)

            def epi_q(ps, m0, msz, n0, nsz):
                (e_qn if m0 < H * 128 else e_q32)(ps, m0, msz, n0, nsz)
            m.linear(CQN, QLp, wf, "w_uq", H * 256, epi_q)
            e_kn = m.epi_store(dst16=KN)
            e_vm = m.epi_store(dst16=VM, row0=-H * 128)

            def epi_kv(ps, m0, msz, n0, nsz):
                (e_kn if m0 < H * 128 else e_vm)(ps, m0, msz, n0, nsz)
            m.linear(CKVN, KVLp, wf, "w_ukv", H * 256, epi_kv)
            m.rope(Q32, 0, H * 64, H * 64, QR, 0)
            m.rope(P32, cfg.seg["kr"][0], cfg.seg["kr"][0] + 64, 64, KR, 0)
            m.rope(P32, cfg.seg["dq"][0], cfg.seg["dqr"][0], BW, DQ, 0)
            m.rope(P32, cfg.seg["dk"][0], cfg.seg["dkr"][0], BW, DK, 0)
            m.cast_rows(P32, cfg.seg["dv"][0], BW, DV, 0)
        with P.scope():
            m.alloc_attn_bufs()
            m.mla_attention(QN, QR, KN, KR, VM, Y, 0)
            m.diff_attention(DQ, DK, DV, Y, 3 * BW, lam_col, gtile, lam_init)
        with P.scope():
            m.alloc_pad_bufs(3, 4)
            m.rwkv_prep_a(P32, RW)
        with P.scope():
            m.alloc_linear_bufs()
            m.rwkv_prep_b(wf, RW)
        with P.scope():
            m.pspool = Pool([P.ps("ps%d" % i, [128, 512], F32) for i in range(4)])
            m.alloc_pad_bufs(6, 4)
            m.rwkv_prep_c(P32, RW)
        with P.scope():
            P.bg_on = False
            m.rwkv_scan(RW, TC=TC_SCAN)
            P.bg_on = True
        with P.scope():
            m.pspool = Pool([P.ps("ps%d" % i, [128, 512], F32) for i in range(4)])
            lng = Res("lng", bct.t[:, 0:BW])
            lnb = Res("lnb", bct.t[:, BW:2 * BW])
            lng.w = lnb.w = bct.w
            lng.r = lnb.r = bct.r
            m.rwkv_readout(RW, Y, BW, lng, lnb)
        with P.scope():
            m.alloc_pad_bufs(4, 4)
            m.conv(P32, Y, 2 * BW)
        if l == 0:
            for n_ in debug_outs:
                if n_ == "Y":
                    dbg["Y"] = Y
                if n_ in RW:
                    dbg[n_] = RW[n_]
        with P.scope():
            m.alloc_linear_bufs()
            import os as _os
            if cfg.split > 1 and cfg.ncores > 1 and not _os.environ.get("NO_YG"):
                for lc in range(4 * BW // 128):
                    P.collective("AllGather", ALU.bypass, pairs, Y, Y.t[lc * 128:(lc + 1) * 128, :],
                                 YG, YG.t[lc * 256:(lc + 1) * 256, :])
            m.merge(GL, YG, wf, ACC16)
            m.linear(ACC16, D, wf, "w_out", D, m.epi_residual(xin, xmid, 2))
        with P.scope():
            m.pspool = Pool([P.ps("ps%d" % i, [128, 512], F32) for i in range(4)])
            m.alloc_norm_bufs()
            m.rmsnorm(xmid, 0, D, hB, 0, lambda c, ci: m.modA.t[:, DC + c, ci:ci + 1], lambda c, ci: m.modS.t[:, 3 * DC + c, ci:ci + 1],
                      deps=[m.modA, m.modS], ones=m.ones)
        with P.scope():
            m.alloc_linear_bufs()
            m.linear(hB, D, wf, "mlp_w1", cfg.DFFL, m.epi_store(dst16=HID16, func=ACT.Relu, square=True))
            nmx = 512
            if cfg.split > 1:
                m.linear(HID16, cfg.DFFL, wf, "mlp_w2", D, m.epi_store(dst32=MP), nmax=nmx)
                if cfg.ncores > 1 and not _os.environ.get("NO_AR"):
                    for r0 in range(0, D, 64):
                        P.collective("AllReduce", ALU.add, pairs, MP, MP.t[r0:r0 + 64, :], MPR, MPR.t[r0:r0 + 64, :])
                for mc in range(DC):
                    for n0, nsz in cfg.tiles:
                        ci = cfg.cond_of(n0)
                        xt_ = m.ld(m.opool, xmid, mc * 128, 128, n0, nsz)
                        pt_ = m.ld(m.opool, MPR, mc * 128, 128, n0, nsz)
                        mcol = m.modS.t[:, 5 * DC + mc, ci:ci + 1]
                        P.op("vector", lambda e, xt_=xt_, pt_=pt_, mcol=mcol, nsz=nsz: e.scalar_tensor_tensor(
                            xt_.t[:, 0:nsz], pt_.t[:, 0:nsz], mcol, xt_.t[:, 0:nsz], ALU.mult, ALU.add),
                            reads=[xt_, pt_, m.modS], writes=[xt_])
                        m.st(xout, mc * 128, 128, n0, nsz, xt_)
            else:
                m.linear(HID16, cfg.DFFL, wf, "mlp_w2", D, m.epi_residual(xmid, xout, 5), nmax=nmx)
        cur = (cur + 2) % 3
        if l == 0 and "X1" in debug_outs:
            dbg["X1"] = xres[cur]
        if l == 0 and "XMID" in debug_outs:
            dbg["XMID"] = xmid
    with P.scope():
        m.pspool = Pool([P.ps("ps%d" % i, [128, 512], F32) for i in range(4)])
        m.alloc_norm_bufs(out_dt=F32)
        xt_tiles = [(a, s) for a, s in cfg.tiles if a >= C]
        m.rmsnorm(xres[cur], 0, D, yT, 0, lambda c, ci: fngt.t[:, c:c + 1], deps=[fngt], ones=m.ones, tiles=xt_tiles, dcol=C)
    outs = [yT]
    dbg_out = {}
    for n_, r in dbg.items():
        shp = list(r.t.shape)
        o = P.dram("dbg_" + n_, shp, F32, kind="ExternalOutput")
        P.dma("gpsimd", o, o.t[:, :], r, r.t[:, :])
        outs.append(o)
        dbg_out[n_] = "dbg_" + n_
    P.finish(outs)
    return nc, P, dbg_out


def tile_weight(W):
    K, M = W.shape
    KC, MC = cdiv(K, 128), cdiv(M, 128)
    Wp = np.zeros((KC * 128, MC * 128), np.float32)
    Wp[:K, :M] = W
    return np.ascontiguousarray(Wp.reshape(KC, 128, MC, 128).transpose(2, 1, 0, 3)).reshape(-1)


def pp(v, nchunks=None):
    v = np.asarray(v, np.float32).reshape(-1)
    k = cdiv(v.size, 128) if nchunks is None else nchunks
    o = np.zeros(k * 128, np.float32)
    o[:v.size] = v
    return np.ascontiguousarray(o.reshape(k, 128).T)


def rot64(idx):
    idx = np.asarray(idx).reshape(-1, 64)
    return np.concatenate([idx[:, 32:], idx[:, :32]], axis=1).reshape(-1)


def host_layer_weights(cfg, inp, l, half=0):
    D, BW, H = cfg.D, cfg.BW, cfg.H_MLA
    BWG = cfg.BWG
    ch = np.arange(half * BW, (half + 1) * BW)
    QL, KVL = cfg.QL, cfg.KVL
    w_in = inp["w_in"][l]
    o_cq, o_ckv, o_kr = 0, QL, QL + KVL
    o_rw = QL + KVL + 64
    o_r, o_k, o_v = o_rw, o_rw + BWG, o_rw + 2 * BWG
    o_wd = o_rw + 3 * BWG
    o_ad = o_wd + 128
    o_gd = o_ad + 128
    o_cv = o_gd + 160
    o_df = o_cv + 3 * BWG
    ext = np.zeros((D, cfg.PCOLS), np.float32)

    def put(name, cols):
        a, s = cfg.seg[name]
        ext[:, a:a + len(cols)] = w_in[:, cols]
    put("cq", np.arange(o_cq, o_cq + QL))
    put("ckv", np.arange(o_ckv, o_ckv + KVL))
    kr = np.arange(o_kr, o_kr + 64)
    put("kr", np.concatenate([kr, rot64(kr)]))
    put("r", o_r + ch)
    put("k", o_k + ch)
    put("v", o_v + ch)
    put("wd", np.arange(o_wd, o_wd + 128))
    put("ad", np.arange(o_ad, o_ad + 128))
    put("gd", np.arange(o_gd, o_gd + 160))
    put("cb", o_cv + ch)
    put("cc", o_cv + BWG + ch)
    put("cu", o_cv + 2 * BWG + ch)
    dq = o_df + ch
    dk = o_df + BWG + ch
    put("dq", dq)
    put("dk", dk)
    put("dv", o_df + 2 * BWG + ch)
    put("dqr", rot64(dq))
    put("dkr", rot64(dk))
    wq = inp["mla_w_uq"][l]
    hs = range(half * H, (half + 1) * H)
    qn = np.concatenate([np.arange(h * 192, h * 192 + 128) for h in hs])
    qr = np.concatenate([np.arange(h * 192 + 128, h * 192 + 192) for h in hs])
    wq_ext = wq[:, np.concatenate([qn, qr, rot64(qr)])]
    wkv = inp["mla_w_ukv"][l]
    kn = np.concatenate([np.arange(h * 256, h * 256 + 128) for h in hs])
    vv = np.concatenate([np.arange(h * 256 + 128, h * 256 + 256) for h in hs])
    wkv_ext = wkv[:, np.concatenate([kn, vv])]
    g2 = np.zeros((256, BW), np.float32)
    g2[:160] = inp["rwkv_g2"][l][:, ch]
    ws = {
        "mod_down": inp["mod_down"][l], "mod_up": inp["mod_up"][l], "w_in": ext, "w_uq": wq_ext, "w_ukv": wkv_ext,
        "w2_0": inp["rwkv_w2"][l, 0][:, ch], "w2_1": inp["rwkv_w2"][l, 1][:, ch],
        "a2_0": inp["rwkv_a2"][l, 0][:, ch], "a2_1": inp["rwkv_a2"][l, 1][:, ch],
        "g2": g2, "gate_down": inp["gate_down"][l], "w_out": inp["w_out"][l],
        "mlp_w1": inp["mlp_w1"][l][:, half * cfg.DFFL:(half + 1) * cfg.DFFL],
        "mlp_w2": inp["mlp_w2"][l][half * cfg.DFFL:(half + 1) * cfg.DFFL, :],
    }
    for i in range(4):
        ws["wb_%d" % i] = inp["w_branch"][l, i]
        ws["gu_%d" % i] = inp["gate_up"][l][:, i, :]
    flats = [np.zeros(cfg.wrows[g] * WCOLS, np.float32) for g in range(cfg.NG)]
    for n, K, M in cfg.wshapes:
        off = cfg.woff[n][0]
        w = ws[n]
        wp = np.zeros((K, M), np.float32)
        wp[:w.shape[0], :w.shape[1]] = w
        t = tile_weight(wp)
        flats[cfg.wgroup_of[n]][off:off + t.size] = t
    return [flats[g].reshape(cfg.wrows[g], WCOLS) for g in range(cfg.NG)]


def host_pvec(cfg, inp, l, half=0):
    pv, PVN = pv_layout(cfg)
    BW, BC, BWG = cfg.BW, cfg.BC, cfg.BWG
    ch = np.arange(half * BW, (half + 1) * BW)
    out = np.zeros((128, PVN), np.float32)

    def put(name, arr):
        a, k = pv[name]
        out[:, a:a + k] = pp(arr, k)
    put("n1g", inp["norm1_g"][l])
    put("n2g", inp["norm2_g"][l])
    put("modb", inp["mod_b"][l])
    put("qg", inp["mla_q_norm_g"][l])
    put("kvg", inp["mla_kv_norm_g"][l])
    for d in range(2):
        mu = inp["rwkv_mu"][l, d]
        mup = np.zeros(3 * BW + 512, np.float32)
        for j in range(3):
            mup[j * BW:(j + 1) * BW] = mu[j * BWG + ch]
        mup[3 * BW:3 * BW + 256] = mu[3 * BWG:3 * BWG + 256]
        mup[3 * BW + 256:3 * BW + 256 + 160] = mu[3 * BWG + 256:]
        put("mu%d" % d, mup)
        put("w0_%d" % d, inp["rwkv_w0"][l, d][ch])
        put("a0_%d" % d, inp["rwkv_a0"][l, d][ch])
    put("kk", inp["rwkv_k_k"][l][ch])
    put("ka", inp["rwkv_k_a"][l][ch])
    put("rk", inp["rwkv_r_k"][l].reshape(-1)[ch])
    for j in range(3):
        put("cw%d" % j, inp["conv_w"][l, j][ch])
    for i in range(4):
        put("gb_%d" % i, inp["gate_b"][l, i])
    return out


def rope_tables(cfg):
    T, C, GW = cfg.T, cfg.C, cfg.GRID_W
    rows = T // GW
    row = np.repeat(np.arange(rows, dtype=np.float32), GW)
    colv = np.tile(np.arange(GW, dtype=np.float32), rows)
    nf = 16
    inv = (10000.0 ** (-np.arange(nf, dtype=np.float32) / nf)).astype(np.float32)
    ang = np.concatenate([row[:, None] * inv, colv[:, None] * inv], axis=-1)
    cos, sin = np.cos(ang).T.astype(np.float32), np.sin(ang).T.astype(np.float32)
    cos64 = np.concatenate([cos, cos], 0)
    sin64 = np.concatenate([-sin, sin], 0)
    c = np.ones((128, cfg.NT), np.float32)
    s = np.zeros((128, cfg.NT), np.float32)
    c[:, C:] = np.concatenate([cos64, cos64], 0)
    s[:, C:] = np.concatenate([sin64, sin64], 0)
    return c, s


def host_consts(cfg):
    G = cfg.BC
    mask16 = np.zeros((16, 512), np.float32)
    esel = np.zeros((16, 128), np.float32)
    for h in range(16):
        g, par = h // 2, h % 2
        if g < G:
            mask16[h, g * 64:(g + 1) * 64] = 1.0
        esel[h, par * 64:(par + 1) * 64] = 1.0
    blk1 = np.zeros((128, 128), np.float32)
    blk1[:64, :64] = 1.0
    blk1[64:, 64:] = 1.0
    oh = np.zeros((TC_SCAN, TC_SCAN, 64), np.float32)
    for t in range(TC_SCAN):
        oh[t, t, :] = 1.0
    c, s = rope_tables(cfg)
    return dict(c_ident=np.eye(128, dtype=np.float32), c_blk1=blk1, c_mask16=mask16, c_esel=esel,
                c_onehot=oh.reshape(TC_SCAN, TC_SCAN * 64), c_cos=c, c_sin=s)


def host_inputs(cfg, inp, ncores, batch_of_core, half_of_core=None):
    L, BW = cfg.DEPTH, cfg.BW
    if half_of_core is None:
        half_of_core = [0] * ncores
    consts = host_consts(cfg)
    dl = np.ascontiguousarray(inp["diff_lambda"].reshape(L, 256))
    fng = pp(inp["final_norm_g"])
    pvecs, bcs = {}, {}
    for half in sorted(set(half_of_core)):
        ch = np.arange(half * BW, (half + 1) * BW)
        pvecs[half] = np.concatenate([host_pvec(cfg, inp, l, half) for l in range(L)], 0)
        bc = np.zeros((L * 128, 2 * BW + 128), np.float32)
        for l in range(L):
            bc[l * 128:(l + 1) * 128, 0:BW] = np.tile(inp["rwkv_ln_g"][l][ch][None, :], (128, 1))
            bc[l * 128:(l + 1) * 128, BW:2 * BW] = np.tile(inp["rwkv_ln_b"][l][ch][None, :], (128, 1))
            bc[l * 128:(l + 1) * 128, 2 * BW:] = np.tile(inp["diff_norm_g"][l][None, :], (128, 1))
        bcs[half] = bc
    NG = cfg.NG
    nshard = max(1, ncores // cfg.split)
    sr = [cfg.wrows[g] // nshard for g in range(NG)]
    wsh = [[np.zeros((L * sr[g], WCOLS), np.float32) for g in range(NG)] for _ in range(ncores)]
    for l in range(L):
        for half in sorted(set(half_of_core)):
            flats = host_layer_weights(cfg, inp, l, half)
            members = [c for c in range(ncores) if half_of_core[c] == half]
            for g in range(NG):
                v = flats[g].reshape(sr[g] // GCH, nshard, GCH, WCOLS)
                for j, c in enumerate(members):
                    wsh[c][g][l * sr[g]:(l + 1) * sr[g]] = v[:, j].reshape(sr[g], WCOLS)
            del flats
    maps = []
    for c in range(ncores):
        b = batch_of_core[c]
        xT = np.ascontiguousarray(np.concatenate([inp["ctx"][b], inp["x"][b]], 0).T)
        cond = np.ascontiguousarray(np.stack([inp["c_ctx"], inp["c"][b]], 1))
        d = dict(xT=xT, cond=cond, pvec=pvecs[half_of_core[c]], bcin=bcs[half_of_core[c]], dlin=dl, fng=fng)
        for g in range(NG):
            d["wflat%d" % g] = wsh[c][g]
        d.update(consts)
        maps.append(d)
    return maps


from concourse.bass_utils import run_bass_kernel_spmd


def kernel(**inputs):
    inputs = {k: np.asarray(v) for k, v in inputs.items()}
    B, T, D = inputs["x"].shape
    C = inputs["ctx"].shape[1]
    L = inputs["w_in"].shape[0]
    ncores = 8
    cfg = Cfg(D=D, T=T, C=C, DEPTH=L, ncores=ncores, split=2)
    nc, P, _ = build_program(cfg)
    batch_of_core = [c % B for c in range(ncores)]
    half_of_core = [c // B for c in range(ncores)]
    maps = host_inputs(cfg, inputs, ncores, batch_of_core, half_of_core)
    res = run_bass_kernel_spmd(nc, maps, core_ids=list(range(ncores)))
    first = {}
    for c in range(ncores):
        first.setdefault(batch_of_core[c], c)
    out = np.stack([np.ascontiguousarray(res.results[first[b]]["yT"].T) for b in range(B)], 0)
    return out.astype(np.float32)
```
